# Optimizing a Trainium2 kernel written in Bass

```python
import math
import jax, jax.numpy as jnp
from jax import lax
import numpy as np

D_MODEL = 1024
BATCH = 2
SEQ = 8192
DEPTH = 4
DEC_BATCH = 128
DEC_SEQ = 1
PAST_LEN = 8192
PAGE_SIZE = 128

N_AB_LAYERS = (DEPTH + 1) // 2
N_SSD_LAYERS = DEPTH // 2

CONF_CH = D_MODEL // 2
CONF_KERNEL = 31
HEAD_DIM = 64
N_Q_HEADS = (D_MODEL // 2) // HEAD_DIM
N_KV_HEADS = 2
Q_PER_KV = N_Q_HEADS // N_KV_HEADS
WINDOW = 128
ATT_BLOCK = 128
ATT_SCALE = HEAD_DIM ** -0.5
AB_IN = 2 * CONF_CH + (N_Q_HEADS + 2 * N_KV_HEADS) * HEAD_DIM
AB_MIX = CONF_CH + N_Q_HEADS * HEAD_DIM
SSD_INNER = 2 * D_MODEL
SSD_HEAD_DIM = 64
SSD_HEADS = SSD_INNER // SSD_HEAD_DIM
SSD_GROUPS = 4
SSD_HPG = SSD_HEADS // SSD_GROUPS
SSD_STATE = 128
SSD_CONV = 4
SSD_CONV_CH = SSD_INNER + 2 * SSD_GROUPS * SSD_STATE
SSD_CHUNK = 128
SSD_IN = SSD_INNER + SSD_CONV_CH + SSD_HEADS
D_FF = 4 * D_MODEL
RMS_EPS = 1e-6
LN_EPS = 1e-5

kernel_name = 'hybrid_conv_swa_ssd_decoder_step'


def _rmsnorm(x, g):
    xf = x.astype(jnp.float32)
    y = xf * lax.rsqrt(jnp.mean(xf * xf, axis=-1, keepdims=True) + RMS_EPS)
    return (y * g.astype(jnp.float32)).astype(x.dtype)


def _layernorm(x, g, b):
    xf = x.astype(jnp.float32)
    xc = xf - jnp.mean(xf, axis=-1, keepdims=True)
    var = jnp.mean(xc * xc, axis=-1, keepdims=True)
    y = xc * lax.rsqrt(var + LN_EPS) * g.astype(jnp.float32) + b.astype(jnp.float32)
    return y.astype(x.dtype)


def _dwconv_valid(xp, w, b):
    y = lax.conv_general_dilated(xp, w[:, None, :].astype(xp.dtype), window_strides=(1,), padding='VALID',
                                 dimension_numbers=('NWC', 'WIO', 'NWC'), feature_group_count=xp.shape[-1])
    return y + b.astype(xp.dtype)


def _sink_softmax(s, mask, sinks):
    sink = sinks.astype(jnp.float32)[:, :, None, None]
    s = jnp.where(mask, s, -jnp.inf)
    m = jnp.maximum(jnp.max(s, axis=-1, keepdims=True), sink)
    p = jnp.exp(s - m)
    return p / (jnp.sum(p, axis=-1, keepdims=True) + jnp.exp(sink - m))


def _conformer_conv(u, hist, dw_w, dw_b, ln_g, ln_b):
    a = u[..., :CONF_CH] * jax.nn.sigmoid(u[..., CONF_CH:])
    ap = jnp.concatenate([hist.astype(a.dtype), a], axis=1)
    c = _dwconv_valid(ap, dw_w, dw_b)
    return jax.nn.silu(_layernorm(c, ln_g, ln_b)), ap


def _swa_prompt(q, k, v, sinks):
    bsz, t = q.shape[:2]
    nb = t // ATT_BLOCK
    qb = q.reshape(bsz, nb, ATT_BLOCK, N_KV_HEADS, Q_PER_KV, HEAD_DIM)
    kb = k.reshape(bsz, nb, ATT_BLOCK, N_KV_HEADS, HEAD_DIM)
    vb = v.reshape(bsz, nb, ATT_BLOCK, N_KV_HEADS, HEAD_DIM)
    kk = jnp.concatenate([jnp.concatenate([jnp.zeros_like(kb[:, :1]), kb[:, :-1]], axis=1), kb], axis=2)
    vv = jnp.concatenate([jnp.concatenate([jnp.zeros_like(vb[:, :1]), vb[:, :-1]], axis=1), vb], axis=2)
    s = jnp.einsum('bnqkgd,bnskd->bnkgqs', qb, kk).astype(jnp.float32) * ATT_SCALE
    qpos = ATT_BLOCK + jnp.arange(ATT_BLOCK)[:, None]
    kpos = jnp.arange(2 * ATT_BLOCK)[None, :]
    band = (kpos <= qpos) & (qpos - kpos < WINDOW)
    has_prev = (jnp.arange(nb) > 0)[:, None, None] | (kpos >= ATT_BLOCK)[None]
    mask = (band[None] & has_prev)[None, :, None, None]
    p = _sink_softmax(s, mask, sinks.reshape(N_KV_HEADS, Q_PER_KV))
    o = jnp.einsum('bnkgqs,bnskd->bnqkgd', p.astype(v.dtype), vv)
    return o.reshape(bsz, t, N_Q_HEADS * HEAD_DIM)


def _swa_sample(q, k, v, k_cache, v_cache, sinks):
    bsz, t = q.shape[:2]
    wc = k_cache.shape[1]
    kk = jnp.concatenate([k_cache.astype(k.dtype), k], axis=1)
    vv = jnp.concatenate([v_cache.astype(v.dtype), v], axis=1)
    qg = q.reshape(bsz, t, N_KV_HEADS, Q_PER_KV, HEAD_DIM)
    s = jnp.einsum('bqkgd,bskd->bkgqs', qg, kk).astype(jnp.float32) * ATT_SCALE
    qpos = wc + jnp.arange(t)[:, None]
    kpos = jnp.arange(wc + t)[None, :]
    mask = (kpos <= qpos) & (qpos - kpos < WINDOW)
    p = _sink_softmax(s, mask, sinks.reshape(N_KV_HEADS, Q_PER_KV))
    o = jnp.einsum('bkgqs,bskd->bqkgd', p.astype(v.dtype), vv)
    return o.reshape(bsz, t, N_Q_HEADS * HEAD_DIM)


def _ssd_chunked(x, dt, a, bm, cm):
    bsz, t = x.shape[:2]
    nc = t // SSD_CHUNK
    L = SSD_CHUNK
    xc = (x.astype(jnp.float32) * dt[..., None]).reshape(bsz, nc, L, SSD_GROUPS, SSD_HPG, SSD_HEAD_DIM)
    cs = jnp.cumsum((dt * a).reshape(bsz, nc, L, SSD_GROUPS, SSD_HPG), axis=2)
    bc = bm.astype(jnp.float32).reshape(bsz, nc, L, SSD_GROUPS, SSD_STATE)
    cc = cm.astype(jnp.float32).reshape(bsz, nc, L, SSD_GROUPS, SSD_STATE)
    causal = jnp.tril(jnp.ones((L, L), dtype=bool))[:, :, None, None]
    decay = jnp.exp(jnp.where(causal, cs[:, :, :, None] - cs[:, :, None, :], -jnp.inf))
    cb = jnp.einsum('bclgn,bcsgn->bclsg', cc, bc)
    y_diag = jnp.einsum('bclsgk,bcsgkp->bclgkp', cb[..., None] * decay, xc)
    decay_to_end = jnp.exp(cs[:, :, -1:] - cs)
    states = jnp.einsum('bclgn,bclgkp->bcgkpn', bc, xc * decay_to_end[..., None])
    chunk_decay = jnp.exp(cs[:, :, -1])

    def step(h, inp):
        st, dcy = inp
        return h * dcy[..., None, None] + st, h

    h0 = jnp.zeros((bsz, SSD_GROUPS, SSD_HPG, SSD_HEAD_DIM, SSD_STATE), jnp.float32)
    h_final, h_in = lax.scan(step, h0, (jnp.moveaxis(states, 1, 0), jnp.moveaxis(chunk_decay, 1, 0)))
    h_in = jnp.moveaxis(h_in, 0, 1)
    y_off = jnp.einsum('bclgn,bcgkpn->bclgkp', cc, h_in) * jnp.exp(cs)[..., None]
    y = (y_diag + y_off).reshape(bsz, t, SSD_HEADS, SSD_HEAD_DIM)
    return y.astype(x.dtype), h_final.reshape(bsz, SSD_HEADS, SSD_HEAD_DIM, SSD_STATE).astype(x.dtype)


def _ssd_recurrent(x, dt, a, bm, cm, h0):
    bsz, t = x.shape[:2]
    xg = (x.astype(jnp.float32) * dt[..., None]).reshape(bsz, t, SSD_GROUPS, SSD_HPG, SSD_HEAD_DIM)
    da = jnp.exp(dt * a).reshape(bsz, t, SSD_GROUPS, SSD_HPG)
    h = h0.astype(jnp.float32).reshape(bsz, SSD_GROUPS, SSD_HPG, SSD_HEAD_DIM, SSD_STATE)

    def step(h, inp):
        xt, dat, bt, ct = inp
        h = h * dat[..., None, None] + jnp.einsum('bgkp,bgn->bgkpn', xt, bt)
        return h, jnp.einsum('bgkpn,bgn->bgkp', h, ct)

    h, ys = lax.scan(step, h, (jnp.moveaxis(xg, 1, 0), jnp.moveaxis(da, 1, 0),
                               jnp.moveaxis(bm.astype(jnp.float32), 1, 0), jnp.moveaxis(cm.astype(jnp.float32), 1, 0)))
    y = jnp.moveaxis(ys, 0, 1).reshape(bsz, t, SSD_HEADS, SSD_HEAD_DIM)
    return y.astype(x.dtype), h.reshape(bsz, SSD_HEADS, SSD_HEAD_DIM, SSD_STATE).astype(x.dtype)


def _ssd_mixer(h, hist, w_in, conv_w, conv_b, dt_bias, a_log, d_skip, norm_g, w_out, h0):
    bsz, t, _ = h.shape
    u = h @ w_in
    z = u[..., :SSD_INNER]
    xbc = u[..., SSD_INNER:SSD_INNER + SSD_CONV_CH]
    dt_raw = u[..., SSD_INNER + SSD_CONV_CH:]
    xbc_p = jnp.concatenate([hist.astype(xbc.dtype), xbc], axis=1)
    xbc_c = jax.nn.silu(_dwconv_valid(xbc_p, conv_w, conv_b))
    gn = SSD_GROUPS * SSD_STATE
    xs = xbc_c[..., :SSD_INNER].reshape(bsz, t, SSD_HEADS, SSD_HEAD_DIM)
    bm = xbc_c[..., SSD_INNER:SSD_INNER + gn].reshape(bsz, t, SSD_GROUPS, SSD_STATE)
    cm = xbc_c[..., SSD_INNER + gn:].reshape(bsz, t, SSD_GROUPS, SSD_STATE)
    dt = jax.nn.softplus(dt_raw.astype(jnp.float32) + dt_bias.astype(jnp.float32))
    a = -jnp.exp(a_log.astype(jnp.float32))
    if h0 is None:
        y, state = _ssd_chunked(xs, dt, a, bm, cm)
    else:
        y, state = _ssd_recurrent(xs, dt, a, bm, cm, h0)
    y = y + d_skip[:, None].astype(y.dtype) * xs
    y = (y.reshape(bsz, t, SSD_INNER) * jax.nn.silu(z)).reshape(bsz, t, SSD_GROUPS, SSD_INNER // SSD_GROUPS)
    y = _rmsnorm(y, norm_g.reshape(SSD_GROUPS, SSD_INNER // SSD_GROUPS)).reshape(bsz, t, SSD_INNER)
    return y @ w_out, state, xbc_p


def _trunk(x, P, cache):
    prompt = cache is None
    bsz, t, _ = x.shape
    win_k, win_v, conf_rows, ssm_states, ssd_rows = [], [], [], [], []
    qw = N_Q_HEADS * HEAD_DIM
    kw = N_KV_HEADS * HEAD_DIM
    o = 2 * CONF_CH
    for layer in range(DEPTH):
        g = P['norm_g'][layer]
        i = layer // 2
        h = _rmsnorm(x, g[0])
        if layer % 2 == 0:
            u = h @ P['ab_w_in'][i]
            q = u[..., o:o + qw].reshape(bsz, t, N_Q_HEADS, HEAD_DIM)
            k = u[..., o + qw:o + qw + kw].reshape(bsz, t, N_KV_HEADS, HEAD_DIM)
            v = u[..., o + qw + kw:].reshape(bsz, t, N_KV_HEADS, HEAD_DIM)
            if prompt:
                hist = jnp.zeros((bsz, CONF_KERNEL - 1, CONF_CH), x.dtype)
            else:
                hist = cache[2][i]
            c_out, ap = _conformer_conv(u[..., :o], hist, P['conf_dw_w'][i], P['conf_dw_b'][i],
                                        P['conf_ln_g'][i], P['conf_ln_b'][i])
            if prompt:
                a_out = _swa_prompt(q, k, v, P['attn_sinks'][i])
                wp = min(WINDOW, t)
                win_k.append(k[:, t - wp:])
                win_v.append(v[:, t - wp:])
                conf_rows.append(ap[:, ap.shape[1] - (CONF_KERNEL - 1):])
            else:
                a_out = _swa_sample(q, k, v, cache[0][i], cache[1][i], P['attn_sinks'][i])
                win_k.append(k)
                win_v.append(v)
                conf_rows.append(ap[:, CONF_KERNEL - 1:])
            mix = jnp.concatenate([c_out, a_out], axis=-1) @ P['ab_w_out'][i]
        else:
            if prompt:
                hist = jnp.zeros((bsz, SSD_CONV - 1, SSD_CONV_CH), x.dtype)
                h0 = None
            else:
                hist = cache[4][i]
                h0 = cache[3][i]
            mix, state, xbc_p = _ssd_mixer(h, hist, P['ssd_w_in'][i], P['ssd_conv_w'][i], P['ssd_conv_b'][i],
                                           P['ssd_dt_bias'][i], P['ssd_a_log'][i], P['ssd_d'][i],
                                           P['ssd_norm_g'][i], P['ssd_w_out'][i], h0)
            ssm_states.append(state)
            if prompt:
                ssd_rows.append(xbc_p[:, xbc_p.shape[1] - (SSD_CONV - 1):])
            else:
                ssd_rows.append(xbc_p[:, SSD_CONV - 1:])
        x = x + _rmsnorm(mix, g[1])
        h = _rmsnorm(x, g[2])
        f = jnp.square(jax.nn.relu(h @ P['mlp_w_up'][layer])) @ P['mlp_w_down'][layer]
        x = x + _rmsnorm(f, g[3])
    return x, jnp.stack(win_k), jnp.stack(win_v), jnp.stack(conf_rows), jnp.stack(ssm_states), jnp.stack(ssd_rows)


def setup_inputs(seed: int = 0) -> dict:
    key = jax.random.key(seed)
    ks = jax.random.split(key, 26)
    f32 = jnp.float32

    def nrm(k, shape, scale):
        return jax.random.normal(k, shape, f32) * scale

    wc = min(WINDOW, PAST_LEN)
    dt0 = jnp.exp(jax.random.uniform(ks[16], (N_SSD_LAYERS, SSD_HEADS), f32, math.log(1e-3), math.log(1e-1)))
    return {
        'x_prompt': nrm(ks[0], (BATCH, SEQ, D_MODEL), 1.0),
        'x_sample': nrm(ks[1], (DEC_BATCH, DEC_SEQ, D_MODEL), 1.0),
        'cache_win_k': nrm(ks[2], (N_AB_LAYERS, DEC_BATCH, wc, N_KV_HEADS, HEAD_DIM), 1.0),
        'cache_win_v': nrm(ks[3], (N_AB_LAYERS, DEC_BATCH, wc, N_KV_HEADS, HEAD_DIM), 1.0),
        'state_conf_conv': nrm(ks[4], (N_AB_LAYERS, DEC_BATCH, CONF_KERNEL - 1, CONF_CH), 0.5),
        'state_ssm': nrm(ks[5], (N_SSD_LAYERS, DEC_BATCH, SSD_HEADS, SSD_HEAD_DIM, SSD_STATE), 0.1),
        'state_ssd_conv': nrm(ks[6], (N_SSD_LAYERS, DEC_BATCH, SSD_CONV - 1, SSD_CONV_CH), 1.0),
        'norm_g': 1.0 + nrm(ks[7], (DEPTH, 4, D_MODEL), 0.05),
        'ab_w_in': nrm(ks[8], (N_AB_LAYERS, D_MODEL, AB_IN), D_MODEL ** -0.5),
        'conf_dw_w': nrm(ks[9], (N_AB_LAYERS, CONF_KERNEL, CONF_CH), CONF_KERNEL ** -0.5),
        'conf_dw_b': nrm(ks[10], (N_AB_LAYERS, CONF_CH), 0.01),
        'conf_ln_g': 1.0 + nrm(ks[11], (N_AB_LAYERS, CONF_CH), 0.05),
        'conf_ln_b': nrm(ks[12], (N_AB_LAYERS, CONF_CH), 0.01),
        'attn_sinks': nrm(ks[13], (N_AB_LAYERS, N_Q_HEADS), 0.5),
        'ab_w_out': nrm(ks[14], (N_AB_LAYERS, AB_MIX, D_MODEL), AB_MIX ** -0.5),
        'ssd_w_in': nrm(ks[15], (N_SSD_LAYERS, D_MODEL, SSD_IN), D_MODEL ** -0.5),
        'ssd_conv_w': nrm(ks[17], (N_SSD_LAYERS, SSD_CONV, SSD_CONV_CH), SSD_CONV ** -0.5),
        'ssd_conv_b': nrm(ks[18], (N_SSD_LAYERS, SSD_CONV_CH), 0.01),
        'ssd_dt_bias': dt0 + jnp.log(-jnp.expm1(-dt0)),
        'ssd_a_log': jnp.log(jax.random.uniform(ks[19], (N_SSD_LAYERS, SSD_HEADS), f32, 1.0, 16.0)),
        'ssd_d': 1.0 + nrm(ks[20], (N_SSD_LAYERS, SSD_HEADS), 0.05),
        'ssd_norm_g': 1.0 + nrm(ks[21], (N_SSD_LAYERS, SSD_INNER), 0.05),
        'ssd_w_out': nrm(ks[22], (N_SSD_LAYERS, SSD_INNER, D_MODEL), SSD_INNER ** -0.5),
        'mlp_w_up': nrm(ks[23], (DEPTH, D_MODEL, D_FF), D_MODEL ** -0.5),
        'mlp_w_down': nrm(ks[24], (DEPTH, D_FF, D_MODEL), D_FF ** -0.5),
    }


def reference(x_prompt, x_sample, cache_win_k, cache_win_v, state_conf_conv, state_ssm, state_ssd_conv,
              norm_g, ab_w_in, conf_dw_w, conf_dw_b, conf_ln_g, conf_ln_b, attn_sinks, ab_w_out,
              ssd_w_in, ssd_conv_w, ssd_conv_b, ssd_dt_bias, ssd_a_log, ssd_d, ssd_norm_g, ssd_w_out,
              mlp_w_up, mlp_w_down):
    P = {'norm_g': norm_g, 'ab_w_in': ab_w_in, 'conf_dw_w': conf_dw_w, 'conf_dw_b': conf_dw_b,
         'conf_ln_g': conf_ln_g, 'conf_ln_b': conf_ln_b, 'attn_sinks': attn_sinks, 'ab_w_out': ab_w_out,
         'ssd_w_in': ssd_w_in, 'ssd_conv_w': ssd_conv_w, 'ssd_conv_b': ssd_conv_b, 'ssd_dt_bias': ssd_dt_bias,
         'ssd_a_log': ssd_a_log, 'ssd_d': ssd_d, 'ssd_norm_g': ssd_norm_g, 'ssd_w_out': ssd_w_out,
         'mlp_w_up': mlp_w_up, 'mlp_w_down': mlp_w_down}
    y_prompt, wk_p, wv_p, cc_p, ssm_p, sc_p = _trunk(x_prompt, P, None)
    y_sample, wk_s, wv_s, cc_s, ssm_s, sc_s = _trunk(
        x_sample, P, (cache_win_k, cache_win_v, state_conf_conv, state_ssm, state_ssd_conv))
    return (y_prompt, y_sample, wk_p, wv_p, wk_s, wv_s, cc_p, cc_s, ssm_p, ssm_s, sc_p, sc_s)
```

```python
import os
import numpy as np
import concourse.bass as bass
import concourse.mybir as mybir
from concourse.bass_utils import run_bass_kernel_spmd
from contextlib import ExitStack

F32 = mybir.dt.float32
BF16 = mybir.dt.bfloat16
I32 = mybir.dt.int32
ALU = mybir.AluOpType
AF = mybir.ActivationFunctionType
AX = mybir.AxisListType
DTB = {F32: 4, BF16: 2, I32: 4}

NCORES = 8
NT = 16
TPC = NT * 128
NS = 16
D = 1024
NEG = -30000.0


class Tl:
    def __init__(self, name, ap, nslots=1):
        self.name = name
        self.ap = ap
        self.n = nslots
        self.lw = [None] * nslots
        self.rd = [[] for _ in range(nslots)]

    def __getitem__(self, k):
        return self.ap[k]

    def s(self, lo, hi=None):
        return (self, lo, lo + 1 if hi is None else hi)


def _norm(x):
    if isinstance(x, Tl):
        return (x, 0, x.n)
    return x


class Ins:
    __slots__ = ("eng", "fn", "deps", "is_dma", "sig", "tick", "sem", "idx", "inc")

    def __init__(self, eng, fn, is_dma, inc):
        self.eng = eng
        self.fn = fn
        self.is_dma = is_dma
        self.inc = inc
        self.deps = set()
        self.sig = False
        self.tick = 0
        self.sem = None


COMPUTE = ("pe", "act", "dve", "pool")


class Prog:
    def __init__(self, nc, n_dma_sems=8):
        self.nc = nc
        self.ins = []
        self.n_dma_sems = n_dma_sems
        self.last_on_eng = {}
        self.barrier_pending = {}
        self.dma_rr = {"sp": 0, "pool": 0}
        self.dma_last = {}

    def add(self, eng, fn, r=(), w=(), dma=False, cc=False):
        i = len(self.ins)
        ins = Ins(eng, fn, dma or cc, 1 if cc else 16)
        ins.idx = i
        for x in r:
            t, lo, hi = _norm(x)
            for s in range(lo, hi):
                if t.lw[s] is not None:
                    ins.deps.add(t.lw[s])
        for x in w:
            t, lo, hi = _norm(x)
            for s in range(lo, hi):
                if t.lw[s] is not None:
                    ins.deps.add(t.lw[s])
                for rr in t.rd[s]:
                    ins.deps.add(rr)
        for x in r:
            t, lo, hi = _norm(x)
            for s in range(lo, hi):
                t.rd[s].append(i)
        for x in w:
            t, lo, hi = _norm(x)
            for s in range(lo, hi):
                t.lw[s] = i
                t.rd[s] = []
        if cc:
            ins.sem = ("cc", 0)
            prev = self.dma_last.get(ins.sem)
            if prev is not None:
                ins.deps.add(prev)
            self.dma_last[ins.sem] = i
        elif dma:
            k = self.dma_rr[eng]
            self.dma_rr[eng] = (k + 1) % self.n_dma_sems
            ins.sem = (eng, k)
            prev = self.dma_last.get(ins.sem)
            if prev is not None:
                ins.deps.add(prev)
            self.dma_last[ins.sem] = i
        if eng in self.barrier_pending:
            ins.deps |= self.barrier_pending.pop(eng)
        ins.deps.discard(i)
        self.ins.append(ins)
        self.last_on_eng[eng] = i
        return i

    def barrier(self):
        pend = set(self.last_on_eng.values())
        for v in self.dma_last.values():
            pend.add(v)
        for e in ("pe", "act", "dve", "pool", "sp"):
            self.barrier_pending[e] = set(pend) | self.barrier_pending.get(e, set())

    def emit(self, stack):
        nc = self.nc
        ins = self.ins
        for x in ins:
            nd = set()
            for d in x.deps:
                p = ins[d]
                if (not p.is_dma) and (not x.is_dma) and p.eng == "pe" and x.eng == "pe":
                    continue
                nd.add(d)
            x.deps = nd
            for d in nd:
                ins[d].sig = True
        cnt = {e: 0 for e in COMPUTE}
        dcnt = {}
        for x in ins:
            if x.is_dma:
                dcnt[x.sem] = dcnt.get(x.sem, 0) + x.inc
                x.tick = dcnt[x.sem]
            elif x.sig:
                cnt[x.eng] += 1
                x.tick = cnt[x.eng]
        sems = {}
        for e in COMPUTE:
            sems[e] = stack.enter_context(nc.semaphore("s_" + e))
        for key in dcnt:
            sems[key] = stack.enter_context(nc.semaphore("d_%s%d" % key))
        per_eng = {e: [] for e in ("pe", "act", "dve", "pool", "sp")}
        for x in ins:
            per_eng[x.eng].append(x)
        final_dma = dict(dcnt)

        def run(eng_name, e):
            waited = {}
            for x in per_eng[eng_name]:
                need = {}
                for d in x.deps:
                    p = ins[d]
                    key = p.sem if p.is_dma else p.eng
                    if p.tick > need.get(key, 0):
                        need[key] = p.tick
                for key, v in need.items():
                    if waited.get(key, 0) < v:
                        e.wait_ge(sems[key], v)
                        waited[key] = v
                bi = x.fn(e)
                if x.is_dma:
                    if x.inc == 1:
                        bi.then_inc(sems[x.sem])
                    else:
                        bi.then_inc(sems[x.sem], 16)
                elif x.sig:
                    bi.then_inc(sems[x.eng], 1)
            for key, v in final_dma.items():
                owner = "pool" if key[0] == "cc" else key[0]
                if owner == eng_name and waited.get(key, 0) < v:
                    e.wait_ge(sems[key], v)

        with nc.Block() as block:
            @block.tensor
            def _(e):
                run("pe", e)

            @block.scalar
            def _(e):
                run("act", e)

            @block.vector
            def _(e):
                run("dve", e)

            @block.gpsimd
            def _(e):
                run("pool", e)

            @block.sync
            def _(e):
                run("sp", e)
        return cnt, dcnt


IN_SPECS = [
    ("xp", [TPC, D], F32), ("xh", [128, D], F32), ("xs", [NS, D], F32),
    ("ck", [2, NS, 128, 128], F32), ("cv", [2, NS, 128, 128], F32),
    ("cconf", [2, NS, 30, 512], F32), ("cssm", [2, NS, 2048, 128], F32),
    ("csc", [2, NS, 3, 3072], F32),
    ("norm_g", [16, D], F32), ("ab_w_in", [2, D, 1792], F32),
    ("conf_dw_w", [2, 31, 512], F32), ("conf_dw_b", [2, 512], F32),
    ("conf_ln_g", [2, 512], F32), ("conf_ln_b", [2, 512], F32),
    ("attn_sinks", [2, 8], F32), ("ab_w_out", [2, D, D], F32),
    ("ssd_w_in", [2, D, 5152], F32), ("ssd_conv_w", [2, 4, 3072], F32),
    ("ssd_conv_b", [2, 3072], F32), ("ssd_dt_bias", [2, 32], F32),
    ("ssd_a_log", [2, 32], F32), ("ssd_d", [2, 32], F32),
    ("ssd_norm_g", [2, 2048], F32), ("ssd_w_out", [2, 2048, D], F32),
    ("mlp_w_up", [4, D, 4096], F32), ("mlp_w_down", [4, 4096, D], F32),
    ("c_ident", [128, 128], F32), ("c_tri", [128, 128], F32), ("c_up", [128, 128], F32),
    ("c_mask0", [128, 256], F32), ("c_mask1", [128, 256], F32),
    ("c_hasprev", [128, 1], F32), ("c_pub", [128, 4], F32),
    ("c_sel", [32, 128], F32), ("c_idxprev", [128, 1], I32), ("c_idxchain", [128, 3], I32), ("c_vmask", [128, 3], F32),
]
OUT_SPECS = [
    ("y_p", [TPC, D]), ("y_s", [NS, D]),
    ("wk_p", [2, 128, 128]), ("wv_p", [2, 128, 128]),
    ("wk_s", [2, NS, 128]), ("wv_s", [2, NS, 128]),
    ("cc_p", [2, 30, 512]), ("cc_s", [2, NS, 512]),
    ("ssm_p", [2, 2048, 128]), ("ssm_s", [2, NS, 2048, 128]),
    ("sc_p", [2, 3, 3072]), ("sc_s", [2, NS, 3072]),
]


def build(nsub=8):
    nc = bass.Bass("TRN2", target_bir_lowering=False)
    st = ExitStack()
    P = Prog(nc)
    I = {}
    for name, shape, dt in IN_SPECS:
        I[name] = nc.dram_tensor(name, shape, dt, kind="ExternalInput")
    O = {}
    for name, shape in OUT_SPECS:
        O[name] = nc.dram_tensor(name, shape, F32, kind="ExternalOutput")
    TI = {k: Tl(k, v) for k, v in I.items()}
    TO = {k: Tl(k, v) for k, v in O.items()}
    xres = nc.dram_tensor("xres", [TPC, D], F32)
    T_xres = Tl("xres", xres, NT)
    EX = {}
    for nm_, w_ in (("h", D), ("s", 2048), ("l", 32)):
        a_ = nc.dram_tensor("ex_in_" + nm_, [4 * 128, w_], F32)
        b_ = nc.dram_tensor("ex_out_" + nm_, [4 * 128, w_], F32)
        EX[nm_] = (a_, b_, Tl("exi" + nm_, a_), Tl("exo" + nm_, b_))
    scr = nc.dram_tensor("scr", [NS, 128], F32)
    T_scr = Tl("scr", scr)
    SCR = {}
    for nm_, w_ in (("xdt", 2048), ("B", 512), ("C", 512), ("y", 2048)):
        a_ = nc.dram_tensor("scr_" + nm_, [NS, w_], F32)
        SCR[nm_] = (a_, Tl("scr_" + nm_, a_))

    uid = [0]

    def sb(shape, dt, n=1, name=None):
        uid[0] += 1
        nm = (name or "t") + str(uid[0])
        return Tl(nm, st.enter_context(nc.sbuf_tensor(nm, shape, dt)), n)

    ARENA_COLS = 90 * 1024
    arena = st.enter_context(nc.sbuf_tensor("arena", [128, ARENA_COLS], BF16))
    apos = [0]

    def areset():
        P.barrier()
        apos[0] = 0

    def al(shape, dt, n=1, name="a"):
        cols = int(np.prod(shape[1:])) * DTB[dt] // 2
        cols = (cols + 15) // 16 * 16
        off = apos[0]
        apos[0] += cols
        assert apos[0] <= ARENA_COLS, ("arena overflow", name, apos[0])
        ap = arena[0:shape[0], off:off + cols]
        if dt != BF16:
            ap = ap.bitcast(dt)
        tot = int(np.prod(shape[1:]))
        ap = ap[:, 0:tot]
        if len(shape) == 3:
            ap = ap.rearrange("p (a b) -> p a b", a=shape[1])
        elif len(shape) == 4:
            ap = ap.rearrange("p (a b c) -> p a b c", a=shape[1], b=shape[2])
        uid[0] += 1
        return Tl(name + str(uid[0]), ap, n)

    psA = Tl("psA", st.enter_context(nc.psum_tensor("psA", [128, 1024], F32)))
    psB = Tl("psB", st.enter_context(nc.psum_tensor("psB", [128, 1024], F32)))
    ps1 = [Tl("ps%d" % i, st.enter_context(nc.psum_tensor("ps%d" % i, [128, 512], F32))) for i in range(4)]
    psB_halves = [Tl("psB0", psB[:, 0:512]), Tl("psB1", psB[:, 512:1024])]
    rr = {"ps": 0}

    def nps():
        rr["ps"] = (rr["ps"] + 1) % 4
        return ps1[rr["ps"]]

    def mm(ot, oap, lt, lap, rt, rap, start=True, stop=True):
        P.add("pe", lambda e: e.matmul(oap, lap, rap, start=start, stop=stop), r=[lt, rt], w=[ot])

    def tr(ot, oap, it, iap, identt, idap):
        P.add("pe", lambda e: e.transpose(oap, iap, idap), r=[it, identt], w=[ot])

    def act(ot, oap, it, iap, func, bias=None, scale=None, accum=None, extra_r=(), extra_w=()):
        kw = {}
        if bias is not None:
            kw["bias"] = bias
        if scale is not None:
            kw["scale"] = scale
        if accum is not None:
            kw["accum_out"] = accum
        P.add("act", lambda e: e.activation(out=oap, in_=iap, func=func, **kw),
              r=[it] + list(extra_r), w=[ot] + list(extra_w))

    def tt(ot, oap, at, aap, bt, bap, op, eng="dve"):
        P.add(eng, lambda e: e.tensor_tensor(out=oap, in0=aap, in1=bap, op=op), r=[at, bt], w=[ot])

    def ts(ot, oap, at, aap, s1, s2, op0, op1=None, extra_r=(), eng="dve", accum=None):
        if op1 is None:
            P.add(eng, lambda e: e.tensor_scalar(out=oap, in0=aap, scalar1=s1, scalar2=None, op0=op0),
                  r=[at] + list(extra_r), w=[ot])
        else:
            P.add(eng, lambda e: e.tensor_scalar(out=oap, in0=aap, scalar1=s1, scalar2=s2, op0=op0, op1=op1),
                  r=[at] + list(extra_r), w=[ot])

    def stt(ot, oap, at, aap, scalar, bt, bap, op0, op1, extra_r=()):
        P.add("dve", lambda e: e.scalar_tensor_tensor(out=oap, in0=aap, scalar=scalar, in1=bap, op0=op0, op1=op1),
              r=[at, bt] + list(extra_r), w=[ot])

    def cp(ot, oap, it, iap, eng="dve"):
        if eng == "act":
            P.add("act", lambda e: e.copy(out=oap, in_=iap), r=[it], w=[ot])
        else:
            P.add(eng, lambda e: e.tensor_copy(out=oap, in_=iap), r=[it], w=[ot])

    def recip(ot, oap, it, iap):
        P.add("dve", lambda e: e.reciprocal(out=oap, in_=iap), r=[it], w=[ot])

    def dma(q, ot, oap, it, iap):
        P.add(q, lambda e: e.dma_start(out=oap, in_=iap), r=[it], w=[ot], dma=True)

    def memset(t, ap, v, eng="pool"):
        P.add(eng, lambda e: e.memset(ap, v), w=[t])

    ident_f = sb([128, 128], F32)
    ident_b = sb([128, 128], BF16)
    tri_f = sb([128, 128], F32)
    tri_b = sb([128, 128], BF16)
    up_b = sb([128, 128], BF16)
    trimask_b = sb([128, 128], BF16)
    ones_b = sb([128, 128], BF16)
    ones_f = sb([128, 128], F32)
    mask0 = sb([128, 256], F32)
    mask1 = sb([128, 256], F32)
    hasprev = sb([128, 1], F32)
    pub = sb([128, 4], F32)
    idxprev = sb([128, 1], I32)
    idxchain = sb([128, 3], I32)
    vmask = sb([128, 3], F32)
    XS = sb([NS, D], F32)
    gbc = [sb([128, D], F32), sb([128, D], F32)]
    junk = sb([128, 1024], BF16)
    ss_t = sb([128, 8], F32)
    sd_t = sb([128, 8], F32)
    xhalo = sb([128, D], F32)
    for t, nm in ((ident_f, "c_ident"), (tri_f, "c_tri"), (mask0, "c_mask0"), (mask1, "c_mask1"),
                  (hasprev, "c_hasprev"), (pub, "c_pub"), (idxprev, "c_idxprev"),
                  (idxchain, "c_idxchain"), (vmask, "c_vmask")):
        dma("sp", t, t[:], TI[nm], I[nm].ap())
    tmpc = sb([128, 128], F32)
    dma("sp", tmpc, tmpc[:], TI["c_up"], I["c_up"].ap())
    cp(ident_b, ident_b[:], ident_f, ident_f[:])
    cp(tri_b, tri_b[:], tri_f, tri_f[:])
    cp(trimask_b, trimask_b[:], tri_f, tri_f[:])
    cp(up_b, up_b[:], tmpc, tmpc[:])
    memset(ones_b, ones_b[:], 1.0)
    memset(ones_f, ones_f[:], 1.0)
    dma("sp", XS, XS[:], TI["xs"], I["xs"].ap())
    dma("sp", xhalo, xhalo[:], TI["xh"], I["xh"].ap())

    gsel = [0]

    def load_gamma(row):
        gsel[0] ^= 1
        g = gbc[gsel[0]]
        dma("sp", g, g[:], TI["norm_g"], I["norm_g"][row:row + 1, :].partition_broadcast(128))
        return g

    EPS = 1e-6

    def rstd_of(src_t, src_ap, M, col, ncols=D, eps=EPS):
        act(ss_t, junk[0:M, 0:ncols], src_t, src_ap, AF.Square, accum=ss_t[0:M, col:col + 1])
        act(sd_t, sd_t[0:M, col:col + 1], ss_t, ss_t[0:M, col:col + 1], AF.Sqrt, scale=1.0 / ncols, bias=eps)
        recip(sd_t, sd_t[0:M, col:col + 1], sd_t, sd_t[0:M, col:col + 1])
        return sd_t[0:M, col:col + 1]

    def norm_T(src_t, src_ap, M, g, hT, hT_slot, tok0, xn_t):
        rs = rstd_of(src_t, src_ap, M, 0)
        stt(xn_t, xn_t[0:M, :], src_t, src_ap, rs, g, g[0:M, :], ALU.mult, ALU.mult, extra_r=[sd_t])
        pt = psA
        ptv = pt[:, 0:512].bitcast(BF16).rearrange("p (k m) -> p k m", k=8)
        for kc in range(8):
            tr(pt, ptv[:, kc, 0:M], xn_t, xn_t[0:M, kc * 128:(kc + 1) * 128], ident_b, ident_b[0:M, 0:M])
        cp(hT_slot, hT[:, :, tok0:tok0 + M], pt, ptv[:, :, 0:M], eng="act")

    def post_norm_add(f_t, f_ap, M, g, x_t, x_ap, out_t, out_ap, tmp_t):
        rs = rstd_of(f_t, f_ap, M, 1)
        stt(tmp_t, tmp_t[0:M, :], f_t, f_ap, rs, g, g[0:M, :], ALU.mult, ALU.mult, extra_r=[sd_t])
        tt(out_t, out_ap, tmp_t, tmp_t[0:M, :], x_t, x_ap, ALU.add)

    state = {"xsrc": I["xp"], "xsrc_t": Tl("xp_rows", I["xp"], NT)}

    def xrows(t):
        return state["xsrc"][t * 128:(t + 1) * 128, :]

    def exchange(kind, pieces, tmp):
        ex_in, ex_out, T_exin, T_exout = EX[kind]
        for (t, ap, c0, w) in pieces:
            for r in range(4):
                ts(tmp, tmp[:, 0:w], t, ap, pub[:, r:r + 1], None, ALU.mult, extra_r=[pub])
                dma("sp", T_exin, ex_in[r * 128:(r + 1) * 128, c0:c0 + w], tmp, tmp[:, 0:w])
        P.add("pool", lambda e: e.collective_compute(
            "AllReduce", ALU.add, replica_groups=[[0, 1, 2, 3], [4, 5, 6, 7]],
            ins=[ex_in.ap().opt()], outs=[ex_out.ap().opt()]), r=[T_exin], w=[T_exout], cc=True)

    def gather_rows(kind, dst_t, dst_ap, idx_t, idx_ap, c0, w):
        ex_in, ex_out, T_exin, T_exout = EX[kind]
        P.add("pool", lambda e: e.indirect_dma_start(
            out=dst_ap, out_offset=None, in_=ex_out[:, :],
            in_offset=bass.IndirectOffsetOnAxis(ap=idx_ap, axis=0)),
            r=[T_exout, idx_t], w=[dst_t], dma=True)

    def mlp(layer, last):
        areset()
        g2 = load_gamma(layer * 4 + 2)
        g3 = load_gamma(layer * 4 + 3)
        hT = al([128, 8, TPC + NS], BF16, n=NT + 1, name="hT")
        Fa = al([128, NT, D], F32, n=NT, name="F")
        Fs = al([NS, D], F32, name="Fs")
        xt = [al([128, D], F32, name="xt") for _ in range(2)]
        xn = [al([128, D], BF16, name="xn") for _ in range(2)]
        wup = [al([128, 8, 512], BF16, name="wup") for _ in range(2)]
        wdn = [al([128, 4, D], BF16, name="wdn") for _ in range(2)]
        aT = [al([128, 4, 512], BF16, name="aT") for _ in range(2)]
        rl = [al([128, 512], BF16, name="rl") for _ in range(2)]
        wstg = al([128, 8, 512], F32, name="wstg")
        for t in range(NT):
            x = xt[t % 2]
            dma("sp", x, x[:], state["xsrc_t"].s(t), xrows(t))
            norm_T(x, x[:], 128, g2, hT, hT.s(t), t * 128, xn[t % 2])
        norm_T(XS, XS[:], NS, g2, hT, hT.s(NT), TPC, xn[0])
        groups = [(tg * 512, 512, list(range(tg * 4, tg * 4 + 4))) for tg in range(4)] + [(TPC, NS, [NT])]
        STOP = int(os.environ.get('MLP_STOP', '9'))
        if STOP <= 1:
            return
        k = 0
        def prep(fb):
            wu, wd = wup[fb % 2], wdn[fb % 2]
            dma("sp", wstg, wstg[:], TI["mlp_w_up"],
                I["mlp_w_up"][layer, :, fb * 512:(fb + 1) * 512].rearrange("(k p) c -> p k c", p=128))
            cp(wu, wu[:], wstg, wstg[:], eng="act")
            w4 = wstg[:].rearrange("p k c -> p (k c)").rearrange("p (k c) -> p k c", k=4)
            dma("sp", wstg, w4, TI["mlp_w_down"],
                I["mlp_w_down"][layer, fb * 512:(fb + 1) * 512, :].rearrange("(k p) c -> p k c", p=128))
            cp(wd, wd[:], wstg, w4, eng="act")

        prep(0)
        for fb in range(8):
            wu, wd = wup[fb % 2], wdn[fb % 2]
            if STOP <= 2 and fb >= 1:
                break
            for gi, (tok0, ntok, slots) in enumerate(groups):
                if gi == 1 and fb < 7:
                    prep(fb + 1)
                a = aT[k % 2]
                k += 1
                for fc in range(4):
                    pu = nps()
                    for kc in range(8):
                        mm(pu, pu[:, 0:ntok], wu, wu[:, kc, fc * 128:(fc + 1) * 128],
                           (hT, slots[0], slots[-1] + 1), hT[:, kc, tok0:tok0 + ntok], start=(kc == 0), stop=(kc == 7))
                    r_ = rl[fc % 2]
                    act(r_, r_[:, 0:ntok], pu, pu[:, 0:ntok], AF.Relu)
                    tt(a, a[:, fc, 0:ntok], r_, r_[:, 0:ntok], r_, r_[:, 0:ntok], ALU.mult)
                if STOP <= 3:
                    continue
                if ntok == NS:
                    for dh in range(2):
                        pd = nps()
                        for fc in range(4):
                            mm(pd, pd[0:NS, :], a, a[:, fc, 0:NS], wd, wd[:, fc, dh * 512:(dh + 1) * 512],
                               start=(fc == 0), stop=(fc == 3))
                        if fb == 0:
                            cp(Fs, Fs[:, dh * 512:(dh + 1) * 512], pd, pd[0:NS, :])
                        else:
                            tt(Fs, Fs[:, dh * 512:(dh + 1) * 512], pd, pd[0:NS, :], Fs, Fs[:, dh * 512:(dh + 1) * 512], ALU.add)
                else:
                    for ti, tslot in enumerate(slots):
                        for dh in range(2):
                            pd = nps()
                            for fc in range(4):
                                mm(pd, pd[:, :], a, a[:, fc, ti * 128:(ti + 1) * 128], wd, wd[:, fc, dh * 512:(dh + 1) * 512],
                                   start=(fc == 0), stop=(fc == 3))
                            fs = Fa.s(tslot)
                            if fb == 0:
                                cp(fs, Fa[:, tslot, dh * 512:(dh + 1) * 512], pd, pd[:, :], eng="act")
                            else:
                                tt(fs, Fa[:, tslot, dh * 512:(dh + 1) * 512], pd, pd[:, :], fs,
                                   Fa[:, tslot, dh * 512:(dh + 1) * 512], ALU.add)
        if STOP <= 4:
            return
        dst = O["y_p"] if last else xres
        dst_t = TO["y_p"] if last else T_xres
        for t in range(NT):
            x = xt[t % 2]
            tm = xn[t % 2]
            dma("sp", x, x[:], state["xsrc_t"].s(t), xrows(t))
            tmpf = xt2[t % 2]
            post_norm_add(Fa.s(t), Fa[:, t, :], 128, g3, x, x[:], x, x[:], tmpf)
            if t == NT - 1 and not last:
                extmp = al([128, D], F32, name="extmp")
                exchange("h", [(x, x[:], 0, D)], extmp)
                gather_rows("h", xhalo, xhalo[:], idxprev, idxprev[:, 0:1], 0, D)
            dma("sp", (dst_t, t, t + 1) if dst_t.n == NT else dst_t, dst[t * 128:(t + 1) * 128, :], x, x[:])
        tmpf = xt2[0]
        post_norm_add(Fs, Fs[:, :], NS, g3, XS, XS[:], XS, XS[:], tmpf)
        if last:
            dma("sp", TO["y_s"], O["y_s"].ap(), XS, XS[:])
        if not last:
            state["xsrc"] = xres
            state["xsrc_t"] = T_xres


    def load_cols(rows, R, C, dst_t, dst_ap_fn, tmp_rows):
        for r_, (t_, ap_) in enumerate(rows):
            dma("sp", tmp_rows, tmp_rows[r_:r_ + 1, 0:C], t_, ap_)
        for cc in range(C // 128):
            pt = nps()
            tr(pt, pt[:, 0:R], tmp_rows, tmp_rows[0:R, cc * 128:(cc + 1) * 128], ident_f, ident_f[0:R, 0:R])
            cp(dst_t, dst_ap_fn(cc), pt, pt[:, 0:R])

    def attn_tiles(streams, sinkbc):
        banks6 = ps1 + psB_halves
        for h in range(8):
            for si, (M, nown, qf, kf, mask_ap, vprev, vown, W, out_t, out_fn) in enumerate(streams):
                nk = 128 + nown
                bk = banks6[3 * si:3 * si + 3]
                sm, p_b, pT, o_b, sc = W["sm"], W["p_b"], W["pT"], W["o_b"], W["sc"]
                i_, two = h % 4, h // 4
                qt, qa = qf(i_, two)
                kt, ka = kf(two)
                s_ps = bk[0]
                mm(s_ps, s_ps[0:M, 0:nk], qt, qa, kt, ka)
                stt(sm, sm[0:M, 0:nk], s_ps, s_ps[0:M, 0:nk], 0.125, mask0, mask_ap, ALU.mult, ALU.add, extra_r=[mask1])
                P.add("dve", (lambda o_, i2: (lambda e: e.reduce_max(out=o_, in_=i2, axis=AX.X)))(sc[0:M, h:h + 1], sm[0:M, 0:nk]),
                      r=[sm], w=[sc])
                ts(sc, sc[0:M, 8 + h:9 + h], sc, sc[0:M, h:h + 1], sinkbc[0:M, h:h + 1], -1.0, ALU.max, ALU.mult, extra_r=[sinkbc])
                act(p_b, p_b[0:M, 0:nk], sm, sm[0:M, 0:nk], AF.Exp, bias=sc[0:M, 8 + h:9 + h],
                    accum=sc[0:M, 16 + h:17 + h], extra_r=[sc], extra_w=[sc])
                act(sc, sc[0:M, 24 + h:25 + h], sinkbc, sinkbc[0:M, h:h + 1], AF.Exp, bias=sc[0:M, 8 + h:9 + h], extra_r=[sc])
                tt(sc, sc[0:M, 32 + h:33 + h], sc, sc[0:M, 16 + h:17 + h], sc, sc[0:M, 24 + h:25 + h], ALU.add)
                recip(sc, sc[0:M, 32 + h:33 + h], sc, sc[0:M, 32 + h:33 + h])
                pT_ps = bk[1]
                pv = pT_ps[:, 0:128].bitcast(BF16).rearrange("p (a b) -> p a b", a=2)
                tr(pT_ps, pv[:, 0, 0:M], p_b, p_b[0:M, 0:128], ident_b, ident_b[0:M, 0:M])
                tr(pT_ps, pv[0:nown, 1, 0:M], p_b, p_b[0:M, 128:128 + nown], ident_b, ident_b[0:M, 0:M])
                cp(pT, pT[:, 0, 0:M], pT_ps, pv[:, 0, 0:M], eng="act")
                cp(pT, pT[0:nown, 1, 0:M], pT_ps, pv[0:nown, 1, 0:M], eng="act")
                o_ps = bk[2]
                mm(o_ps, o_ps[0:M, 0:64], pT, pT[:, 0, 0:M], vprev[0], vprev[1][:, two * 64:(two + 1) * 64], start=True, stop=False)
                mm(o_ps, o_ps[0:M, 0:64], pT, pT[0:nown, 1, 0:M], vown[0], vown[1][0:nown, two * 64:(two + 1) * 64], start=False, stop=True)
                ts(o_b, o_b[0:M, h * 64:(h + 1) * 64], o_ps, o_ps[0:M, 0:64], sc[0:M, 32 + h:33 + h], None, ALU.mult, extra_r=[sc])
        for (M, nown, qf, kf, mask_ap, vprev, vown, W, out_t, out_fn) in streams:
            o_b = W["o_b"]
            ot_ps = nps()
            ov = ot_ps[:, 0:256].bitcast(BF16).rearrange("p (a b) -> p a b", a=4)
            for c4 in range(4):
                tr(ot_ps, ov[:, c4, 0:M], o_b, o_b[0:M, c4 * 128:(c4 + 1) * 128], ident_b, ident_b[0:M, 0:M])
            cp(out_t, out_fn(), ot_ps, ov[:, :, 0:M])

    def ab_layer(layer):
        i = layer // 2
        areset()
        g0 = load_gamma(layer * 4 + 0)
        g1 = load_gamma(layer * 4 + 1)
        Win = al([128, 8, 1792], BF16, name="Win")
        Wout = al([128, 8, D], BF16, name="Wout")
        diag = al([128, 4, 31, 128], BF16, name="diag")
        wT = al([128, 4, 31], F32, name="wT")
        pcol = al([128, 4, 3], F32, name="pcol")
        rowtmp = al([32, 512], F32, name="rowtmp")
        sinkbc = al([128, 8], F32, name="sink")
        wsrc = I["ab_w_in"]
        dma("pool", Win, Win[:, :, 0:1024], TI["ab_w_in"], wsrc[i, :, 0:1024].rearrange("(k p) c -> p k c", p=128))
        for two in range(2):
            for kc in range(8):
                dma("pool", Win, Win[:, kc, 1024:1536].rearrange("p (i t d) -> p t i d", t=2, d=64)[:, two],
                    TI["ab_w_in"], wsrc[i, kc * 128:(kc + 1) * 128, 1024 + two * 256:1024 + (two + 1) * 256]
                    .rearrange("p (i d) -> p i d", d=64))
        dma("pool", Win, Win[:, :, 1536:1792], TI["ab_w_in"], wsrc[i, :, 1536:1792].rearrange("(k p) c -> p k c", p=128))
        dma("pool", Wout, Wout[:], TI["ab_w_out"], I["ab_w_out"][i].rearrange("(k p) c -> p k c", p=128))
        dma("sp", sinkbc, sinkbc[:], TI["attn_sinks"], I["attn_sinks"][i:i + 1, :].partition_broadcast(128))
        load_cols([(TI["conf_dw_w"], I["conf_dw_w"][i, j:j + 1, :]) for j in range(31)], 31, 512, wT,
                  lambda cc: wT[:, cc, :], rowtmp)
        load_cols([(TI["conf_dw_b"], I["conf_dw_b"][i:i + 1, :]), (TI["conf_ln_g"], I["conf_ln_g"][i:i + 1, :]),
                   (TI["conf_ln_b"], I["conf_ln_b"][i:i + 1, :])], 3, 512, pcol, lambda cc: pcol[:, cc, :], rowtmp)
        for cc in range(4):
            for j in range(31):
                ts(diag, diag[:, cc, j, :], ident_b, ident_b[:], wT[:, cc, j:j + 1], None, ALU.mult, extra_r=[wT])

        amark = apos[0]
        hT = [al([128, 8, 512], BF16, n=4, name="hT")] * 2
        hTh = al([128, 8, 128], BF16, name="hTh")
        kT_all = al([128, 17 * 128], BF16, n=17, name="kT")
        v_all = al([128, 17, 128], BF16, n=17, name="v")
        qT = al([128, 4, 512], BF16, name="qT")
        aTe = [al([128, 4, 544], BF16, name="aTe") for _ in range(2)]
        a32l = al([128, 4, 32], F32, name="a32l")
        sg = [al([128, 512], F32, name="sg") for _ in range(2)]
        c32 = al([128, 4, 512], F32, name="c32")
        cb16 = al([128, 4, 512], BF16, name="cb16")
        csq = al([128, 4, 512], BF16, name="csq")
        mean = al([128, 512], F32, name="mean")
        var = al([128, 512], F32, name="var")
        t1 = [al([128, 512], F32, name="t1") for _ in range(2)]
        catT = [al([128, 8, 512], BF16, name="catT")] * 2
        xtg = [al([128, D], F32, name="xtg") for _ in range(4)]
        xn = [al([128, D], BF16, name="xn")] * 2
        kv32 = [al([128, 256], F32, name="kv32")] * 2
        mix32 = [al([128, D], F32, name="mix")] * 2
        AW = [dict(sm=al([128, 256], F32, name="sm"), p_b=al([128, 256], BF16, name="pb"),
                   pT=al([128, 2, 128], BF16, name="pT"), o_b=al([128, 512], BF16, name="ob"),
                   sc=al([128, 40], F32, name="sc")) for _ in range(2)]

        def glu(hT_t, hT_ap_fn, n, cc, sg_t):
            p1 = nps()
            for kc in range(8):
                mm(p1, p1[:, 0:n], Win, Win[:, kc, cc * 128:(cc + 1) * 128], hT_t, hT_ap_fn(kc), start=(kc == 0), stop=(kc == 7))
            p2 = nps()
            for kc in range(8):
                mm(p2, p2[:, 0:n], Win, Win[:, kc, 512 + cc * 128:512 + (cc + 1) * 128], hT_t, hT_ap_fn(kc),
                   start=(kc == 0), stop=(kc == 7))
            act(sg_t, sg_t[:, 0:n], p2, p2[:, 0:n], AF.Sigmoid)
            return p1

        def conf_tail(n, rhs_fn, rhs_t, cat_t, cat_fn):
            for cc in range(4):
                pc = nps()
                for j in range(31):
                    mm(pc, pc[:, 0:n], diag, diag[:, cc, j, :], rhs_t, rhs_fn(cc, j), start=(j == 0), stop=(j == 30))
                act(c32, c32[:, cc, 0:n], pc, pc[:, 0:n], AF.Identity, bias=pcol[:, cc, 0:1], extra_r=[pcol])
                act(csq, csq[:, cc, 0:n], pc, pc[:, 0:n], AF.Square, bias=pcol[:, cc, 0:1], extra_r=[pcol])
                cp(cb16, cb16[:, cc, 0:n], c32, c32[:, cc, 0:n])
            st1 = nps()
            for cc in range(4):
                mm(st1, st1[:, 0:n], ones_b, ones_b[:], cb16, cb16[:, cc, 0:n], start=(cc == 0), stop=(cc == 3))
            st2 = nps()
            for cc in range(4):
                mm(st2, st2[:, 0:n], ones_b, ones_b[:], csq, csq[:, cc, 0:n], start=(cc == 0), stop=(cc == 3))
            ts(mean, mean[:, 0:n], st1, st1[:, 0:n], 1.0 / 512, None, ALU.mult)
            tt(var, var[:, 0:n], mean, mean[:, 0:n], mean, mean[:, 0:n], ALU.mult)
            stt(var, var[:, 0:n], st2, st2[:, 0:n], 1.0 / 512, var, var[:, 0:n], ALU.mult, ALU.subtract)
            act(var, var[:, 0:n], var, var[:, 0:n], AF.Sqrt, bias=1e-5)
            recip(var, var[:, 0:n], var, var[:, 0:n])
            for cc in range(4):
                t_ = t1[cc % 2]
                tt(t_, t_[:, 0:n], c32, c32[:, cc, 0:n], mean, mean[:, 0:n], ALU.subtract)
                tt(t_, t_[:, 0:n], t_, t_[:, 0:n], var, var[:, 0:n], ALU.mult)
                act(cat_t, cat_fn(cc), t_, t_[:, 0:n], AF.Silu, bias=pcol[:, cc, 2:3], scale=pcol[:, cc, 1:2], extra_r=[pcol])

        norm_T(xhalo, xhalo[:], 128, g0, hTh, hTh, 0, xn[0])
        for cc in range(4):
            p1 = glu(hTh, lambda kc: hTh[:, kc, :], 128, cc, sg[0])
            tt(t1[0], t1[0][:, 0:128], p1, p1[:, 0:128], sg[0], sg[0][:, 0:128], ALU.mult)
            ts(aTe[0], aTe[0][:, cc, 0:30], t1[0], t1[0][:, 98:128], hasprev[:, 0:1], None, ALU.mult, extra_r=[hasprev])
        pk = nps()
        for kc in range(8):
            mm(pk, pk[:, 0:128], Win, Win[:, kc, 1536:1664], hTh, hTh[:, kc, :], start=(kc == 0), stop=(kc == 7))
        cp(kT_all.s(0), kT_all[:, 0:128], pk, pk[:, 0:128])
        pv_ = nps()
        for kc in range(8):
            mm(pv_, pv_[:, 0:256], hTh, hTh[:, kc, :], Win, Win[:, kc, 1536:1792], start=(kc == 0), stop=(kc == 7))
        cp(v_all.s(0), v_all[:, 0, :], pv_, pv_[:, 128:256])

        for tg in range(4):
            h_ = hT[tg % 2]
            ae = aTe[tg % 2]
            ct = catT[tg % 2]
            for lt in range(4):
                t = tg * 4 + lt
                dma("sp", xtg[lt], xtg[lt][:], state["xsrc_t"].s(t), xrows(t))
                norm_T(xtg[lt], xtg[lt][:], 128, g0, h_, h_.s(lt), lt * 128, xn[lt % 2])
            for cc in range(4):
                s_ = sg[cc % 2]
                p1 = glu(h_, lambda kc: h_[:, kc, :], 512, cc, s_)
                tt(ae, ae[:, cc, 30:542], p1, p1[:, :], s_, s_[:, :], ALU.mult)
                if tg == 3:
                    tt(a32l, a32l[:, cc, :], p1, p1[:, 480:512], s_, s_[:, 480:512], ALU.mult)
            if tg > 0:
                cp(ae, ae[:, :, 0:30], aTe[(tg - 1) % 2], aTe[(tg - 1) % 2][:, :, 512:542])
            conf_tail(512, lambda cc, j: ae[:, cc, j:j + 512], ae, ct, lambda cc: ct[:, cc, :])
            for qi in range(4):
                pq = nps()
                for kc in range(8):
                    mm(pq, pq[:, :], Win, Win[:, kc, 1024 + qi * 128:1024 + (qi + 1) * 128], h_, h_[:, kc, :],
                       start=(kc == 0), stop=(kc == 7))
                cp(qT, qT[:, qi, :], pq, pq[:, :], eng="act")
            pk = nps()
            for kc in range(8):
                mm(pk, pk[:, :], Win, Win[:, kc, 1536:1664], h_, h_[:, kc, :], start=(kc == 0), stop=(kc == 7))
            cp(kT_all.s(tg * 4 + 1, tg * 4 + 5), kT_all[:, (tg * 4 + 1) * 128:(tg * 4 + 5) * 128], pk, pk[:, :])
            for lt in range(4):
                t = tg * 4 + lt
                kv = kv32[lt % 2]
                pv_ = nps()
                for kc in range(8):
                    mm(pv_, pv_[:, 0:256], h_.s(lt), h_[:, kc, lt * 128:(lt + 1) * 128], Win, Win[:, kc, 1536:1792],
                       start=(kc == 0), stop=(kc == 7))
                cp(kv, kv[:], pv_, pv_[:, 0:256], eng="act")
                cp(v_all.s(t + 1), v_all[:, t + 1, :], kv, kv[:, 128:256])
                if t == NT - 1:
                    dma("sp", TO["wk_p"], O["wk_p"][i], kv, kv[:, 0:128])
                    dma("sp", TO["wv_p"], O["wv_p"][i], kv, kv[:, 128:256])
            for lp in range(2):
                streams = []
                for lt in (2 * lp, 2 * lp + 1):
                    t = tg * 4 + lt
                    mk = mask0 if t == 0 else mask1
                    streams.append((128, 128,
                                    (lambda lt_: (lambda i_, two: (qT, qT[64 * two:64 * two + 64, i_, lt_ * 128:(lt_ + 1) * 128])))(lt),
                                    (lambda t_: (lambda two: (kT_all.s(t_, t_ + 2), kT_all[64 * two:64 * two + 64, t_ * 128:(t_ + 2) * 128])))(t),
                                    mk[:, :], (v_all.s(t), v_all[:, t, :]), (v_all.s(t + 1), v_all[:, t + 1, :]),
                                    AW[lt % 2], ct, (lambda lt_: (lambda: ct[:, 4:8, lt_ * 128:(lt_ + 1) * 128]))(lt)))
                attn_tiles(streams, sinkbc)
            for lt in range(4):
                t = tg * 4 + lt
                mx_ = mix32[lt % 2]
                for dh in range(2):
                    po = nps()
                    for c8 in range(8):
                        mm(po, po[:, :], ct, ct[:, c8, lt * 128:(lt + 1) * 128], Wout, Wout[:, c8, dh * 512:(dh + 1) * 512],
                           start=(c8 == 0), stop=(c8 == 7))
                    cp(mx_, mx_[:, dh * 512:(dh + 1) * 512], po, po[:, :], eng="act")
                post_norm_add(mx_, mx_[:], 128, g1, xtg[lt], xtg[lt][:], xtg[lt], xtg[lt][:], xt2[lt % 2])
                dma("sp", T_xres.s(t), xres[t * 128:(t + 1) * 128, :], xtg[lt], xtg[lt][:])
        pa = nps()
        for cc in range(4):
            tr(pa, pa[0:32, cc * 128:(cc + 1) * 128], a32l, a32l[:, cc, :], ident_f, ident_f[:])
        cp(mean, mean[0:32, :], pa, pa[0:32, :])
        dma("sp", TO["cc_p"], O["cc_p"][i], mean, mean[2:32, :])

        P.barrier()
        apos[0] = amark
        sg = [al([128, 512], F32, name="sg")]
        c32 = al([128, 4, 512], F32, name="c32")
        cb16 = al([128, 4, 512], BF16, name="cb16")
        csq = al([128, 4, 512], BF16, name="csq")
        mean = al([128, 512], F32, name="mean")
        var = al([128, 512], F32, name="var")
        t1 = [al([128, 512], F32, name="t1") for _ in range(2)]
        xn = [al([128, D], BF16, name="xn")]
        mix32 = [al([128, D], F32, name="mix")]
        AW = [dict(sm=al([128, 256], F32, name="sm"), p_b=al([128, 256], BF16, name="pb"),
                   pT=al([128, 2, 128], BF16, name="pT"), o_b=al([128, 512], BF16, name="ob"),
                   sc=al([128, 40], F32, name="sc")) for _ in range(2)]
        hTs = al([128, 8, NS], BF16, name="hTs")
        norm_T(XS, XS[:], NS, g0, hTs, hTs, 0, xn[0])
        a_tok = al([NS, 512], F32, name="a_tok")
        sgs = al([NS, 512], F32, name="sgs")
        pu1 = nps()
        for kc in range(8):
            mm(pu1, pu1[0:NS, :], hTs, hTs[:, kc, :], Win, Win[:, kc, 0:512], start=(kc == 0), stop=(kc == 7))
        pu2 = nps()
        for kc in range(8):
            mm(pu2, pu2[0:NS, :], hTs, hTs[:, kc, :], Win, Win[:, kc, 512:1024], start=(kc == 0), stop=(kc == 7))
        act(sgs, sgs[:], pu2, pu2[0:NS, :], AF.Sigmoid)
        tt(a_tok, a_tok[:], pu1, pu1[0:NS, :], sgs, sgs[:], ALU.mult)
        dma("sp", TO["cc_s"], O["cc_s"][i], a_tok, a_tok[:])
        kvs = al([NS, 256], F32, name="kvs")
        pkv = nps()
        for kc in range(8):
            mm(pkv, pkv[0:NS, 0:256], hTs, hTs[:, kc, :], Win, Win[:, kc, 1536:1792], start=(kc == 0), stop=(kc == 7))
        cp(kvs, kvs[:], pkv, pkv[0:NS, 0:256])
        dma("sp", TO["wk_s"], O["wk_s"][i], kvs, kvs[:, 0:128])
        dma("sp", TO["wv_s"], O["wv_s"][i], kvs, kvs[:, 128:256])
        dma("sp", T_scr, scr[:, 0:128], kvs, kvs[:, 128:256])
        vrow = al([1, NS, 128], F32, name="vrow")
        dma("sp", vrow, vrow[:], T_scr, scr[:, 0:128].rearrange("(o b) c -> o b c", o=1))
        vrow_b = al([1, NS, 128], BF16, name="vrowb")
        cp(vrow_b, vrow_b[:], vrow, vrow[:])
        aTs = al([128, 4, NS, 32], BF16, name="aTs")
        for cc in range(4):
            p1 = glu(hTs, lambda kc: hTs[:, kc, :], NS, cc, sg[0])
            tt(aTs, aTs[:, cc, :, 30], p1, p1[:, 0:NS], sg[0], sg[0][:, 0:NS], ALU.mult)
        hst = [al([32, 512], F32, name="hst") for _ in range(2)]
        for b in range(NS):
            hs = hst[b % 2]
            dma("sp", hs, hs[0:30, :], TI["cconf"], I["cconf"][i, b])
            ph = nps()
            phv = ph[:, 0:128].rearrange("p (c j) -> p c j", c=4)
            for cc in range(4):
                tr(ph, phv[:, cc, 0:30], hs, hs[0:30, cc * 128:(cc + 1) * 128], ident_f, ident_f[0:30, 0:30])
            cp(aTs, aTs[:, :, b, 0:30], ph, phv[:, :, 0:30])
        cts = al([128, 8, NS], BF16, name="cts")
        conf_tail(NS, lambda cc, j: aTs[:, cc, :, j], aTs, cts, lambda cc: cts[:, cc, :])
        qTs = al([128, 4, NS], BF16, name="qTs")
        for qi in range(4):
            pq = nps()
            for kc in range(8):
                mm(pq, pq[:, 0:NS], Win, Win[:, kc, 1024 + qi * 128:1024 + (qi + 1) * 128], hTs, hTs[:, kc, :],
                   start=(kc == 0), stop=(kc == 7))
            cp(qTs, qTs[:, qi, :], pq, pq[:, 0:NS])
        kTs = al([128, NS], BF16, name="kTs")
        pk = nps()
        for kc in range(8):
            mm(pk, pk[:, 0:NS], Win, Win[:, kc, 1536:1664], hTs, hTs[:, kc, :], start=(kc == 0), stop=(kc == 7))
        cp(kTs, kTs[:], pk, pk[:, 0:NS])
        kc_b = [al([128, 128], BF16, name="kcb") for _ in range(2)]
        vc_b = [al([128, 128], BF16, name="vcb") for _ in range(2)]
        kTe = [al([128, 144], BF16, name="kTe") for _ in range(2)]
        for bp in range(NS // 2):
            streams = []
            for b in (2 * bp, 2 * bp + 1):
                kc_, vc_, ke = kc_b[b % 2], vc_b[b % 2], kTe[b % 2]
                dma("pool", kc_, kc_[:], TI["ck"], I["ck"][i, b])
                dma("pool", vc_, vc_[:], TI["cv"], I["cv"][i, b])
                pkt = nps()
                pkv_ = pkt[:, 0:64].bitcast(BF16)
                tr(pkt, pkv_[:, 0:128], kc_, kc_[:], ident_b, ident_b[:])
                cp(ke, ke[:, 0:128], pkt, pkv_[:, 0:128])
                cp(ke, ke[:, 128:129], kTs, kTs[:, b:b + 1])
                streams.append((1, 1,
                                (lambda b_: (lambda i_, two: (qTs, qTs[64 * two:64 * two + 64, i_, b_:b_ + 1])))(b),
                                (lambda ke_: (lambda two: (ke_, ke_[64 * two:64 * two + 64, 0:129])))(ke),
                                mask1[0:1, 0:129], (vc_, vc_[:, :]), (vrow_b, vrow_b[0:1, b, :]),
                                AW[b % 2], cts, (lambda b_: (lambda: cts[:, 4:8, b_:b_ + 1]))(b)))
            attn_tiles(streams, sinkbc)
        mxs = mix32[0]
        for dh in range(2):
            po = nps()
            for c8 in range(8):
                mm(po, po[0:NS, :], cts, cts[:, c8, :], Wout, Wout[:, c8, dh * 512:(dh + 1) * 512], start=(c8 == 0), stop=(c8 == 7))
            cp(mxs, mxs[0:NS, dh * 512:(dh + 1) * 512], po, po[0:NS, :])
        post_norm_add(mxs, mxs[0:NS, :], NS, g1, XS, XS[:], XS, XS[:], xt2[0])
        state["xsrc"] = xres
        state["xsrc_t"] = T_xres


    def ssd_layer(layer):
        i = layer // 2
        areset()
        g0 = load_gamma(layer * 4 + 0)
        g1 = load_gamma(layer * 4 + 1)
        win = I["ssd_w_in"]
        Wout = al([128, 16, D], BF16, name="Wout")
        dma("pool", Wout, Wout[:], TI["ssd_w_out"], I["ssd_w_out"][i].rearrange("(k p) c -> p k c", p=128))
        ngb = al([128, 2048], BF16, name="ngb")
        dma("pool", ngb, ngb[:], TI["ssd_norm_g"], I["ssd_norm_g"][i:i + 1, :].partition_broadcast(128))
        dbc = al([128, 32], F32, name="dbc")
        abc = al([128, 32], F32, name="abc")
        dtb = al([128, 32], F32, name="dtb")
        dma("sp", dbc, dbc[:], TI["ssd_d"], I["ssd_d"][i:i + 1, :].partition_broadcast(128))
        dma("sp", abc, abc[:], TI["ssd_a_log"], I["ssd_a_log"][i:i + 1, :].partition_broadcast(128))
        dma("sp", dtb, dtb[:], TI["ssd_dt_bias"], I["ssd_dt_bias"][i:i + 1, :].partition_broadcast(128))
        act(abc, abc[:], abc, abc[:], AF.Exp)
        ts(abc, abc[:], abc, abc[:], -1.0, None, ALU.mult)
        diagD = al([128, 32, 128], BF16, name="diagD")
        for h in range(32):
            ts(diagD, diagD[:, h, :], ident_b, ident_b[:], dbc[:, h:h + 1], None, ALU.mult, extra_r=[dbc])
        cwT = al([128, 24, 5], F32, name="cwT")
        rowtmp = al([8, 512], F32, name="rowtmp")
        for blk in range(6):
            rows = [(TI["ssd_conv_w"], I["ssd_conv_w"][i, k:k + 1, blk * 512:(blk + 1) * 512]) for k in range(4)]
            rows.append((TI["ssd_conv_b"], I["ssd_conv_b"][i:i + 1, blk * 512:(blk + 1) * 512]))
            load_cols(rows, 5, 512, cwT, lambda cc: cwT[:, blk * 4 + cc, :], rowtmp)
        hist0 = al([128, 24, 4], BF16, name="hist0")
        hist3 = al([128, 24, 4], BF16, name="hist3")
        hist3f = al([128, 24, 3], F32, name="hist3f")
        Wb = [al([128, 8, 512], BF16, name="Wb") for _ in range(2)]
        wsel = [0]

        def load_wblk(c0, w):
            wsel[0] ^= 1
            wb = Wb[wsel[0]]
            dma("pool", wb, wb[:, :, 0:w], TI["ssd_w_in"], win[i, :, c0:c0 + w].rearrange("(k p) c -> p k c", p=128))
            return wb

        amark = apos[0]
        hT = al([128, 8, 512], BF16, n=4, name="hT")
        xn = al([128, D], BF16, name="xn")
        xtl = [al([128, D], F32, name="xtl")] * 2
        rawb = [al([128, 516], BF16, name="rawb") for _ in range(2)]
        dg = [al([128, 4, 128], BF16, name="dg") for _ in range(2)]
        xc = al([128, 16, 512], BF16, name="xc")
        BT = al([128, 4, 512], BF16, name="BT")
        CT = al([128, 4, 512], BF16, name="CT")
        zs = al([128, 4, 2048], BF16, n=4, name="zs")
        dts = al([128, 4, 64], F32, n=4, name="dts")
        xdt = al([128, 2048], BF16, name="xdt")
        xtok = al([128, 2048], BF16, name="xtok")
        xde = al([128, 2048], BF16, name="xde")
        hTh = Tl("hTh_alias", xde[:, 0:1024].rearrange("p (k m) -> p k m", k=8))
        hTh.lw, hTh.rd = xde.lw, xde.rd
        Btok = al([128, 512], BF16, name="Btok")
        sm_ = al([128, 8, 32], F32, name="small")
        hi_ = al([128, 32], BF16, name="hi")
        lo_ = al([128, 32], BF16, name="lo")
        Zhi = al([128, 8, 128], BF16, name="Zhi")
        Zlo = al([128, 8, 128], BF16, name="Zlo")
        Eb = al([128, 8, 128], BF16, name="Eb")
        cbm = al([128, 128], BF16, name="cbm")
        MT = al([128, 8, 128], BF16, name="MT")
        yacc = al([128, 2048], F32, name="yacc")
        yn = al([128, 2048], BF16, name="yn")
        yo = Tl("yo_alias", yn[:, 0:1024].bitcast(F32))
        yo.lw, yo.rd = yn.lw, yn.rd
        ynT = al([128, 16, 128], BF16, name="ynT")
        H = al([128, 2048], F32, name="H")
        Hb = al([128, 2048], BF16, name="Hb")

        def v3(ap, a):
            return ap.rearrange("p (a b) -> p a b", a=a)

        def conv_chunk(j, pu, n, dst_t, dst_ap, hist_src):
            rb = rawb[j % 2]
            d_ = dg[j % 2]
            cp(rb, rb[:, 3:3 + n], pu, pu[:, 0:n], eng="act")
            cp(rb, rb[:, 0:3], hist_src, hist_src[:, j, 0:3])
            cp(hist3, hist3[:, j, 0:3], rb, rb[:, n:n + 3])
            cp(hist3f, hist3f[:, j, :], pu, pu[:, n - 3:n], eng="act")
            for k in range(4):
                ts(d_, d_[:, k, :], ident_b, ident_b[:], cwT[:, j, k:k + 1], None, ALU.mult, extra_r=[cwT])

            def tail():
                pc = nps()
                for k in range(4):
                    mm(pc, pc[:, 0:n], d_, d_[:, k, :], rb, rb[:, k:k + n], start=(k == 0), stop=(k == 3))
                act(dst_t, dst_ap, pc, pc[:, 0:n], AF.Silu, bias=cwT[:, j, 4:5], extra_r=[cwT])
            return tail

        def dt_tile(pdt, M, dst_t, dst_ap64):
            tt(dst_t, dst_ap64[:, 0:32], pdt, pdt[0:M, 0:32], dtb, dtb[0:M, :], ALU.add)
            act(dst_t, dst_ap64[:, 0:32], dst_t, dst_ap64[:, 0:32], AF.Exp)
            act(dst_t, dst_ap64[:, 0:32], dst_t, dst_ap64[:, 0:32], AF.Ln, bias=1.0)
            tt(dst_t, dst_ap64[:, 32:64], dst_t, dst_ap64[:, 0:32], abc, abc[0:M, :], ALU.mult)

        norm_T(xhalo, xhalo[:], 128, g0, hTh, hTh, 0, xn)
        for blk in range(6):
            wb = load_wblk(2048 + blk * 512, 512)
            for jj in range(4):
                j = blk * 4 + jj
                pu = nps()
                for kc in range(8):
                    mm(pu, pu[:, 0:128], wb, wb[:, kc, jj * 128:(jj + 1) * 128], hTh, hTh[:, kc, :], start=(kc == 0), stop=(kc == 7))
                ts(hist0, hist0[:, j, 0:3], pu, pu[:, 125:128], hasprev[:, 0:1], None, ALU.mult, extra_r=[hasprev])

        def run_pass(full):
            memset(sm_, sm_[:, 6, :], 0.0, eng="dve")
            for tg in range(int(os.environ.get('SSD_TGS', '4')) if full else 4):
                hsrc = hist0 if tg == 0 else hist3
                for lt in range(4):
                    t = tg * 4 + lt
                    x = xtl[lt % 2]
                    dma("sp", x, x[:], state["xsrc_t"].s(t), xrows(t))
                    norm_T(x, x[:], 128, g0, hT, hT.s(lt), lt * 128, xn)
                pending = None
                for blk in range(6):
                    if not full and blk == 5:
                        continue
                    wb = load_wblk(2048 + blk * 512, 512)
                    for jj in range(4):
                        j = blk * 4 + jj
                        pu = nps()
                        for kc in range(8):
                            mm(pu, pu[:, :], wb, wb[:, kc, jj * 128:(jj + 1) * 128], hT, hT[:, kc, :], start=(kc == 0), stop=(kc == 7))
                        if pending is not None:
                            pending()
                        if j < 16:
                            pending = conv_chunk(j, pu, 512, xc, xc[:, j, :], hsrc)
                        elif j < 20:
                            pending = conv_chunk(j, pu, 512, BT, BT[:, j - 16, :], hsrc)
                        else:
                            pending = conv_chunk(j, pu, 512, CT, CT[:, j - 20, :], hsrc)
                pending()
                wb = load_wblk(5120, 32)
                for lt in range(4):
                    pdt = nps()
                    for kc in range(8):
                        mm(pdt, pdt[:, 0:32], hT.s(lt), hT[:, kc, lt * 128:(lt + 1) * 128], wb, wb[:, kc, 0:32], start=(kc == 0), stop=(kc == 7))
                    dt_tile(pdt, 128, dts.s(lt), dts[:, lt, :])
                if full and not os.environ.get('ZSKIP'):
                    for blk in range(4):
                        wb = load_wblk(blk * 512, 512)
                        for lt in range(4):
                            pz = nps()
                            for kc in range(8):
                                mm(pz, pz[:, :], hT.s(lt), hT[:, kc, lt * 128:(lt + 1) * 128], wb, wb[:, kc, :], start=(kc == 0), stop=(kc == 7))
                            act(zs.s(lt), zs[:, lt, blk * 512:(blk + 1) * 512], pz, pz[:, :], AF.Silu)
                for lt in range(4):
                    t = tg * 4 + lt
                    cols = slice(lt * 128, (lt + 1) * 128)
                    dtv = dts[:, lt, 0:32]
                    dta = dts[:, lt, 32:64]
                    dsl = dts.s(lt)
                    px = psA[:, 0:1024].bitcast(BF16)
                    for j in range(16):
                        tr(psA, px[:, j * 128:(j + 1) * 128], xc, xc[:, j, cols], ident_b, ident_b[:])
                    tt(xdt, v3(xdt[:], 32), psA, v3(px, 32), dsl, dtv.unsqueeze(2).to_broadcast([128, 32, 64]), ALU.mult)
                    if full and not os.environ.get('XSKIP'):
                        cp(xtok, xtok[:], psA, px)
                    pb = nps()
                    pbv = pb[:, 0:256].bitcast(BF16)
                    for g in range(4):
                        tr(pb, pbv[:, g * 128:(g + 1) * 128], BT, BT[:, g, cols], ident_b, ident_b[:])
                    cp(Btok, Btok[:], pb, pbv)
                    pl = nps()
                    mm(pl, pl[:, 0:32], ones_f, ones_f[:], dsl, dta)
                    mm(pl, pl[:, 32:64], tri_f, tri_f[:], dsl, dta)
                    cp(sm_, sm_[:, 0, :], pl, pl[:, 0:32])
                    cp(sm_, sm_[:, 1, :], pl, pl[:, 32:64])
                    tt(sm_, sm_[:, 6, :], sm_, sm_[:, 6, :], sm_, sm_[:, 0, :], ALU.add)
                    tt(sm_, sm_[:, 5, :], sm_, sm_[:, 0, :], sm_, sm_[:, 1, :], ALU.subtract)
                    act(sm_, sm_[:, 3, :], sm_, sm_[:, 5, :], AF.Exp)
                    act(sm_, sm_[:, 4, :], sm_, sm_[:, 0, :], AF.Exp)
                    tt(xde, v3(xde[:], 32), xdt, v3(xdt[:], 32), sm_, sm_[:, 3, :].unsqueeze(2).to_broadcast([128, 32, 64]), ALU.mult)
                    BSTOP = int(os.environ.get('BSTOP', '9'))
                    if full and BSTOP >= 2:
                        act(sm_, sm_[:, 2, :], sm_, sm_[:, 1, :], AF.Exp)
                        cp(hi_, hi_[:], dsl, dta)
                        tt(sm_, sm_[:, 5, :], dsl, dta, hi_, hi_[:], ALU.subtract)
                        cp(lo_, lo_[:], sm_, sm_[:, 5, :])
                        for g in range(4):
                            hs = slice(g * 8, (g + 1) * 8)
                            upb = up_b[:].unsqueeze(1).to_broadcast([128, 8, 128])
                            tt(Zhi, Zhi[:], up_b, upb, hi_, hi_[:, hs].unsqueeze(2).to_broadcast([128, 8, 128]), ALU.mult)
                            tt(Zlo, Zlo[:], up_b, upb, lo_, lo_[:, hs].unsqueeze(2).to_broadcast([128, 8, 128]), ALU.mult)
                            pD = v3(psB[:, :], 8)
                            for hh in range(8):
                                mm(psB, pD[:, hh, :], Zhi, Zhi[:, hh, :], tri_b, tri_b[:], start=True, stop=False)
                                mm(psB, pD[:, hh, :], Zlo, Zlo[:, hh, :], tri_b, tri_b[:], start=False, stop=True)
                            act(Eb, Eb[:, 0:4, :], psB, pD[:, 0:4, :], AF.Exp)
                            act(Eb, Eb[:, 4:8, :], psB, pD[:, 4:8, :], AF.Exp)
                            if BSTOP <= 2:
                                continue
                            pcb = nps()
                            mm(pcb, pcb[:, 0:128], BT, BT[:, g, cols], CT, CT[:, g, cols])
                            tt(cbm, cbm[:], pcb, pcb[:, 0:128], trimask_b, trimask_b[:], ALU.mult)
                            tt(MT, MT[:], Eb, Eb[:], cbm, cbm[:].unsqueeze(1).to_broadcast([128, 8, 128]), ALU.mult)
                            py = nps()
                            for hh in range(8):
                                h = g * 8 + hh
                                mm(py, py[:, hh * 64:(hh + 1) * 64], MT, MT[:, hh, :], xdt, xdt[:, h * 64:(h + 1) * 64], start=True, stop=False)
                                mm(py, py[:, hh * 64:(hh + 1) * 64], diagD, diagD[:, h, :], xtok, xtok[:, h * 64:(h + 1) * 64], start=False, stop=True)
                            po_ = nps()
                            mm(po_, po_[:, :], CT, CT[:, g, cols], Hb, Hb[:, g * 512:(g + 1) * 512])
                            tt(yo, v3(yo[:], 8), po_, v3(po_[:, :], 8), sm_, sm_[:, 2, hs].unsqueeze(2).to_broadcast([128, 8, 64]), ALU.mult)
                            tt(yacc, yacc[:, g * 512:(g + 1) * 512], py, py[:, :], yo, yo[:], ALU.add)
                        if BSTOP <= 3:
                            continue
                        tt(yacc, yacc[:], yacc, yacc[:], zs.s(lt), zs[:, lt, :], ALU.mult)
                        for g in range(4):
                            rs = rstd_of(yacc, yacc[:, g * 512:(g + 1) * 512], 128, 2 + g, ncols=512)
                            stt(yn, yn[:, g * 512:(g + 1) * 512], yacc, yacc[:, g * 512:(g + 1) * 512], rs, ngb,
                                ngb[:, g * 512:(g + 1) * 512], ALU.mult, ALU.mult, extra_r=[sd_t])
                        if BSTOP <= 4:
                            continue
                        pt = psA[:, 0:1024].bitcast(BF16).rearrange("p (k m) -> p k m", k=16)
                        for c in range(16):
                            tr(psA, pt[:, c, :], yn, yn[:, c * 128:(c + 1) * 128], ident_b, ident_b[:])
                        cp(ynT, ynT[:, 0:8, :], psA, pt[:, 0:8, :], eng="act")
                        cp(ynT, ynT[:, 8:16, :], psA, pt[:, 8:16, :], eng="act")
                        for dh in range(2):
                            po = nps()
                            for c in range(16):
                                mm(po, po[:, :], ynT, ynT[:, c, :], Wout, Wout[:, c, dh * 512:(dh + 1) * 512], start=(c == 0), stop=(c == 15))
                            cp(yacc, yacc[:, dh * 512:(dh + 1) * 512], po, po[:, :], eng="act")
                        x = xtl[lt % 2]
                        dma("sp", x, x[:], state["xsrc_t"].s(t), xrows(t))
                        post_norm_add(yacc, yacc[:, 0:D], 128, g1, x, x[:], x, x[:], xt2[lt % 2])
                        dma("sp", T_xres.s(t), xres[t * 128:(t + 1) * 128, :], x, x[:])
                    for g in range(4):
                        hs = slice(g * 8, (g + 1) * 8)
                        psg = nps()
                        mm(psg, psg[:, :], Btok, Btok[:, g * 128:(g + 1) * 128], xde, xde[:, g * 512:(g + 1) * 512])
                        Hg = H[:, g * 512:(g + 1) * 512]
                        tt(H, v3(Hg, 8), H, v3(Hg, 8), sm_, sm_[:, 4, hs].unsqueeze(2).to_broadcast([128, 8, 64]), ALU.mult)
                        tt(H, Hg, H, Hg, psg, psg[:, :], ALU.add)
                    if full:
                        cp(Hb, Hb[:], H, H[:], eng="act")

        SSTOP = int(os.environ.get('SSD_STOP', '9'))

        def prompt_part():
            if SSTOP <= 1:
                return
            memset(H, H[:], 0.0, eng="dve")
            run_pass(False)
            if os.environ.get('EXTRA_A'):
                run_pass(False)
            if SSTOP <= 2:
                return
            extmp = yacc
            exchange("s", [(H, H[:], 0, 2048)], extmp)
            exchange("l", [(sm_, sm_[:, 6, :], 0, 32)], extmp)
            memset(H, H[:], 0.0, eng="dve")
            for m in range(3):
                gather_rows("s", yacc, yacc[:], idxchain, idxchain[:, m:m + 1], 0, 2048)
                gather_rows("l", sm_, sm_[:, 7, :], idxchain, idxchain[:, m:m + 1], 0, 32)
                ts(sm_, sm_[:, 7, :], sm_, sm_[:, 7, :], vmask[:, m:m + 1], None, ALU.mult, extra_r=[vmask])
                act(sm_, sm_[:, 7, :], sm_, sm_[:, 7, :], AF.Exp)
                tt(H, v3(H[:], 32), H, v3(H[:], 32), sm_, sm_[:, 7, :].unsqueeze(2).to_broadcast([128, 32, 64]), ALU.mult)
                stt(H, H[:], yacc, yacc[:], vmask[:, m:m + 1], H, H[:], ALU.mult, ALU.add, extra_r=[vmask])
            cp(Hb, Hb[:], H, H[:], eng="act")
            if SSTOP <= 3:
                return
            run_pass(True)
            if SSTOP <= 4:
                return
            pst = [psA, psB]
            for c in range(16):
                p_ = pst[c % 2]
                tr(p_, p_[:, 0:128], H, H[:, c * 128:(c + 1) * 128], ident_f, ident_f[:])
                cp(yacc, yacc[:, (c % 8) * 128:(c % 8 + 1) * 128], p_, p_[:, 0:128])
                if c % 8 == 7:
                    c0 = c - 7
                    dma("sp", TO["ssm_p"], O["ssm_p"][i, c0 * 128:(c0 + 8) * 128, :].rearrange("(c p) n -> p c n", p=128),
                        yacc, v3(yacc[:, 0:1024], 8))
            ph = nps()
            tr(ph, ph[0:72, 0:128], hist3f, hist3f[:].rearrange("p j r -> p (j r)"), ident_f, ident_f[:])
            cp(yo, yo[0:72, 0:128], ph, ph[0:72, 0:128])
            for j in range(24):
                dma("sp", TO["sc_p"], O["sc_p"][i, :, j * 128:(j + 1) * 128], yo, yo[j * 3:(j + 1) * 3, 0:128])


        if not os.environ.get('SKIP_PROMPT'):
            prompt_part()

        if SSTOP <= 5:
            state["xsrc"] = xres
            state["xsrc_t"] = T_xres
            return
        P.barrier()
        apos[0] = amark
        xn = al([128, D], BF16, name="xn")
        hTs = al([128, 8, NS], BF16, name="hTs")
        zs_s = al([NS, 2048], F32, name="zs_s")
        raw_s = al([NS, 3072], F32, name="raw_s")
        xbc_s = al([NS, 3072], F32, name="xbc_s")
        dts_s = al([NS, 64], F32, name="dts_s")
        hs_b = al([NS, 3, 512], F32, name="hs_b")
        cw_b = al([NS, 5, 512], F32, name="cw_b")
        tmp_s = al([NS, 2048], F32, name="tmp_s")
        sel_t = al([32, 128], F32, name="sel")
        dma("sp", sel_t, sel_t[:], TI["c_sel"], I["c_sel"].ap())
        norm_T(XS, XS[:], NS, g0, hTs, hTs, 0, xn)
        for blk in range(11):
            c0 = blk * 512
            w = 512 if blk < 10 else 32
            wb = load_wblk(c0, w)
            pu = nps()
            for kc in range(8):
                mm(pu, pu[0:NS, 0:w], hTs, hTs[:, kc, :], wb, wb[:, kc, 0:w], start=(kc == 0), stop=(kc == 7))
            if blk < 4:
                act(zs_s, zs_s[:, c0:c0 + 512], pu, pu[0:NS, :], AF.Silu)
            elif blk < 10:
                cp(raw_s, raw_s[:, c0 - 2048:c0 - 1536], pu, pu[0:NS, :])
            else:
                dt_tile(pu, NS, dts_s, dts_s[:, :])
        dma("sp", TO["sc_s"], O["sc_s"][i], raw_s, raw_s[:])
        for blk in range(6):
            cs_ = slice(blk * 512, (blk + 1) * 512)
            dma("sp", hs_b, hs_b[:], TI["csc"], I["csc"][i, :, :, cs_])
            for k in range(4):
                dma("sp", cw_b, cw_b[:, k, :], TI["ssd_conv_w"], I["ssd_conv_w"][i, k:k + 1, cs_].partition_broadcast(NS))
            dma("sp", cw_b, cw_b[:, 4, :], TI["ssd_conv_b"], I["ssd_conv_b"][i:i + 1, cs_].partition_broadcast(NS))
            acc = xbc_s[:, cs_]
            tt(xbc_s, acc, raw_s, raw_s[:, cs_], cw_b, cw_b[:, 3, :], ALU.mult)
            tt(xbc_s, acc, xbc_s, acc, cw_b, cw_b[:, 4, :], ALU.add)
            for k in range(3):
                tt(tmp_s, tmp_s[:, 0:512], hs_b, hs_b[:, k, :], cw_b, cw_b[:, k, :], ALU.mult)
                tt(xbc_s, acc, xbc_s, acc, tmp_s, tmp_s[:, 0:512], ALU.add)
        act(xbc_s, xbc_s[:], xbc_s, xbc_s[:], AF.Silu)
        SAMP = int(os.environ.get('SAMP_STOP', '9'))
        if SAMP <= 1:
            return
        xdt_s = al([NS, 2048], F32, name="xdt_s")
        tt(xdt_s, v3(xdt_s[:], 32), xbc_s, v3(xbc_s[:, 0:2048], 32), dts_s, dts_s[:, 0:32].unsqueeze(2).to_broadcast([NS, 32, 64]), ALU.mult)
        dma("sp", SCR["xdt"][1], SCR["xdt"][0].ap(), xdt_s, xdt_s[:])
        dma("sp", SCR["B"][1], SCR["B"][0].ap(), xbc_s, xbc_s[:, 2048:2560])
        dma("sp", SCR["C"][1], SCR["C"][0].ap(), xbc_s, xbc_s[:, 2560:3072])
        xq = al([128, NS, 16], F32, name="xq")
        dma("sp", xq, xq[:], SCR["xdt"][1], SCR["xdt"][0].ap().rearrange("b (q j) -> q b j", j=16))
        Bbc = al([128, NS, 128], F32, name="Bbc")
        Cbc = al([128, NS, 128], F32, name="Cbc")
        for g in range(4):
            dma("sp", Bbc, Bbc[32 * g:32 * (g + 1), :, :], SCR["B"][1], SCR["B"][0][:, g * 128:(g + 1) * 128].partition_broadcast(32))
            dma("sp", Cbc, Cbc[32 * g:32 * (g + 1), :, :], SCR["C"][1], SCR["C"][0][:, g * 128:(g + 1) * 128].partition_broadcast(32))
        da_s = al([NS, 32], F32, name="da_s")
        act(da_s, da_s[:], dts_s, dts_s[:, 32:64], AF.Exp)
        pda = nps()
        tr(pda, pda[0:32, 0:NS], da_s, da_s[:], ident_f, ident_f[0:NS, 0:NS])
        daT = al([32, NS], F32, name="daT")
        cp(daT, daT[:], pda, pda[0:32, 0:NS])
        pdq = nps()
        mm(pdq, pdq[:, 0:NS], sel_t, sel_t[:], daT, daT[:])
        da_q = al([128, NS], F32, name="da_q")
        cp(da_q, da_q[:], pdq, pdq[:, 0:NS])
        if SAMP <= 2:
            return
        yq = al([128, NS, 16], F32, name="yq")
        St = [al([128, 16, 128], F32, name="St")] * 2
        tmp3 = al([128, 16, 128], F32, name="tmp3")
        for b in range(NS):
            S = St[b % 2]
            dma("sp", S, S[:], TI["cssm"], I["cssm"][i, b].rearrange("(q j) n -> q j n", j=16))
            tt(tmp3, tmp3[:], xq, xq[:, b, :].unsqueeze(2).to_broadcast([128, 16, 128]),
               Bbc, Bbc[:, b, :].unsqueeze(1).to_broadcast([128, 16, 128]), ALU.mult)
            stt(S, S[:], S, S[:], da_q[:, b:b + 1], tmp3, tmp3[:], ALU.mult, ALU.add, extra_r=[da_q])
            dma("sp", TO["ssm_s"], O["ssm_s"][i, b].rearrange("(q j) n -> q j n", j=16), S, S[:])
            tt(tmp3, tmp3[:], S, S[:], Cbc, Cbc[:, b, :].unsqueeze(1).to_broadcast([128, 16, 128]), ALU.mult)
            P.add("dve", (lambda o_, i_: (lambda e: e.reduce_sum(out=o_, in_=i_, axis=AX.X)))(yq[:, b, :], tmp3[:]),
                  r=[tmp3], w=[yq])
        if SAMP <= 3:
            return
        dma("sp", SCR["y"][1], SCR["y"][0].ap().rearrange("b (q j) -> q b j", j=16), yq, yq[:])
        y_s = Tl("y_s_alias", raw_s[:, 0:2048])
        y_s.lw, y_s.rd = raw_s.lw, raw_s.rd
        dma("sp", y_s, y_s[:], SCR["y"][1], SCR["y"][0].ap())
        if SAMP <= 4:
            return
        tt(tmp_s, v3(tmp_s[:], 32), xbc_s, v3(xbc_s[:, 0:2048], 32), dbc, dbc[0:NS, :].unsqueeze(2).to_broadcast([NS, 32, 64]), ALU.mult)
        tt(y_s, y_s[:], y_s, y_s[:], tmp_s, tmp_s[:], ALU.add)
        tt(y_s, y_s[:], y_s, y_s[:], zs_s, zs_s[:], ALU.mult)
        yn_s = Tl("yn_s_alias", tmp_s[:, 0:1024].bitcast(BF16))
        yn_s.lw, yn_s.rd = tmp_s.lw, tmp_s.rd
        for g in range(4):
            rs = rstd_of(y_s, y_s[:, g * 512:(g + 1) * 512], NS, 2 + g, ncols=512)
            stt(yn_s, yn_s[:, g * 512:(g + 1) * 512], y_s, y_s[:, g * 512:(g + 1) * 512], rs, ngb,
                ngb[0:NS, g * 512:(g + 1) * 512], ALU.mult, ALU.mult, extra_r=[sd_t])
        if SAMP <= 5:
            return
        ptv_ = psA[:, 0:1024].bitcast(BF16).rearrange("p (k m) -> p k m", k=16)
        for c in range(16):
            tr(psA, ptv_[:, c, 0:NS], yn_s, yn_s[:, c * 128:(c + 1) * 128], ident_b, ident_b[0:NS, 0:NS])
        ynT_s = al([128, 16, NS], BF16, name="ynT_s")
        cp(ynT_s, ynT_s[:], psA, ptv_[:, :, 0:NS])
        mix_s = Tl("mix_s_alias", zs_s[:, 0:1024])
        mix_s.lw, mix_s.rd = zs_s.lw, zs_s.rd
        for dh in range(2):
            po = nps()
            for c in range(16):
                mm(po, po[0:NS, :], ynT_s, ynT_s[:, c, :], Wout, Wout[:, c, dh * 512:(dh + 1) * 512], start=(c == 0), stop=(c == 15))
            cp(mix_s, mix_s[:, dh * 512:(dh + 1) * 512], po, po[0:NS, :])
        post_norm_add(mix_s, mix_s[:], NS, g1, XS, XS[:], XS, XS[:], xt2[0])
        state["xsrc"] = xres
        state["xsrc_t"] = T_xres

    xt2 = [sb([128, D], F32)] * 2

    sub = 0
    for layer in range(4):
        if sub >= nsub:
            break
        if layer % 2 == 0:
            ab_layer(layer)
        else:
            ssd_layer(layer)
        sub += 1
        if sub >= nsub:
            break
        mlp(layer, last=(layer == 3))
        sub += 1
    if nsub < 8:
        for t in range(NT):
            P.add("sp", (lambda tt_: (lambda e: e.dma_start(out=O["y_p"][tt_ * 128:(tt_ + 1) * 128, :],
                                                          in_=state["xsrc"][tt_ * 128:(tt_ + 1) * 128, :])))(t),
                  r=[state["xsrc_t"].s(t)], w=[TO["y_p"]], dma=True)
        dma("sp", TO["y_s"], O["y_s"].ap(), XS, XS[:])
    P.barrier()
    cnt, dcnt = P.emit(st)
    st.close()
    return nc, len(P.ins)


def _consts(c):
    pos = c % 4
    i = np.arange(128)
    ident = np.eye(128, dtype=np.float32)
    tri = (i[:, None] <= i[None, :]).astype(np.float32)
    up = (i[:, None] > i[None, :]).astype(np.float32)
    m_own = np.where(i[None, :] <= i[:, None], 0.0, NEG).astype(np.float32)
    m_prev = np.where(i[None, :] > i[:, None], 0.0, NEG).astype(np.float32)
    mask1 = np.concatenate([m_prev, m_own], axis=1)
    mask0 = mask1.copy()
    if pos == 0:
        mask0[:, :128] = NEG
    hasprev = np.full((128, 1), 1.0 if pos > 0 else 0.0, np.float32)
    pub = np.zeros((128, 4), np.float32)
    pub[:, pos] = 1.0
    prev = max(pos - 1, 0)
    idxprev = (prev * 128 + i).astype(np.int32).reshape(128, 1)
    idxchain = np.stack([(m * 128 + i) for m in range(3)], axis=1).astype(np.int32)
    vmask = np.zeros((128, 3), np.float32)
    for m in range(3):
        if m < pos:
            vmask[:, m] = 1.0
    sel = (i[None, :] // 4 == np.arange(32)[:, None]).astype(np.float32)
    return dict(c_sel=sel, c_ident=ident, c_tri=tri, c_up=up, c_mask0=mask0, c_mask1=mask1, c_hasprev=hasprev,
                c_pub=pub, c_idxprev=idxprev, c_idxchain=idxchain, c_vmask=vmask)


def make_in_maps(inp):
    f = lambda a: np.ascontiguousarray(np.asarray(a, dtype=np.float32))
    xp = f(inp["x_prompt"])
    maps = []
    shared = dict(
        norm_g=f(inp["norm_g"]).reshape(16, D), ab_w_in=f(inp["ab_w_in"]), conf_dw_w=f(inp["conf_dw_w"]),
        conf_dw_b=f(inp["conf_dw_b"]), conf_ln_g=f(inp["conf_ln_g"]), conf_ln_b=f(inp["conf_ln_b"]),
        attn_sinks=f(inp["attn_sinks"]), ab_w_out=f(inp["ab_w_out"]), ssd_w_in=f(inp["ssd_w_in"]),
        ssd_conv_w=f(inp["ssd_conv_w"]), ssd_conv_b=f(inp["ssd_conv_b"]), ssd_dt_bias=f(inp["ssd_dt_bias"]),
        ssd_a_log=f(inp["ssd_a_log"]), ssd_d=f(inp["ssd_d"]), ssd_norm_g=f(inp["ssd_norm_g"]),
        ssd_w_out=f(inp["ssd_w_out"]), mlp_w_up=f(inp["mlp_w_up"]), mlp_w_down=f(inp["mlp_w_down"]))
    for c in range(NCORES):
        seq, pos = c // 4, c % 4
        m = dict(shared)
        m["xp"] = xp[seq, pos * TPC:(pos + 1) * TPC]
        m["xh"] = xp[seq, pos * TPC - 128:pos * TPC] if pos > 0 else np.zeros((128, D), np.float32)
        sl = slice(c * NS, (c + 1) * NS)
        m["xs"] = f(inp["x_sample"])[sl, 0]
        m["ck"] = f(inp["cache_win_k"])[:, sl].reshape(2, NS, 128, 128)
        m["cv"] = f(inp["cache_win_v"])[:, sl].reshape(2, NS, 128, 128)
        m["cconf"] = f(inp["state_conf_conv"])[:, sl]
        m["cssm"] = f(inp["state_ssm"])[:, sl].reshape(2, NS, 2048, 128)
        m["csc"] = f(inp["state_ssd_conv"])[:, sl]
        m.update(_consts(c))
        m = {k: np.ascontiguousarray(v) for k, v in m.items()}
        maps.append(m)
    return maps


_CACHE = {}


def run(inp, nsub=8):
    if nsub not in _CACHE:
        _CACHE[nsub] = build(nsub)
    nc, _ = _CACHE[nsub]
    res = run_bass_kernel_spmd(nc, make_in_maps(inp), core_ids=list(range(NCORES)))
    return res.results


def kernel(**inp):
    r = run(inp, 8)
    cat = lambda name, cores: np.concatenate([r[c][name] for c in cores], axis=0)
    y_p = np.stack([cat("y_p", range(0, 4)), cat("y_p", range(4, 8))])
    y_s = cat("y_s", range(8)).reshape(128, 1, D)
    wk_p = np.stack([r[3]["wk_p"], r[7]["wk_p"]], axis=1).reshape(2, 2, 128, 2, 64)
    wv_p = np.stack([r[3]["wv_p"], r[7]["wv_p"]], axis=1).reshape(2, 2, 128, 2, 64)
    wk_s = np.concatenate([r[c]["wk_s"] for c in range(8)], axis=1).reshape(2, 128, 1, 2, 64)
    wv_s = np.concatenate([r[c]["wv_s"] for c in range(8)], axis=1).reshape(2, 128, 1, 2, 64)
    cc_p = np.stack([r[3]["cc_p"], r[7]["cc_p"]], axis=1)
    cc_s = np.concatenate([r[c]["cc_s"] for c in range(8)], axis=1).reshape(2, 128, 1, 512)
    ssm_p = np.stack([r[3]["ssm_p"], r[7]["ssm_p"]], axis=1).reshape(2, 2, 32, 64, 128)
    ssm_s = np.concatenate([r[c]["ssm_s"] for c in range(8)], axis=1).reshape(2, 128, 32, 64, 128)
    sc_p = np.stack([r[3]["sc_p"], r[7]["sc_p"]], axis=1)
    sc_s = np.concatenate([r[c]["sc_s"] for c in range(8)], axis=1).reshape(2, 128, 1, 3072)
    outs = (y_p, y_s, wk_p, wv_p, wk_s, wv_s, cc_p, cc_s, ssm_p, ssm_s, sc_p, sc_s)
    return tuple(np.ascontiguousarray(o, dtype=np.float32) for o in outs)
```

```python
import os
import numpy as np
import concourse.bass as bass
import concourse.mybir as mybir
from concourse.bass_utils import run_bass_kernel_spmd
from contextlib import ExitStack

F32 = mybir.dt.float32
BF16 = mybir.dt.bfloat16
I32 = mybir.dt.int32
ALU = mybir.AluOpType
AF = mybir.ActivationFunctionType
AX = mybir.AxisListType
DTB = {F32: 4, BF16: 2, I32: 4}

NCORES = 8
NT = 16
TPC = NT * 128
NS = 16
D = 1024
NEG = -30000.0


class Tl:
    def __init__(self, name, ap, nslots=1):
        self.name = name
        self.ap = ap
        self.n = nslots
        self.lw = [None] * nslots
        self.rd = [[] for _ in range(nslots)]

    def __getitem__(self, k):
        return self.ap[k]

    def s(self, lo, hi=None):
        return (self, lo, lo + 1 if hi is None else hi)


def _norm(x):
    if isinstance(x, Tl):
        return (x, 0, x.n)
    return x


class Ins:
    __slots__ = ("eng", "fn", "deps", "is_dma", "sig", "tick", "sem", "idx", "inc")

    def __init__(self, eng, fn, is_dma, inc):
        self.eng = eng
        self.fn = fn
        self.is_dma = is_dma
        self.inc = inc
        self.deps = set()
        self.sig = False
        self.tick = 0
        self.sem = None


COMPUTE = ("pe", "act", "dve", "pool")


class Prog:
    def __init__(self, nc, n_dma_sems=8):
        self.nc = nc
        self.ins = []
        self.n_dma_sems = n_dma_sems
        self.last_on_eng = {}
        self.barrier_pending = {}
        self.dma_rr = {"sp": 0, "pool": 0}
        self.dma_last = {}

    def add(self, eng, fn, r=(), w=(), dma=False, cc=False):
        i = len(self.ins)
        ins = Ins(eng, fn, dma or cc, 1 if cc else 16)
        ins.idx = i
        for x in r:
            t, lo, hi = _norm(x)
            for s in range(lo, hi):
                if t.lw[s] is not None:
                    ins.deps.add(t.lw[s])
        for x in w:
            t, lo, hi = _norm(x)
            for s in range(lo, hi):
                if t.lw[s] is not None:
                    ins.deps.add(t.lw[s])
                for rr in t.rd[s]:
                    ins.deps.add(rr)
        for x in r:
            t, lo, hi = _norm(x)
            for s in range(lo, hi):
                t.rd[s].append(i)
        for x in w:
            t, lo, hi = _norm(x)
            for s in range(lo, hi):
                t.lw[s] = i
                t.rd[s] = []
        if cc:
            ins.sem = ("cc", 0)
            prev = self.dma_last.get(ins.sem)
            if prev is not None:
                ins.deps.add(prev)
            self.dma_last[ins.sem] = i
        elif dma:
            k = self.dma_rr[eng]
            self.dma_rr[eng] = (k + 1) % self.n_dma_sems
            ins.sem = (eng, k)
            prev = self.dma_last.get(ins.sem)
            if prev is not None:
                ins.deps.add(prev)
            self.dma_last[ins.sem] = i
        if eng in self.barrier_pending:
            ins.deps |= self.barrier_pending.pop(eng)
        ins.deps.discard(i)
        self.ins.append(ins)
        self.last_on_eng[eng] = i
        return i

    def barrier(self):
        pend = set(self.last_on_eng.values())
        for v in self.dma_last.values():
            pend.add(v)
        for e in ("pe", "act", "dve", "pool", "sp"):
            self.barrier_pending[e] = set(pend) | self.barrier_pending.get(e, set())

    def emit(self, stack):
        nc = self.nc
        ins = self.ins
        for x in ins:
            nd = set()
            for d in x.deps:
                p = ins[d]
                if (not p.is_dma) and (not x.is_dma) and p.eng == "pe" and x.eng == "pe":
                    continue
                nd.add(d)
            x.deps = nd
            for d in nd:
                ins[d].sig = True
        cnt = {e: 0 for e in COMPUTE}
        dcnt = {}
        for x in ins:
            if x.is_dma:
                dcnt[x.sem] = dcnt.get(x.sem, 0) + x.inc
                x.tick = dcnt[x.sem]
            elif x.sig:
                cnt[x.eng] += 1
                x.tick = cnt[x.eng]
        sems = {}
        for e in COMPUTE:
            sems[e] = stack.enter_context(nc.semaphore("s_" + e))
        for key in dcnt:
            sems[key] = stack.enter_context(nc.semaphore("d_%s%d" % key))
        per_eng = {e: [] for e in ("pe", "act", "dve", "pool", "sp")}
        for x in ins:
            per_eng[x.eng].append(x)
        final_dma = dict(dcnt)

        def run(eng_name, e):
            waited = {}
            for x in per_eng[eng_name]:
                need = {}
                for d in x.deps:
                    p = ins[d]
                    key = p.sem if p.is_dma else p.eng
                    if p.tick > need.get(key, 0):
                        need[key] = p.tick
                for key, v in need.items():
                    if waited.get(key, 0) < v:
                        e.wait_ge(sems[key], v)
                        waited[key] = v
                bi = x.fn(e)
                if x.is_dma:
                    if x.inc == 1:
                        bi.then_inc(sems[x.sem])
                    else:
                        bi.then_inc(sems[x.sem], 16)
                elif x.sig:
                    bi.then_inc(sems[x.eng], 1)
            for key, v in final_dma.items():
                owner = "pool" if key[0] == "cc" else key[0]
                if owner == eng_name and waited.get(key, 0) < v:
                    e.wait_ge(sems[key], v)

        with nc.Block() as block:
            @block.tensor
            def _(e):
                run("pe", e)

            @block.scalar
            def _(e):
                run("act", e)

            @block.vector
            def _(e):
                run("dve", e)

            @block.gpsimd
            def _(e):
                run("pool", e)

            @block.sync
            def _(e):
                run("sp", e)
        return cnt, dcnt


IN_SPECS = [
    ("xp", [TPC, D], F32), ("xh", [128, D], F32), ("xs", [NS, D], F32),
    ("ck", [2, NS, 128, 128], F32), ("cv", [2, NS, 128, 128], F32),
    ("cconf", [2, NS, 30, 512], F32), ("cssm", [2, NS, 2048, 128], F32),
    ("csc", [2, NS, 3, 3072], F32),
    ("norm_g", [16, D], F32), ("ab_w_in", [2, D, 1792], F32),
    ("conf_dw_w", [2, 31, 512], F32), ("conf_dw_b", [2, 512], F32),
    ("conf_ln_g", [2, 512], F32), ("conf_ln_b", [2, 512], F32),
    ("attn_sinks", [2, 8], F32), ("ab_w_out", [2, D, D], F32),
    ("ssd_w_in", [2, D, 5152], F32), ("ssd_conv_w", [2, 4, 3072], F32),
    ("ssd_conv_b", [2, 3072], F32), ("ssd_dt_bias", [2, 32], F32),
    ("ssd_a_log", [2, 32], F32), ("ssd_d", [2, 32], F32),
    ("ssd_norm_g", [2, 2048], F32), ("ssd_w_out", [2, 2048, D], F32),
    ("mlp_w_up", [4, D, 4096], F32), ("mlp_w_down", [4, 4096, D], F32),
    ("c_ident", [128, 128], F32), ("c_tri", [128, 128], F32), ("c_up", [128, 128], F32),
    ("c_mask0", [128, 256], F32), ("c_mask1", [128, 256], F32),
    ("c_hasprev", [128, 1], F32), ("c_pub", [128, 4], F32),
    ("c_sel", [32, 128], F32), ("c_idxprev", [128, 1], I32), ("c_idxchain", [128, 3], I32), ("c_vmask", [128, 3], F32),
]
OUT_SPECS = [
    ("y_p", [TPC, D]), ("y_s", [NS, D]),
    ("wk_p", [2, 128, 128]), ("wv_p", [2, 128, 128]),
    ("wk_s", [2, NS, 128]), ("wv_s", [2, NS, 128]),
    ("cc_p", [2, 30, 512]), ("cc_s", [2, NS, 512]),
    ("ssm_p", [2, 2048, 128]), ("ssm_s", [2, NS, 2048, 128]),
    ("sc_p", [2, 3, 3072]), ("sc_s", [2, NS, 3072]),
]


def build(nsub=8):
    nc = bass.Bass("TRN2", target_bir_lowering=False)
    st = ExitStack()
    P = Prog(nc)
    I = {}
    for name, shape, dt in IN_SPECS:
        I[name] = nc.dram_tensor(name, shape, dt, kind="ExternalInput")
    O = {}
    for name, shape in OUT_SPECS:
        O[name] = nc.dram_tensor(name, shape, F32, kind="ExternalOutput")
    TI = {k: Tl(k, v) for k, v in I.items()}
    TO = {k: Tl(k, v) for k, v in O.items()}
    xres = nc.dram_tensor("xres", [TPC, D], F32)
    T_xres = Tl("xres", xres, NT)
    EX = {}
    for nm_, w_ in (("h", D), ("s", 2048), ("l", 32)):
        a_ = nc.dram_tensor("ex_in_" + nm_, [4 * 128, w_], F32)
        b_ = nc.dram_tensor("ex_out_" + nm_, [4 * 128, w_], F32)
        EX[nm_] = (a_, b_, Tl("exi" + nm_, a_), Tl("exo" + nm_, b_))
    scr = nc.dram_tensor("scr", [NS, 128], F32)
    T_scr = Tl("scr", scr)
    SCR = {}
    for nm_, w_ in (("xdt", 2048), ("B", 512), ("C", 512), ("y", 2048)):
        a_ = nc.dram_tensor("scr_" + nm_, [NS, w_], F32)
        SCR[nm_] = (a_, Tl("scr_" + nm_, a_))

    uid = [0]

    def sb(shape, dt, n=1, name=None):
        uid[0] += 1
        nm = (name or "t") + str(uid[0])
        return Tl(nm, st.enter_context(nc.sbuf_tensor(nm, shape, dt)), n)

    ARENA_COLS = 90 * 1024
    arena = st.enter_context(nc.sbuf_tensor("arena", [128, ARENA_COLS], BF16))
    apos = [0]

    def areset():
        P.barrier()
        apos[0] = 0

    def al(shape, dt, n=1, name="a"):
        cols = int(np.prod(shape[1:])) * DTB[dt] // 2
        cols = (cols + 15) // 16 * 16
        off = apos[0]
        apos[0] += cols
        assert apos[0] <= ARENA_COLS, ("arena overflow", name, apos[0])
        ap = arena[0:shape[0], off:off + cols]
        if dt != BF16:
            ap = ap.bitcast(dt)
        tot = int(np.prod(shape[1:]))
        ap = ap[:, 0:tot]
        if len(shape) == 3:
            ap = ap.rearrange("p (a b) -> p a b", a=shape[1])
        elif len(shape) == 4:
            ap = ap.rearrange("p (a b c) -> p a b c", a=shape[1], b=shape[2])
        uid[0] += 1
        return Tl(name + str(uid[0]), ap, n)

    psA = Tl("psA", st.enter_context(nc.psum_tensor("psA", [128, 1024], F32)))
    psB = Tl("psB", st.enter_context(nc.psum_tensor("psB", [128, 1024], F32)))
    ps1 = [Tl("ps%d" % i, st.enter_context(nc.psum_tensor("ps%d" % i, [128, 512], F32))) for i in range(4)]
    psB_halves = [Tl("psB0", psB[:, 0:512]), Tl("psB1", psB[:, 512:1024])]
    rr = {"ps": 0}

    def nps():
        rr["ps"] = (rr["ps"] + 1) % 4
        return ps1[rr["ps"]]

    def mm(ot, oap, lt, lap, rt, rap, start=True, stop=True):
        P.add("pe", lambda e: e.matmul(oap, lap, rap, start=start, stop=stop), r=[lt, rt], w=[ot])

    def tr(ot, oap, it, iap, identt, idap):
        P.add("pe", lambda e: e.transpose(oap, iap, idap), r=[it, identt], w=[ot])

    def act(ot, oap, it, iap, func, bias=None, scale=None, accum=None, extra_r=(), extra_w=()):
        kw = {}
        if bias is not None:
            kw["bias"] = bias
        if scale is not None:
            kw["scale"] = scale
        if accum is not None:
            kw["accum_out"] = accum
        P.add("act", lambda e: e.activation(out=oap, in_=iap, func=func, **kw),
              r=[it] + list(extra_r), w=[ot] + list(extra_w))

    def tt(ot, oap, at, aap, bt, bap, op, eng="dve"):
        P.add(eng, lambda e: e.tensor_tensor(out=oap, in0=aap, in1=bap, op=op), r=[at, bt], w=[ot])

    def ts(ot, oap, at, aap, s1, s2, op0, op1=None, extra_r=(), eng="dve", accum=None):
        if op1 is None:
            P.add(eng, lambda e: e.tensor_scalar(out=oap, in0=aap, scalar1=s1, scalar2=None, op0=op0),
                  r=[at] + list(extra_r), w=[ot])
        else:
            P.add(eng, lambda e: e.tensor_scalar(out=oap, in0=aap, scalar1=s1, scalar2=s2, op0=op0, op1=op1),
                  r=[at] + list(extra_r), w=[ot])

    def stt(ot, oap, at, aap, scalar, bt, bap, op0, op1, extra_r=()):
        P.add("dve", lambda e: e.scalar_tensor_tensor(out=oap, in0=aap, scalar=scalar, in1=bap, op0=op0, op1=op1),
              r=[at, bt] + list(extra_r), w=[ot])

    def cp(ot, oap, it, iap, eng="dve"):
        if eng == "act":
            P.add("act", lambda e: e.copy(out=oap, in_=iap), r=[it], w=[ot])
        else:
            P.add(eng, lambda e: e.tensor_copy(out=oap, in_=iap), r=[it], w=[ot])

    def recip(ot, oap, it, iap):
        P.add("dve", lambda e: e.reciprocal(out=oap, in_=iap), r=[it], w=[ot])

    def dma(q, ot, oap, it, iap):
        P.add(q, lambda e: e.dma_start(out=oap, in_=iap), r=[it], w=[ot], dma=True)

    def memset(t, ap, v, eng="pool"):
        P.add(eng, lambda e: e.memset(ap, v), w=[t])

    ident_f = sb([128, 128], F32)
    ident_b = sb([128, 128], BF16)
    tri_f = sb([128, 128], F32)
    tri_b = sb([128, 128], BF16)
    up_b = sb([128, 128], BF16)
    trimask_b = sb([128, 128], BF16)
    ones_b = sb([128, 128], BF16)
    ones_f = sb([128, 128], F32)
    mask0 = sb([128, 256], F32)
    mask1 = sb([128, 256], F32)
    hasprev = sb([128, 1], F32)
    pub = sb([128, 4], F32)
    idxprev = sb([128, 1], I32)
    idxchain = sb([128, 3], I32)
    vmask = sb([128, 3], F32)
    XS = sb([NS, D], F32)
    gbc = [sb([128, D], F32), sb([128, D], F32)]
    junk = sb([128, 1024], BF16)
    ss_t = sb([128, 8], F32)
    sd_t = sb([128, 8], F32)
    xhalo = sb([128, D], F32)
    for t, nm in ((ident_f, "c_ident"), (tri_f, "c_tri"), (mask0, "c_mask0"), (mask1, "c_mask1"),
                  (hasprev, "c_hasprev"), (pub, "c_pub"), (idxprev, "c_idxprev"),
                  (idxchain, "c_idxchain"), (vmask, "c_vmask")):
        dma("sp", t, t[:], TI[nm], I[nm].ap())
    tmpc = sb([128, 128], F32)
    dma("sp", tmpc, tmpc[:], TI["c_up"], I["c_up"].ap())
    cp(ident_b, ident_b[:], ident_f, ident_f[:])
    cp(tri_b, tri_b[:], tri_f, tri_f[:])
    cp(trimask_b, trimask_b[:], tri_f, tri_f[:])
    cp(up_b, up_b[:], tmpc, tmpc[:])
    memset(ones_b, ones_b[:], 1.0)
    memset(ones_f, ones_f[:], 1.0)
    dma("sp", XS, XS[:], TI["xs"], I["xs"].ap())
    dma("sp", xhalo, xhalo[:], TI["xh"], I["xh"].ap())

    gsel = [0]

    def load_gamma(row):
        gsel[0] ^= 1
        g = gbc[gsel[0]]
        dma("sp", g, g[:], TI["norm_g"], I["norm_g"][row:row + 1, :].partition_broadcast(128))
        return g

    EPS = 1e-6

    def rstd_of(src_t, src_ap, M, col, ncols=D, eps=EPS):
        act(ss_t, junk[0:M, 0:ncols], src_t, src_ap, AF.Square, accum=ss_t[0:M, col:col + 1])
        act(sd_t, sd_t[0:M, col:col + 1], ss_t, ss_t[0:M, col:col + 1], AF.Sqrt, scale=1.0 / ncols, bias=eps)
        recip(sd_t, sd_t[0:M, col:col + 1], sd_t, sd_t[0:M, col:col + 1])
        return sd_t[0:M, col:col + 1]

    def norm_T(src_t, src_ap, M, g, hT, hT_slot, tok0, xn_t):
        rs = rstd_of(src_t, src_ap, M, 0)
        stt(xn_t, xn_t[0:M, :], src_t, src_ap, rs, g, g[0:M, :], ALU.mult, ALU.mult, extra_r=[sd_t])
        pt = psA
        ptv = pt[:, 0:512].bitcast(BF16).rearrange("p (k m) -> p k m", k=8)
        for kc in range(8):
            tr(pt, ptv[:, kc, 0:M], xn_t, xn_t[0:M, kc * 128:(kc + 1) * 128], ident_b, ident_b[0:M, 0:M])
        cp(hT_slot, hT[:, :, tok0:tok0 + M], pt, ptv[:, :, 0:M], eng="act")

    def post_norm_add(f_t, f_ap, M, g, x_t, x_ap, out_t, out_ap, tmp_t):
        rs = rstd_of(f_t, f_ap, M, 1)
        stt(tmp_t, tmp_t[0:M, :], f_t, f_ap, rs, g, g[0:M, :], ALU.mult, ALU.mult, extra_r=[sd_t])
        tt(out_t, out_ap, tmp_t, tmp_t[0:M, :], x_t, x_ap, ALU.add)

    state = {"xsrc": I["xp"], "xsrc_t": Tl("xp_rows", I["xp"], NT)}

    def xrows(t):
        return state["xsrc"][t * 128:(t + 1) * 128, :]

    def exchange(kind, pieces, tmp):
        ex_in, ex_out, T_exin, T_exout = EX[kind]
        for (t, ap, c0, w) in pieces:
            for r in range(4):
                ts(tmp, tmp[:, 0:w], t, ap, pub[:, r:r + 1], None, ALU.mult, extra_r=[pub])
                dma("sp", T_exin, ex_in[r * 128:(r + 1) * 128, c0:c0 + w], tmp, tmp[:, 0:w])
        P.add("pool", lambda e: e.collective_compute(
            "AllReduce", ALU.add, replica_groups=[[0, 1, 2, 3], [4, 5, 6, 7]],
            ins=[ex_in.ap().opt()], outs=[ex_out.ap().opt()]), r=[T_exin], w=[T_exout], cc=True)

    def gather_rows(kind, dst_t, dst_ap, idx_t, idx_ap, c0, w):
        ex_in, ex_out, T_exin, T_exout = EX[kind]
        P.add("pool", lambda e: e.indirect_dma_start(
            out=dst_ap, out_offset=None, in_=ex_out[:, :],
            in_offset=bass.IndirectOffsetOnAxis(ap=idx_ap, axis=0)),
            r=[T_exout, idx_t], w=[dst_t], dma=True)

    def mlp(layer, last):
        areset()
        g2 = load_gamma(layer * 4 + 2)
        g3 = load_gamma(layer * 4 + 3)
        hT = al([128, 8, TPC + NS], BF16, n=NT + 1, name="hT")
        Fa = al([128, NT, D], F32, n=NT, name="F")
        Fs = al([NS, D], F32, name="Fs")
        xt = [al([128, D], F32, name="xt") for _ in range(2)]
        xn = [al([128, D], BF16, name="xn") for _ in range(2)]
        wup = [al([128, 8, 512], BF16, name="wup") for _ in range(2)]
        wdn = [al([128, 4, D], BF16, name="wdn") for _ in range(2)]
        aT = [al([128, 4, 512], BF16, name="aT") for _ in range(2)]
        rl = [al([128, 512], BF16, name="rl") for _ in range(2)]
        for t in range(NT):
            x = xt[t % 2]
            dma("sp", x, x[:], state["xsrc_t"].s(t), xrows(t))
            norm_T(x, x[:], 128, g2, hT, hT.s(t), t * 128, xn[t % 2])
        norm_T(XS, XS[:], NS, g2, hT, hT.s(NT), TPC, xn[0])
        groups = [(tg * 512, 512, list(range(tg * 4, tg * 4 + 4))) for tg in range(4)] + [(TPC, NS, [NT])]
        STOP = int(os.environ.get('MLP_STOP', '9'))
        if STOP <= 1:
            return
        k = 0
        for fb in range(8):
            wu, wd = wup[fb % 2], wdn[fb % 2]
            dma("pool", wu, wu[:], TI["mlp_w_up"],
                I["mlp_w_up"][layer, :, fb * 512:(fb + 1) * 512].rearrange("(k p) c -> p k c", p=128))
            dma("pool", wd, wd[:], TI["mlp_w_down"],
                I["mlp_w_down"][layer, fb * 512:(fb + 1) * 512, :].rearrange("(k p) c -> p k c", p=128))
            if STOP <= 2 and fb >= 1:
                break
            for (tok0, ntok, slots) in groups:
                a = aT[k % 2]
                k += 1
                for fc in range(4):
                    pu = nps()
                    for kc in range(8):
                        mm(pu, pu[:, 0:ntok], wu, wu[:, kc, fc * 128:(fc + 1) * 128],
                           (hT, slots[0], slots[-1] + 1), hT[:, kc, tok0:tok0 + ntok], start=(kc == 0), stop=(kc == 7))
                    r_ = rl[fc % 2]
                    act(r_, r_[:, 0:ntok], pu, pu[:, 0:ntok], AF.Relu)
                    tt(a, a[:, fc, 0:ntok], r_, r_[:, 0:ntok], r_, r_[:, 0:ntok], ALU.mult)
                if STOP <= 3:
                    continue
                if ntok == NS:
                    for dh in range(2):
                        pd = nps()
                        for fc in range(4):
                            mm(pd, pd[0:NS, :], a, a[:, fc, 0:NS], wd, wd[:, fc, dh * 512:(dh + 1) * 512],
                               start=(fc == 0), stop=(fc == 3))
                        if fb == 0:
                            cp(Fs, Fs[:, dh * 512:(dh + 1) * 512], pd, pd[0:NS, :])
                        else:
                            tt(Fs, Fs[:, dh * 512:(dh + 1) * 512], pd, pd[0:NS, :], Fs, Fs[:, dh * 512:(dh + 1) * 512], ALU.add)
                else:
                    for ti, tslot in enumerate(slots):
                        for dh in range(2):
                            pd = nps()
                            for fc in range(4):
                                mm(pd, pd[:, :], a, a[:, fc, ti * 128:(ti + 1) * 128], wd, wd[:, fc, dh * 512:(dh + 1) * 512],
                                   start=(fc == 0), stop=(fc == 3))
                            fs = Fa.s(tslot)
                            if fb == 0:
                                cp(fs, Fa[:, tslot, dh * 512:(dh + 1) * 512], pd, pd[:, :], eng="act")
                            else:
                                tt(fs, Fa[:, tslot, dh * 512:(dh + 1) * 512], pd, pd[:, :], fs,
                                   Fa[:, tslot, dh * 512:(dh + 1) * 512], ALU.add)
        if STOP <= 4:
            return
        dst = O["y_p"] if last else xres
        dst_t = TO["y_p"] if last else T_xres
        for t in range(NT):
            x = xt[t % 2]
            tm = xn[t % 2]
            dma("sp", x, x[:], state["xsrc_t"].s(t), xrows(t))
            tmpf = xt2[t % 2]
            post_norm_add(Fa.s(t), Fa[:, t, :], 128, g3, x, x[:], x, x[:], tmpf)
            if t == NT - 1 and not last:
                extmp = al([128, D], F32, name="extmp")
                exchange("h", [(x, x[:], 0, D)], extmp)
                gather_rows("h", xhalo, xhalo[:], idxprev, idxprev[:, 0:1], 0, D)
            dma("sp", (dst_t, t, t + 1) if dst_t.n == NT else dst_t, dst[t * 128:(t + 1) * 128, :], x, x[:])
        tmpf = xt2[0]
        post_norm_add(Fs, Fs[:, :], NS, g3, XS, XS[:], XS, XS[:], tmpf)
        if last:
            dma("sp", TO["y_s"], O["y_s"].ap(), XS, XS[:])
        if not last:
            state["xsrc"] = xres
            state["xsrc_t"] = T_xres


    def load_cols(rows, R, C, dst_t, dst_ap_fn, tmp_rows):
        for r_, (t_, ap_) in enumerate(rows):
            dma("sp", tmp_rows, tmp_rows[r_:r_ + 1, 0:C], t_, ap_)
        for cc in range(C // 128):
            pt = nps()
            tr(pt, pt[:, 0:R], tmp_rows, tmp_rows[0:R, cc * 128:(cc + 1) * 128], ident_f, ident_f[0:R, 0:R])
            cp(dst_t, dst_ap_fn(cc), pt, pt[:, 0:R])

    def attn_tiles(streams, sinkbc):
        banks6 = ps1 + psB_halves
        for h in range(8):
            for si, (M, nown, qf, kf, mask_ap, vprev, vown, W, out_t, out_fn) in enumerate(streams):
                nk = 128 + nown
                bk = banks6[3 * si:3 * si + 3]
                sm, p_b, pT, o_b, sc = W["sm"], W["p_b"], W["pT"], W["o_b"], W["sc"]
                i_, two = h % 4, h // 4
                qt, qa = qf(i_, two)
                kt, ka = kf(two)
                s_ps = bk[0]
                mm(s_ps, s_ps[0:M, 0:nk], qt, qa, kt, ka)
                stt(sm, sm[0:M, 0:nk], s_ps, s_ps[0:M, 0:nk], 0.125, mask0, mask_ap, ALU.mult, ALU.add, extra_r=[mask1])
                P.add("dve", (lambda o_, i2: (lambda e: e.reduce_max(out=o_, in_=i2, axis=AX.X)))(sc[0:M, h:h + 1], sm[0:M, 0:nk]),
                      r=[sm], w=[sc])
                ts(sc, sc[0:M, 8 + h:9 + h], sc, sc[0:M, h:h + 1], sinkbc[0:M, h:h + 1], -1.0, ALU.max, ALU.mult, extra_r=[sinkbc])
                act(p_b, p_b[0:M, 0:nk], sm, sm[0:M, 0:nk], AF.Exp, bias=sc[0:M, 8 + h:9 + h],
                    accum=sc[0:M, 16 + h:17 + h], extra_r=[sc], extra_w=[sc])
                act(sc, sc[0:M, 24 + h:25 + h], sinkbc, sinkbc[0:M, h:h + 1], AF.Exp, bias=sc[0:M, 8 + h:9 + h], extra_r=[sc])
                tt(sc, sc[0:M, 32 + h:33 + h], sc, sc[0:M, 16 + h:17 + h], sc, sc[0:M, 24 + h:25 + h], ALU.add)
                recip(sc, sc[0:M, 32 + h:33 + h], sc, sc[0:M, 32 + h:33 + h])
                pT_ps = bk[1]
                pv = pT_ps[:, 0:128].bitcast(BF16).rearrange("p (a b) -> p a b", a=2)
                tr(pT_ps, pv[:, 0, 0:M], p_b, p_b[0:M, 0:128], ident_b, ident_b[0:M, 0:M])
                tr(pT_ps, pv[0:nown, 1, 0:M], p_b, p_b[0:M, 128:128 + nown], ident_b, ident_b[0:M, 0:M])
                cp(pT, pT[:, 0, 0:M], pT_ps, pv[:, 0, 0:M], eng="act")
                cp(pT, pT[0:nown, 1, 0:M], pT_ps, pv[0:nown, 1, 0:M], eng="act")
                o_ps = bk[2]
                mm(o_ps, o_ps[0:M, 0:64], pT, pT[:, 0, 0:M], vprev[0], vprev[1][:, two * 64:(two + 1) * 64], start=True, stop=False)
                mm(o_ps, o_ps[0:M, 0:64], pT, pT[0:nown, 1, 0:M], vown[0], vown[1][0:nown, two * 64:(two + 1) * 64], start=False, stop=True)
                ts(o_b, o_b[0:M, h * 64:(h + 1) * 64], o_ps, o_ps[0:M, 0:64], sc[0:M, 32 + h:33 + h], None, ALU.mult, extra_r=[sc])
        for (M, nown, qf, kf, mask_ap, vprev, vown, W, out_t, out_fn) in streams:
            o_b = W["o_b"]
            ot_ps = nps()
            ov = ot_ps[:, 0:256].bitcast(BF16).rearrange("p (a b) -> p a b", a=4)
            for c4 in range(4):
                tr(ot_ps, ov[:, c4, 0:M], o_b, o_b[0:M, c4 * 128:(c4 + 1) * 128], ident_b, ident_b[0:M, 0:M])
            cp(out_t, out_fn(), ot_ps, ov[:, :, 0:M])

    def ab_layer(layer):
        i = layer // 2
        areset()
        g0 = load_gamma(layer * 4 + 0)
        g1 = load_gamma(layer * 4 + 1)
        Win = al([128, 8, 1792], BF16, name="Win")
        Wout = al([128, 8, D], BF16, name="Wout")
        diag = al([128, 4, 31, 128], BF16, name="diag")
        wT = al([128, 4, 31], F32, name="wT")
        pcol = al([128, 4, 3], F32, name="pcol")
        rowtmp = al([32, 512], F32, name="rowtmp")
        sinkbc = al([128, 8], F32, name="sink")
        wsrc = I["ab_w_in"]
        dma("pool", Win, Win[:, :, 0:1024], TI["ab_w_in"], wsrc[i, :, 0:1024].rearrange("(k p) c -> p k c", p=128))
        for two in range(2):
            for kc in range(8):
                dma("pool", Win, Win[:, kc, 1024:1536].rearrange("p (i t d) -> p t i d", t=2, d=64)[:, two],
                    TI["ab_w_in"], wsrc[i, kc * 128:(kc + 1) * 128, 1024 + two * 256:1024 + (two + 1) * 256]
                    .rearrange("p (i d) -> p i d", d=64))
        dma("pool", Win, Win[:, :, 1536:1792], TI["ab_w_in"], wsrc[i, :, 1536:1792].rearrange("(k p) c -> p k c", p=128))
        dma("pool", Wout, Wout[:], TI["ab_w_out"], I["ab_w_out"][i].rearrange("(k p) c -> p k c", p=128))
        dma("sp", sinkbc, sinkbc[:], TI["attn_sinks"], I["attn_sinks"][i:i + 1, :].partition_broadcast(128))
        load_cols([(TI["conf_dw_w"], I["conf_dw_w"][i, j:j + 1, :]) for j in range(31)], 31, 512, wT,
                  lambda cc: wT[:, cc, :], rowtmp)
        load_cols([(TI["conf_dw_b"], I["conf_dw_b"][i:i + 1, :]), (TI["conf_ln_g"], I["conf_ln_g"][i:i + 1, :]),
                   (TI["conf_ln_b"], I["conf_ln_b"][i:i + 1, :])], 3, 512, pcol, lambda cc: pcol[:, cc, :], rowtmp)
        for cc in range(4):
            for j in range(31):
                ts(diag, diag[:, cc, j, :], ident_b, ident_b[:], wT[:, cc, j:j + 1], None, ALU.mult, extra_r=[wT])

        amark = apos[0]
        hT = [al([128, 8, 512], BF16, n=4, name="hT")] * 2
        hTh = al([128, 8, 128], BF16, name="hTh")
        kT_all = al([128, 17 * 128], BF16, n=17, name="kT")
        v_all = al([128, 17, 128], BF16, n=17, name="v")
        qT = al([128, 4, 512], BF16, name="qT")
        aTe = [al([128, 4, 544], BF16, name="aTe") for _ in range(2)]
        a32l = al([128, 4, 32], F32, name="a32l")
        sg = [al([128, 512], F32, name="sg") for _ in range(2)]
        c32 = al([128, 4, 512], F32, name="c32")
        cb16 = al([128, 4, 512], BF16, name="cb16")
        csq = al([128, 4, 512], BF16, name="csq")
        mean = al([128, 512], F32, name="mean")
        var = al([128, 512], F32, name="var")
        t1 = [al([128, 512], F32, name="t1") for _ in range(2)]
        catT = [al([128, 8, 512], BF16, name="catT")] * 2
        xtg = [al([128, D], F32, name="xtg") for _ in range(4)]
        xn = [al([128, D], BF16, name="xn")] * 2
        kv32 = [al([128, 256], F32, name="kv32")] * 2
        mix32 = [al([128, D], F32, name="mix")] * 2
        AW = [dict(sm=al([128, 256], F32, name="sm"), p_b=al([128, 256], BF16, name="pb"),
                   pT=al([128, 2, 128], BF16, name="pT"), o_b=al([128, 512], BF16, name="ob"),
                   sc=al([128, 40], F32, name="sc")) for _ in range(2)]

        def glu(hT_t, hT_ap_fn, n, cc, sg_t):
            p1 = nps()
            for kc in range(8):
                mm(p1, p1[:, 0:n], Win, Win[:, kc, cc * 128:(cc + 1) * 128], hT_t, hT_ap_fn(kc), start=(kc == 0), stop=(kc == 7))
            p2 = nps()
            for kc in range(8):
                mm(p2, p2[:, 0:n], Win, Win[:, kc, 512 + cc * 128:512 + (cc + 1) * 128], hT_t, hT_ap_fn(kc),
                   start=(kc == 0), stop=(kc == 7))
            act(sg_t, sg_t[:, 0:n], p2, p2[:, 0:n], AF.Sigmoid)
            return p1

        def conf_tail(n, rhs_fn, rhs_t, cat_t, cat_fn):
            for cc in range(4):
                pc = nps()
                for j in range(31):
                    mm(pc, pc[:, 0:n], diag, diag[:, cc, j, :], rhs_t, rhs_fn(cc, j), start=(j == 0), stop=(j == 30))
                act(c32, c32[:, cc, 0:n], pc, pc[:, 0:n], AF.Identity, bias=pcol[:, cc, 0:1], extra_r=[pcol])
                act(csq, csq[:, cc, 0:n], pc, pc[:, 0:n], AF.Square, bias=pcol[:, cc, 0:1], extra_r=[pcol])
                cp(cb16, cb16[:, cc, 0:n], c32, c32[:, cc, 0:n])
            st1 = nps()
            for cc in range(4):
                mm(st1, st1[:, 0:n], ones_b, ones_b[:], cb16, cb16[:, cc, 0:n], start=(cc == 0), stop=(cc == 3))
            st2 = nps()
            for cc in range(4):
                mm(st2, st2[:, 0:n], ones_b, ones_b[:], csq, csq[:, cc, 0:n], start=(cc == 0), stop=(cc == 3))
            ts(mean, mean[:, 0:n], st1, st1[:, 0:n], 1.0 / 512, None, ALU.mult)
            tt(var, var[:, 0:n], mean, mean[:, 0:n], mean, mean[:, 0:n], ALU.mult)
            stt(var, var[:, 0:n], st2, st2[:, 0:n], 1.0 / 512, var, var[:, 0:n], ALU.mult, ALU.subtract)
            act(var, var[:, 0:n], var, var[:, 0:n], AF.Sqrt, bias=1e-5)
            recip(var, var[:, 0:n], var, var[:, 0:n])
            for cc in range(4):
                t_ = t1[cc % 2]
                tt(t_, t_[:, 0:n], c32, c32[:, cc, 0:n], mean, mean[:, 0:n], ALU.subtract)
                tt(t_, t_[:, 0:n], t_, t_[:, 0:n], var, var[:, 0:n], ALU.mult)
                act(cat_t, cat_fn(cc), t_, t_[:, 0:n], AF.Silu, bias=pcol[:, cc, 2:3], scale=pcol[:, cc, 1:2], extra_r=[pcol])

        norm_T(xhalo, xhalo[:], 128, g0, hTh, hTh, 0, xn[0])
        for cc in range(4):
            p1 = glu(hTh, lambda kc: hTh[:, kc, :], 128, cc, sg[0])
            tt(t1[0], t1[0][:, 0:128], p1, p1[:, 0:128], sg[0], sg[0][:, 0:128], ALU.mult)
            ts(aTe[0], aTe[0][:, cc, 0:30], t1[0], t1[0][:, 98:128], hasprev[:, 0:1], None, ALU.mult, extra_r=[hasprev])
        pk = nps()
        for kc in range(8):
            mm(pk, pk[:, 0:128], Win, Win[:, kc, 1536:1664], hTh, hTh[:, kc, :], start=(kc == 0), stop=(kc == 7))
        cp(kT_all.s(0), kT_all[:, 0:128], pk, pk[:, 0:128])
        pv_ = nps()
        for kc in range(8):
            mm(pv_, pv_[:, 0:256], hTh, hTh[:, kc, :], Win, Win[:, kc, 1536:1792], start=(kc == 0), stop=(kc == 7))
        cp(v_all.s(0), v_all[:, 0, :], pv_, pv_[:, 128:256])

        for tg in range(4):
            h_ = hT[tg % 2]
            ae = aTe[tg % 2]
            ct = catT[tg % 2]
            for lt in range(4):
                t = tg * 4 + lt
                dma("sp", xtg[lt], xtg[lt][:], state["xsrc_t"].s(t), xrows(t))
                norm_T(xtg[lt], xtg[lt][:], 128, g0, h_, h_.s(lt), lt * 128, xn[lt % 2])
            for cc in range(4):
                s_ = sg[cc % 2]
                p1 = glu(h_, lambda kc: h_[:, kc, :], 512, cc, s_)
                tt(ae, ae[:, cc, 30:542], p1, p1[:, :], s_, s_[:, :], ALU.mult)
                if tg == 3:
                    tt(a32l, a32l[:, cc, :], p1, p1[:, 480:512], s_, s_[:, 480:512], ALU.mult)
            if tg > 0:
                cp(ae, ae[:, :, 0:30], aTe[(tg - 1) % 2], aTe[(tg - 1) % 2][:, :, 512:542])
            conf_tail(512, lambda cc, j: ae[:, cc, j:j + 512], ae, ct, lambda cc: ct[:, cc, :])
            for qi in range(4):
                pq = nps()
                for kc in range(8):
                    mm(pq, pq[:, :], Win, Win[:, kc, 1024 + qi * 128:1024 + (qi + 1) * 128], h_, h_[:, kc, :],
                       start=(kc == 0), stop=(kc == 7))
                cp(qT, qT[:, qi, :], pq, pq[:, :], eng="act")
            pk = nps()
            for kc in range(8):
                mm(pk, pk[:, :], Win, Win[:, kc, 1536:1664], h_, h_[:, kc, :], start=(kc == 0), stop=(kc == 7))
            cp(kT_all.s(tg * 4 + 1, tg * 4 + 5), kT_all[:, (tg * 4 + 1) * 128:(tg * 4 + 5) * 128], pk, pk[:, :])
            for lt in range(4):
                t = tg * 4 + lt
                kv = kv32[lt % 2]
                pv_ = nps()
                for kc in range(8):
                    mm(pv_, pv_[:, 0:256], h_.s(lt), h_[:, kc, lt * 128:(lt + 1) * 128], Win, Win[:, kc, 1536:1792],
                       start=(kc == 0), stop=(kc == 7))
                cp(kv, kv[:], pv_, pv_[:, 0:256], eng="act")
                cp(v_all.s(t + 1), v_all[:, t + 1, :], kv, kv[:, 128:256])
                if t == NT - 1:
                    dma("sp", TO["wk_p"], O["wk_p"][i], kv, kv[:, 0:128])
                    dma("sp", TO["wv_p"], O["wv_p"][i], kv, kv[:, 128:256])
            for lp in range(2):
                streams = []
                for lt in (2 * lp, 2 * lp + 1):
                    t = tg * 4 + lt
                    mk = mask0 if t == 0 else mask1
                    streams.append((128, 128,
                                    (lambda lt_: (lambda i_, two: (qT, qT[64 * two:64 * two + 64, i_, lt_ * 128:(lt_ + 1) * 128])))(lt),
                                    (lambda t_: (lambda two: (kT_all.s(t_, t_ + 2), kT_all[64 * two:64 * two + 64, t_ * 128:(t_ + 2) * 128])))(t),
                                    mk[:, :], (v_all.s(t), v_all[:, t, :]), (v_all.s(t + 1), v_all[:, t + 1, :]),
                                    AW[lt % 2], ct, (lambda lt_: (lambda: ct[:, 4:8, lt_ * 128:(lt_ + 1) * 128]))(lt)))
                attn_tiles(streams, sinkbc)
            for lt in range(4):
                t = tg * 4 + lt
                mx_ = mix32[lt % 2]
                for dh in range(2):
                    po = nps()
                    for c8 in range(8):
                        mm(po, po[:, :], ct, ct[:, c8, lt * 128:(lt + 1) * 128], Wout, Wout[:, c8, dh * 512:(dh + 1) * 512],
                           start=(c8 == 0), stop=(c8 == 7))
                    cp(mx_, mx_[:, dh * 512:(dh + 1) * 512], po, po[:, :], eng="act")
                post_norm_add(mx_, mx_[:], 128, g1, xtg[lt], xtg[lt][:], xtg[lt], xtg[lt][:], xt2[lt % 2])
                dma("sp", T_xres.s(t), xres[t * 128:(t + 1) * 128, :], xtg[lt], xtg[lt][:])
        pa = nps()
        for cc in range(4):
            tr(pa, pa[0:32, cc * 128:(cc + 1) * 128], a32l, a32l[:, cc, :], ident_f, ident_f[:])
        cp(mean, mean[0:32, :], pa, pa[0:32, :])
        dma("sp", TO["cc_p"], O["cc_p"][i], mean, mean[2:32, :])

        P.barrier()
        apos[0] = amark
        sg = [al([128, 512], F32, name="sg")]
        c32 = al([128, 4, 512], F32, name="c32")
        cb16 = al([128, 4, 512], BF16, name="cb16")
        csq = al([128, 4, 512], BF16, name="csq")
        mean = al([128, 512], F32, name="mean")
        var = al([128, 512], F32, name="var")
        t1 = [al([128, 512], F32, name="t1") for _ in range(2)]
        xn = [al([128, D], BF16, name="xn")]
        mix32 = [al([128, D], F32, name="mix")]
        AW = [dict(sm=al([128, 256], F32, name="sm"), p_b=al([128, 256], BF16, name="pb"),
                   pT=al([128, 2, 128], BF16, name="pT"), o_b=al([128, 512], BF16, name="ob"),
                   sc=al([128, 40], F32, name="sc")) for _ in range(2)]
        hTs = al([128, 8, NS], BF16, name="hTs")
        norm_T(XS, XS[:], NS, g0, hTs, hTs, 0, xn[0])
        a_tok = al([NS, 512], F32, name="a_tok")
        sgs = al([NS, 512], F32, name="sgs")
        pu1 = nps()
        for kc in range(8):
            mm(pu1, pu1[0:NS, :], hTs, hTs[:, kc, :], Win, Win[:, kc, 0:512], start=(kc == 0), stop=(kc == 7))
        pu2 = nps()
        for kc in range(8):
            mm(pu2, pu2[0:NS, :], hTs, hTs[:, kc, :], Win, Win[:, kc, 512:1024], start=(kc == 0), stop=(kc == 7))
        act(sgs, sgs[:], pu2, pu2[0:NS, :], AF.Sigmoid)
        tt(a_tok, a_tok[:], pu1, pu1[0:NS, :], sgs, sgs[:], ALU.mult)
        dma("sp", TO["cc_s"], O["cc_s"][i], a_tok, a_tok[:])
        kvs = al([NS, 256], F32, name="kvs")
        pkv = nps()
        for kc in range(8):
            mm(pkv, pkv[0:NS, 0:256], hTs, hTs[:, kc, :], Win, Win[:, kc, 1536:1792], start=(kc == 0), stop=(kc == 7))
        cp(kvs, kvs[:], pkv, pkv[0:NS, 0:256])
        dma("sp", TO["wk_s"], O["wk_s"][i], kvs, kvs[:, 0:128])
        dma("sp", TO["wv_s"], O["wv_s"][i], kvs, kvs[:, 128:256])
        dma("sp", T_scr, scr[:, 0:128], kvs, kvs[:, 128:256])
        vrow = al([1, NS, 128], F32, name="vrow")
        dma("sp", vrow, vrow[:], T_scr, scr[:, 0:128].rearrange("(o b) c -> o b c", o=1))
        vrow_b = al([1, NS, 128], BF16, name="vrowb")
        cp(vrow_b, vrow_b[:], vrow, vrow[:])
        aTs = al([128, 4, NS, 32], BF16, name="aTs")
        for cc in range(4):
            p1 = glu(hTs, lambda kc: hTs[:, kc, :], NS, cc, sg[0])
            tt(aTs, aTs[:, cc, :, 30], p1, p1[:, 0:NS], sg[0], sg[0][:, 0:NS], ALU.mult)
        hst = [al([32, 512], F32, name="hst") for _ in range(2)]
        for b in range(NS):
            hs = hst[b % 2]
            dma("sp", hs, hs[0:30, :], TI["cconf"], I["cconf"][i, b])
            ph = nps()
            phv = ph[:, 0:128].rearrange("p (c j) -> p c j", c=4)
            for cc in range(4):
                tr(ph, phv[:, cc, 0:30], hs, hs[0:30, cc * 128:(cc + 1) * 128], ident_f, ident_f[0:30, 0:30])
            cp(aTs, aTs[:, :, b, 0:30], ph, phv[:, :, 0:30])
        cts = al([128, 8, NS], BF16, name="cts")
        conf_tail(NS, lambda cc, j: aTs[:, cc, :, j], aTs, cts, lambda cc: cts[:, cc, :])
        qTs = al([128, 4, NS], BF16, name="qTs")
        for qi in range(4):
            pq = nps()
            for kc in range(8):
                mm(pq, pq[:, 0:NS], Win, Win[:, kc, 1024 + qi * 128:1024 + (qi + 1) * 128], hTs, hTs[:, kc, :],
                   start=(kc == 0), stop=(kc == 7))
            cp(qTs, qTs[:, qi, :], pq, pq[:, 0:NS])
        kTs = al([128, NS], BF16, name="kTs")
        pk = nps()
        for kc in range(8):
            mm(pk, pk[:, 0:NS], Win, Win[:, kc, 1536:1664], hTs, hTs[:, kc, :], start=(kc == 0), stop=(kc == 7))
        cp(kTs, kTs[:], pk, pk[:, 0:NS])
        kc_b = [al([128, 128], BF16, name="kcb") for _ in range(2)]
        vc_b = [al([128, 128], BF16, name="vcb") for _ in range(2)]
        kTe = [al([128, 144], BF16, name="kTe") for _ in range(2)]
        for bp in range(NS // 2):
            streams = []
            for b in (2 * bp, 2 * bp + 1):
                kc_, vc_, ke = kc_b[b % 2], vc_b[b % 2], kTe[b % 2]
                dma("pool", kc_, kc_[:], TI["ck"], I["ck"][i, b])
                dma("pool", vc_, vc_[:], TI["cv"], I["cv"][i, b])
                pkt = nps()
                pkv_ = pkt[:, 0:64].bitcast(BF16)
                tr(pkt, pkv_[:, 0:128], kc_, kc_[:], ident_b, ident_b[:])
                cp(ke, ke[:, 0:128], pkt, pkv_[:, 0:128])
                cp(ke, ke[:, 128:129], kTs, kTs[:, b:b + 1])
                streams.append((1, 1,
                                (lambda b_: (lambda i_, two: (qTs, qTs[64 * two:64 * two + 64, i_, b_:b_ + 1])))(b),
                                (lambda ke_: (lambda two: (ke_, ke_[64 * two:64 * two + 64, 0:129])))(ke),
                                mask1[0:1, 0:129], (vc_, vc_[:, :]), (vrow_b, vrow_b[0:1, b, :]),
                                AW[b % 2], cts, (lambda b_: (lambda: cts[:, 4:8, b_:b_ + 1]))(b)))
            attn_tiles(streams, sinkbc)
        mxs = mix32[0]
        for dh in range(2):
            po = nps()
            for c8 in range(8):
                mm(po, po[0:NS, :], cts, cts[:, c8, :], Wout, Wout[:, c8, dh * 512:(dh + 1) * 512], start=(c8 == 0), stop=(c8 == 7))
            cp(mxs, mxs[0:NS, dh * 512:(dh + 1) * 512], po, po[0:NS, :])
        post_norm_add(mxs, mxs[0:NS, :], NS, g1, XS, XS[:], XS, XS[:], xt2[0])
        state["xsrc"] = xres
        state["xsrc_t"] = T_xres


    def ssd_layer(layer):
        i = layer // 2
        areset()
        g0 = load_gamma(layer * 4 + 0)
        g1 = load_gamma(layer * 4 + 1)
        win = I["ssd_w_in"]
        Wout = al([128, 16, D], BF16, name="Wout")
        ngb = al([128, 2048], BF16, name="ngb")
        dbc = al([128, 32], F32, name="dbc")
        abc = al([128, 32], F32, name="abc")
        dtb = al([128, 32], F32, name="dtb")
        dma("sp", dbc, dbc[:], TI["ssd_d"], I["ssd_d"][i:i + 1, :].partition_broadcast(128))
        dma("sp", abc, abc[:], TI["ssd_a_log"], I["ssd_a_log"][i:i + 1, :].partition_broadcast(128))
        dma("sp", dtb, dtb[:], TI["ssd_dt_bias"], I["ssd_dt_bias"][i:i + 1, :].partition_broadcast(128))
        act(abc, abc[:], abc, abc[:], AF.Exp)
        ts(abc, abc[:], abc, abc[:], -1.0, None, ALU.mult)
        diagD = al([128, 32, 128], BF16, name="diagD")
        for h in range(32):
            ts(diagD, diagD[:, h, :], ident_b, ident_b[:], dbc[:, h:h + 1], None, ALU.mult, extra_r=[dbc])
        cwT = al([128, 24, 5], F32, name="cwT")
        rowtmp = al([8, 512], F32, name="rowtmp")
        for blk in range(6):
            rows = [(TI["ssd_conv_w"], I["ssd_conv_w"][i, k:k + 1, blk * 512:(blk + 1) * 512]) for k in range(4)]
            rows.append((TI["ssd_conv_b"], I["ssd_conv_b"][i:i + 1, blk * 512:(blk + 1) * 512]))
            load_cols(rows, 5, 512, cwT, lambda cc: cwT[:, blk * 4 + cc, :], rowtmp)
        hist0 = al([128, 24, 4], BF16, name="hist0")
        hist3 = al([128, 24, 4], BF16, name="hist3")
        hist3f = al([128, 24, 3], F32, name="hist3f")
        Wb = [al([128, 8, 512], BF16, name="Wb") for _ in range(2)]
        wsel = [0]

        def load_wblk(c0, w):
            wsel[0] ^= 1
            wb = Wb[wsel[0]]
            dma("pool", wb, wb[:, :, 0:w], TI["ssd_w_in"], win[i, :, c0:c0 + w].rearrange("(k p) c -> p k c", p=128))
            return wb

        amark = apos[0]
        hT = al([128, 8, 512], BF16, n=4, name="hT")
        xn = al([128, D], BF16, name="xn")
        xtl = [al([128, D], F32, name="xtl")] * 2
        rawb = [al([128, 516], BF16, name="rawb") for _ in range(2)]
        dg = [al([128, 4, 128], BF16, name="dg") for _ in range(2)]
        xc = al([128, 16, 512], BF16, name="xc")
        BT = al([128, 4, 512], BF16, name="BT")
        CT = al([128, 4, 512], BF16, name="CT")
        zs = al([128, 4, 2048], BF16, n=4, name="zs")
        dts = al([128, 4, 64], F32, n=4, name="dts")
        xdt = al([128, 2048], BF16, name="xdt")
        xtok = al([128, 2048], BF16, name="xtok")
        xde = al([128, 2048], BF16, name="xde")
        hTh = Tl("hTh_alias", xde[:, 0:1024].rearrange("p (k m) -> p k m", k=8))
        hTh.lw, hTh.rd = xde.lw, xde.rd
        Btok = al([128, 512], BF16, name="Btok")
        sm_ = al([128, 8, 32], F32, name="small")
        hi_ = al([128, 32], BF16, name="hi")
        lo_ = al([128, 32], BF16, name="lo")
        Zhi = al([128, 8, 128], BF16, name="Zhi")
        Zlo = al([128, 8, 128], BF16, name="Zlo")
        Eb = al([128, 8, 128], BF16, name="Eb")
        cbm = al([128, 128], BF16, name="cbm")
        MT = al([128, 8, 128], BF16, name="MT")
        yacc = al([128, 2048], F32, name="yacc")
        yn = al([128, 2048], BF16, name="yn")
        yo = Tl("yo_alias", yn[:, 0:1024].bitcast(F32))
        yo.lw, yo.rd = yn.lw, yn.rd
        ynT = al([128, 16, 128], BF16, name="ynT")
        H = al([128, 2048], F32, name="H")
        Hb = al([128, 2048], BF16, name="Hb")

        def v3(ap, a):
            return ap.rearrange("p (a b) -> p a b", a=a)

        def conv_chunk(j, pu, n, dst_t, dst_ap, hist_src):
            rb = rawb[j % 2]
            d_ = dg[j % 2]
            cp(rb, rb[:, 3:3 + n], pu, pu[:, 0:n], eng="act")
            cp(rb, rb[:, 0:3], hist_src, hist_src[:, j, 0:3])
            cp(hist3, hist3[:, j, 0:3], rb, rb[:, n:n + 3])
            cp(hist3f, hist3f[:, j, :], pu, pu[:, n - 3:n], eng="act")
            for k in range(4):
                ts(d_, d_[:, k, :], ident_b, ident_b[:], cwT[:, j, k:k + 1], None, ALU.mult, extra_r=[cwT])

            def tail():
                pc = nps()
                for k in range(4):
                    mm(pc, pc[:, 0:n], d_, d_[:, k, :], rb, rb[:, k:k + n], start=(k == 0), stop=(k == 3))
                act(dst_t, dst_ap, pc, pc[:, 0:n], AF.Silu, bias=cwT[:, j, 4:5], extra_r=[cwT])
            return tail

        def dt_tile(pdt, M, dst_t, dst_ap64):
            tt(dst_t, dst_ap64[:, 0:32], pdt, pdt[0:M, 0:32], dtb, dtb[0:M, :], ALU.add)
            act(dst_t, dst_ap64[:, 0:32], dst_t, dst_ap64[:, 0:32], AF.Exp)
            act(dst_t, dst_ap64[:, 0:32], dst_t, dst_ap64[:, 0:32], AF.Ln, bias=1.0)
            tt(dst_t, dst_ap64[:, 32:64], dst_t, dst_ap64[:, 0:32], abc, abc[0:M, :], ALU.mult)

        norm_T(xhalo, xhalo[:], 128, g0, hTh, hTh, 0, xn)
        for blk in range(6):
            wb = load_wblk(2048 + blk * 512, 512)
            for jj in range(4):
                j = blk * 4 + jj
                pu = nps()
                for kc in range(8):
                    mm(pu, pu[:, 0:128], wb, wb[:, kc, jj * 128:(jj + 1) * 128], hTh, hTh[:, kc, :], start=(kc == 0), stop=(kc == 7))
                ts(hist0, hist0[:, j, 0:3], pu, pu[:, 125:128], hasprev[:, 0:1], None, ALU.mult, extra_r=[hasprev])

        def run_pass(full):
            memset(sm_, sm_[:, 6, :], 0.0, eng="dve")
            for tg in range(int(os.environ.get('SSD_TGS', '4')) if full else 4):
                hsrc = hist0 if tg == 0 else hist3
                for lt in range(4):
                    t = tg * 4 + lt
                    x = xtl[lt % 2]
                    dma("sp", x, x[:], state["xsrc_t"].s(t), xrows(t))
                    norm_T(x, x[:], 128, g0, hT, hT.s(lt), lt * 128, xn)
                pending = None
                for blk in range(6):
                    if not full and blk == 5:
                        continue
                    wb = load_wblk(2048 + blk * 512, 512)
                    for jj in range(4):
                        j = blk * 4 + jj
                        pu = nps()
                        for kc in range(8):
                            mm(pu, pu[:, :], wb, wb[:, kc, jj * 128:(jj + 1) * 128], hT, hT[:, kc, :], start=(kc == 0), stop=(kc == 7))
                        if pending is not None:
                            pending()
                        if j < 16:
                            pending = conv_chunk(j, pu, 512, xc, xc[:, j, :], hsrc)
                        elif j < 20:
                            pending = conv_chunk(j, pu, 512, BT, BT[:, j - 16, :], hsrc)
                        else:
                            pending = conv_chunk(j, pu, 512, CT, CT[:, j - 20, :], hsrc)
                pending()
                wb = load_wblk(5120, 32)
                for lt in range(4):
                    pdt = nps()
                    for kc in range(8):
                        mm(pdt, pdt[:, 0:32], hT.s(lt), hT[:, kc, lt * 128:(lt + 1) * 128], wb, wb[:, kc, 0:32], start=(kc == 0), stop=(kc == 7))
                    dt_tile(pdt, 128, dts.s(lt), dts[:, lt, :])
                if full and not os.environ.get('ZSKIP'):
                    for blk in range(4):
                        wb = load_wblk(blk * 512, 512)
                        for lt in range(4):
                            pz = nps()
                            for kc in range(8):
                                mm(pz, pz[:, :], hT.s(lt), hT[:, kc, lt * 128:(lt + 1) * 128], wb, wb[:, kc, :], start=(kc == 0), stop=(kc == 7))
                            act(zs.s(lt), zs[:, lt, blk * 512:(blk + 1) * 512], pz, pz[:, :], AF.Silu)
                for lt in range(4):
                    t = tg * 4 + lt
                    cols = slice(lt * 128, (lt + 1) * 128)
                    dtv = dts[:, lt, 0:32]
                    dta = dts[:, lt, 32:64]
                    dsl = dts.s(lt)
                    px = psA[:, 0:1024].bitcast(BF16)
                    for j in range(16):
                        tr(psA, px[:, j * 128:(j + 1) * 128], xc, xc[:, j, cols], ident_b, ident_b[:])
                    tt(xdt, v3(xdt[:], 32), psA, v3(px, 32), dsl, dtv.unsqueeze(2).to_broadcast([128, 32, 64]), ALU.mult)
                    if full and not os.environ.get('XSKIP'):
                        cp(xtok, xtok[:], psA, px)
                    pb = nps()
                    pbv = pb[:, 0:256].bitcast(BF16)
                    for g in range(4):
                        tr(pb, pbv[:, g * 128:(g + 1) * 128], BT, BT[:, g, cols], ident_b, ident_b[:])
                    cp(Btok, Btok[:], pb, pbv)
                    pl = nps()
                    mm(pl, pl[:, 0:32], ones_f, ones_f[:], dsl, dta)
                    mm(pl, pl[:, 32:64], tri_f, tri_f[:], dsl, dta)
                    cp(sm_, sm_[:, 0, :], pl, pl[:, 0:32])
                    cp(sm_, sm_[:, 1, :], pl, pl[:, 32:64])
                    tt(sm_, sm_[:, 6, :], sm_, sm_[:, 6, :], sm_, sm_[:, 0, :], ALU.add)
                    tt(sm_, sm_[:, 5, :], sm_, sm_[:, 0, :], sm_, sm_[:, 1, :], ALU.subtract)
                    act(sm_, sm_[:, 3, :], sm_, sm_[:, 5, :], AF.Exp)
                    act(sm_, sm_[:, 4, :], sm_, sm_[:, 0, :], AF.Exp)
                    tt(xde, v3(xde[:], 32), xdt, v3(xdt[:], 32), sm_, sm_[:, 3, :].unsqueeze(2).to_broadcast([128, 32, 64]), ALU.mult)
                    BSTOP = int(os.environ.get('BSTOP', '9'))
                    if full and BSTOP >= 2:
                        act(sm_, sm_[:, 2, :], sm_, sm_[:, 1, :], AF.Exp)
                        cp(hi_, hi_[:], dsl, dta)
                        tt(sm_, sm_[:, 5, :], dsl, dta, hi_, hi_[:], ALU.subtract)
                        cp(lo_, lo_[:], sm_, sm_[:, 5, :])
                        for g in range(4):
                            hs = slice(g * 8, (g + 1) * 8)
                            upb = up_b[:].unsqueeze(1).to_broadcast([128, 8, 128])
                            tt(Zhi, Zhi[:], up_b, upb, hi_, hi_[:, hs].unsqueeze(2).to_broadcast([128, 8, 128]), ALU.mult)
                            tt(Zlo, Zlo[:], up_b, upb, lo_, lo_[:, hs].unsqueeze(2).to_broadcast([128, 8, 128]), ALU.mult)
                            pD = v3(psB[:, :], 8)
                            for hh in range(8):
                                mm(psB, pD[:, hh, :], Zhi, Zhi[:, hh, :], tri_b, tri_b[:], start=True, stop=False)
                                mm(psB, pD[:, hh, :], Zlo, Zlo[:, hh, :], tri_b, tri_b[:], start=False, stop=True)
                            act(Eb, Eb[:, 0:4, :], psB, pD[:, 0:4, :], AF.Exp)
                            act(Eb, Eb[:, 4:8, :], psB, pD[:, 4:8, :], AF.Exp)
                            if BSTOP <= 2:
                                continue
                            pcb = nps()
                            mm(pcb, pcb[:, 0:128], BT, BT[:, g, cols], CT, CT[:, g, cols])
                            tt(cbm, cbm[:], pcb, pcb[:, 0:128], trimask_b, trimask_b[:], ALU.mult)
                            tt(MT, MT[:], Eb, Eb[:], cbm, cbm[:].unsqueeze(1).to_broadcast([128, 8, 128]), ALU.mult)
                            py = nps()
                            for hh in range(8):
                                h = g * 8 + hh
                                mm(py, py[:, hh * 64:(hh + 1) * 64], MT, MT[:, hh, :], xdt, xdt[:, h * 64:(h + 1) * 64], start=True, stop=False)
                                mm(py, py[:, hh * 64:(hh + 1) * 64], diagD, diagD[:, h, :], xtok, xtok[:, h * 64:(h + 1) * 64], start=False, stop=True)
                            po_ = nps()
                            mm(po_, po_[:, :], CT, CT[:, g, cols], Hb, Hb[:, g * 512:(g + 1) * 512])
                            tt(yo, v3(yo[:], 8), po_, v3(po_[:, :], 8), sm_, sm_[:, 2, hs].unsqueeze(2).to_broadcast([128, 8, 64]), ALU.mult)
                            tt(yacc, yacc[:, g * 512:(g + 1) * 512], py, py[:, :], yo, yo[:], ALU.add)
                        if BSTOP <= 3:
                            continue
                        tt(yacc, yacc[:], yacc, yacc[:], zs.s(lt), zs[:, lt, :], ALU.mult)
                        for g in range(4):
                            rs = rstd_of(yacc, yacc[:, g * 512:(g + 1) * 512], 128, 2 + g, ncols=512)
                            stt(yn, yn[:, g * 512:(g + 1) * 512], yacc, yacc[:, g * 512:(g + 1) * 512], rs, ngb,
                                ngb[:, g * 512:(g + 1) * 512], ALU.mult, ALU.mult, extra_r=[sd_t])
                        if BSTOP <= 4:
                            continue
                        pt = psA[:, 0:1024].bitcast(BF16).rearrange("p (k m) -> p k m", k=16)
                        for c in range(16):
                            tr(psA, pt[:, c, :], yn, yn[:, c * 128:(c + 1) * 128], ident_b, ident_b[:])
                        cp(ynT, ynT[:, 0:8, :], psA, pt[:, 0:8, :], eng="act")
                        cp(ynT, ynT[:, 8:16, :], psA, pt[:, 8:16, :], eng="act")
                        for dh in range(2):
                            po = nps()
                            for c in range(16):
                                mm(po, po[:, :], ynT, ynT[:, c, :], Wout, Wout[:, c, dh * 512:(dh + 1) * 512], start=(c == 0), stop=(c == 15))
                            cp(yacc, yacc[:, dh * 512:(dh + 1) * 512], po, po[:, :], eng="act")
                        x = xtl[lt % 2]
                        dma("sp", x, x[:], state["xsrc_t"].s(t), xrows(t))
                        post_norm_add(yacc, yacc[:, 0:D], 128, g1, x, x[:], x, x[:], xt2[lt % 2])
                        dma("sp", T_xres.s(t), xres[t * 128:(t + 1) * 128, :], x, x[:])
                    for g in range(4):
                        hs = slice(g * 8, (g + 1) * 8)
                        psg = nps()
                        mm(psg, psg[:, :], Btok, Btok[:, g * 128:(g + 1) * 128], xde, xde[:, g * 512:(g + 1) * 512])
                        Hg = H[:, g * 512:(g + 1) * 512]
                        tt(H, v3(Hg, 8), H, v3(Hg, 8), sm_, sm_[:, 4, hs].unsqueeze(2).to_broadcast([128, 8, 64]), ALU.mult)
                        tt(H, Hg, H, Hg, psg, psg[:, :], ALU.add)
                    if full:
                        cp(Hb, Hb[:], H, H[:], eng="act")

        SSTOP = int(os.environ.get('SSD_STOP', '9'))

        def prompt_part():
            if SSTOP <= 1:
                return
            memset(H, H[:], 0.0, eng="dve")
            run_pass(False)
            if os.environ.get('EXTRA_A'):
                run_pass(False)
            if SSTOP <= 2:
                return
            extmp = yacc
            exchange("s", [(H, H[:], 0, 2048)], extmp)
            exchange("l", [(sm_, sm_[:, 6, :], 0, 32)], extmp)
            dma("pool", Wout, Wout[:], TI["ssd_w_out"], I["ssd_w_out"][i].rearrange("(k p) c -> p k c", p=128))
            dma("pool", ngb, ngb[:], TI["ssd_norm_g"], I["ssd_norm_g"][i:i + 1, :].partition_broadcast(128))
            memset(H, H[:], 0.0, eng="dve")
            for m in range(3):
                gather_rows("s", yacc, yacc[:], idxchain, idxchain[:, m:m + 1], 0, 2048)
                gather_rows("l", sm_, sm_[:, 7, :], idxchain, idxchain[:, m:m + 1], 0, 32)
                ts(sm_, sm_[:, 7, :], sm_, sm_[:, 7, :], vmask[:, m:m + 1], None, ALU.mult, extra_r=[vmask])
                act(sm_, sm_[:, 7, :], sm_, sm_[:, 7, :], AF.Exp)
                tt(H, v3(H[:], 32), H, v3(H[:], 32), sm_, sm_[:, 7, :].unsqueeze(2).to_broadcast([128, 32, 64]), ALU.mult)
                stt(H, H[:], yacc, yacc[:], vmask[:, m:m + 1], H, H[:], ALU.mult, ALU.add, extra_r=[vmask])
            cp(Hb, Hb[:], H, H[:], eng="act")
            if SSTOP <= 3:
                return
            run_pass(True)
            if SSTOP <= 4:
                return
            pst = [psA, psB]
            for c in range(16):
                p_ = pst[c % 2]
                tr(p_, p_[:, 0:128], H, H[:, c * 128:(c + 1) * 128], ident_f, ident_f[:])
                cp(yacc, yacc[:, (c % 8) * 128:(c % 8 + 1) * 128], p_, p_[:, 0:128])
                if c % 8 == 7:
                    c0 = c - 7
                    dma("sp", TO["ssm_p"], O["ssm_p"][i, c0 * 128:(c0 + 8) * 128, :].rearrange("(c p) n -> p c n", p=128),
                        yacc, v3(yacc[:, 0:1024], 8))
            ph = nps()
            tr(ph, ph[0:72, 0:128], hist3f, hist3f[:].rearrange("p j r -> p (j r)"), ident_f, ident_f[:])
            cp(yo, yo[0:72, 0:128], ph, ph[0:72, 0:128])
            for j in range(24):
                dma("sp", TO["sc_p"], O["sc_p"][i, :, j * 128:(j + 1) * 128], yo, yo[j * 3:(j + 1) * 3, 0:128])


        if not os.environ.get('SKIP_PROMPT'):
            prompt_part()

        if SSTOP <= 5:
            state["xsrc"] = xres
            state["xsrc_t"] = T_xres
            return
        P.barrier()
        apos[0] = amark
        xn = al([128, D], BF16, name="xn")
        hTs = al([128, 8, NS], BF16, name="hTs")
        zs_s = al([NS, 2048], F32, name="zs_s")
        raw_s = al([NS, 3072], F32, name="raw_s")
        xbc_s = al([NS, 3072], F32, name="xbc_s")
        dts_s = al([NS, 64], F32, name="dts_s")
        hs_b = al([NS, 3, 512], F32, name="hs_b")
        cw_b = al([NS, 5, 512], F32, name="cw_b")
        tmp_s = al([NS, 2048], F32, name="tmp_s")
        sel_t = al([32, 128], F32, name="sel")
        dma("sp", sel_t, sel_t[:], TI["c_sel"], I["c_sel"].ap())
        norm_T(XS, XS[:], NS, g0, hTs, hTs, 0, xn)
        for blk in range(11):
            c0 = blk * 512
            w = 512 if blk < 10 else 32
            wb = load_wblk(c0, w)
            pu = nps()
            for kc in range(8):
                mm(pu, pu[0:NS, 0:w], hTs, hTs[:, kc, :], wb, wb[:, kc, 0:w], start=(kc == 0), stop=(kc == 7))
            if blk < 4:
                act(zs_s, zs_s[:, c0:c0 + 512], pu, pu[0:NS, :], AF.Silu)
            elif blk < 10:
                cp(raw_s, raw_s[:, c0 - 2048:c0 - 1536], pu, pu[0:NS, :])
            else:
                dt_tile(pu, NS, dts_s, dts_s[:, :])
        dma("sp", TO["sc_s"], O["sc_s"][i], raw_s, raw_s[:])
        for blk in range(6):
            cs_ = slice(blk * 512, (blk + 1) * 512)
            dma("sp", hs_b, hs_b[:], TI["csc"], I["csc"][i, :, :, cs_])
            for k in range(4):
                dma("sp", cw_b, cw_b[:, k, :], TI["ssd_conv_w"], I["ssd_conv_w"][i, k:k + 1, cs_].partition_broadcast(NS))
            dma("sp", cw_b, cw_b[:, 4, :], TI["ssd_conv_b"], I["ssd_conv_b"][i:i + 1, cs_].partition_broadcast(NS))
            acc = xbc_s[:, cs_]
            tt(xbc_s, acc, raw_s, raw_s[:, cs_], cw_b, cw_b[:, 3, :], ALU.mult)
            tt(xbc_s, acc, xbc_s, acc, cw_b, cw_b[:, 4, :], ALU.add)
            for k in range(3):
                tt(tmp_s, tmp_s[:, 0:512], hs_b, hs_b[:, k, :], cw_b, cw_b[:, k, :], ALU.mult)
                tt(xbc_s, acc, xbc_s, acc, tmp_s, tmp_s[:, 0:512], ALU.add)
        act(xbc_s, xbc_s[:], xbc_s, xbc_s[:], AF.Silu)
        SAMP = int(os.environ.get('SAMP_STOP', '9'))
        if SAMP <= 1:
            return
        xdt_s = al([NS, 2048], F32, name="xdt_s")
        tt(xdt_s, v3(xdt_s[:], 32), xbc_s, v3(xbc_s[:, 0:2048], 32), dts_s, dts_s[:, 0:32].unsqueeze(2).to_broadcast([NS, 32, 64]), ALU.mult)
        dma("sp", SCR["xdt"][1], SCR["xdt"][0].ap(), xdt_s, xdt_s[:])
        dma("sp", SCR["B"][1], SCR["B"][0].ap(), xbc_s, xbc_s[:, 2048:2560])
        dma("sp", SCR["C"][1], SCR["C"][0].ap(), xbc_s, xbc_s[:, 2560:3072])
        xq = al([128, NS, 16], F32, name="xq")
        dma("sp", xq, xq[:], SCR["xdt"][1], SCR["xdt"][0].ap().rearrange("b (q j) -> q b j", j=16))
        Bbc = al([128, NS, 128], F32, name="Bbc")
        Cbc = al([128, NS, 128], F32, name="Cbc")
        for g in range(4):
            dma("sp", Bbc, Bbc[32 * g:32 * (g + 1), :, :], SCR["B"][1], SCR["B"][0][:, g * 128:(g + 1) * 128].partition_broadcast(32))
            dma("sp", Cbc, Cbc[32 * g:32 * (g + 1), :, :], SCR["C"][1], SCR["C"][0][:, g * 128:(g + 1) * 128].partition_broadcast(32))
        da_s = al([NS, 32], F32, name="da_s")
        act(da_s, da_s[:], dts_s, dts_s[:, 32:64], AF.Exp)
        pda = nps()
        tr(pda, pda[0:32, 0:NS], da_s, da_s[:], ident_f, ident_f[0:NS, 0:NS])
        daT = al([32, NS], F32, name="daT")
        cp(daT, daT[:], pda, pda[0:32, 0:NS])
        pdq = nps()
        mm(pdq, pdq[:, 0:NS], sel_t, sel_t[:], daT, daT[:])
        da_q = al([128, NS], F32, name="da_q")
        cp(da_q, da_q[:], pdq, pdq[:, 0:NS])
        if SAMP <= 2:
            return
        yq = al([128, NS, 16], F32, name="yq")
        St = [al([128, 16, 128], F32, name="St")] * 2
        tmp3 = al([128, 16, 128], F32, name="tmp3")
        for b in range(NS):
            S = St[b % 2]
            dma("sp", S, S[:], TI["cssm"], I["cssm"][i, b].rearrange("(q j) n -> q j n", j=16))
            tt(tmp3, tmp3[:], xq, xq[:, b, :].unsqueeze(2).to_broadcast([128, 16, 128]),
               Bbc, Bbc[:, b, :].unsqueeze(1).to_broadcast([128, 16, 128]), ALU.mult)
            stt(S, S[:], S, S[:], da_q[:, b:b + 1], tmp3, tmp3[:], ALU.mult, ALU.add, extra_r=[da_q])
            dma("sp", TO["ssm_s"], O["ssm_s"][i, b].rearrange("(q j) n -> q j n", j=16), S, S[:])
            tt(tmp3, tmp3[:], S, S[:], Cbc, Cbc[:, b, :].unsqueeze(1).to_broadcast([128, 16, 128]), ALU.mult)
            P.add("dve", (lambda o_, i_: (lambda e: e.reduce_sum(out=o_, in_=i_, axis=AX.X)))(yq[:, b, :], tmp3[:]),
                  r=[tmp3], w=[yq])
        if SAMP <= 3:
            return
        dma("sp", SCR["y"][1], SCR["y"][0].ap().rearrange("b (q j) -> q b j", j=16), yq, yq[:])
        y_s = Tl("y_s_alias", raw_s[:, 0:2048])
        y_s.lw, y_s.rd = raw_s.lw, raw_s.rd
        dma("sp", y_s, y_s[:], SCR["y"][1], SCR["y"][0].ap())
        if SAMP <= 4:
            return
        tt(tmp_s, v3(tmp_s[:], 32), xbc_s, v3(xbc_s[:, 0:2048], 32), dbc, dbc[0:NS, :].unsqueeze(2).to_broadcast([NS, 32, 64]), ALU.mult)
        tt(y_s, y_s[:], y_s, y_s[:], tmp_s, tmp_s[:], ALU.add)
        tt(y_s, y_s[:], y_s, y_s[:], zs_s, zs_s[:], ALU.mult)
        yn_s = Tl("yn_s_alias", tmp_s[:, 0:1024].bitcast(BF16))
        yn_s.lw, yn_s.rd = tmp_s.lw, tmp_s.rd
        for g in range(4):
            rs = rstd_of(y_s, y_s[:, g * 512:(g + 1) * 512], NS, 2 + g, ncols=512)
            stt(yn_s, yn_s[:, g * 512:(g + 1) * 512], y_s, y_s[:, g * 512:(g + 1) * 512], rs, ngb,
                ngb[0:NS, g * 512:(g + 1) * 512], ALU.mult, ALU.mult, extra_r=[sd_t])
        if SAMP <= 5:
            return
        ptv_ = psA[:, 0:1024].bitcast(BF16).rearrange("p (k m) -> p k m", k=16)
        for c in range(16):
            tr(psA, ptv_[:, c, 0:NS], yn_s, yn_s[:, c * 128:(c + 1) * 128], ident_b, ident_b[0:NS, 0:NS])
        ynT_s = al([128, 16, NS], BF16, name="ynT_s")
        cp(ynT_s, ynT_s[:], psA, ptv_[:, :, 0:NS])
        mix_s = Tl("mix_s_alias", zs_s[:, 0:1024])
        mix_s.lw, mix_s.rd = zs_s.lw, zs_s.rd
        for dh in range(2):
            po = nps()
            for c in range(16):
                mm(po, po[0:NS, :], ynT_s, ynT_s[:, c, :], Wout, Wout[:, c, dh * 512:(dh + 1) * 512], start=(c == 0), stop=(c == 15))
            cp(mix_s, mix_s[:, dh * 512:(dh + 1) * 512], po, po[0:NS, :])
        post_norm_add(mix_s, mix_s[:], NS, g1, XS, XS[:], XS, XS[:], xt2[0])
        state["xsrc"] = xres
        state["xsrc_t"] = T_xres

    xt2 = [sb([128, D], F32)] * 2

    sub = 0
    for layer in range(4):
        if sub >= nsub:
            break
        if layer % 2 == 0:
            ab_layer(layer)
        else:
            ssd_layer(layer)
        sub += 1
        if sub >= nsub:
            break
        mlp(layer, last=(layer == 3))
        sub += 1
    if nsub < 8:
        for t in range(NT):
            P.add("sp", (lambda tt_: (lambda e: e.dma_start(out=O["y_p"][tt_ * 128:(tt_ + 1) * 128, :],
                                                          in_=state["xsrc"][tt_ * 128:(tt_ + 1) * 128, :])))(t),
                  r=[state["xsrc_t"].s(t)], w=[TO["y_p"]], dma=True)
        dma("sp", TO["y_s"], O["y_s"].ap(), XS, XS[:])
    P.barrier()
    cnt, dcnt = P.emit(st)
    st.close()
    return nc, len(P.ins)


def _consts(c):
    pos = c % 4
    i = np.arange(128)
    ident = np.eye(128, dtype=np.float32)
    tri = (i[:, None] <= i[None, :]).astype(np.float32)
    up = (i[:, None] > i[None, :]).astype(np.float32)
    m_own = np.where(i[None, :] <= i[:, None], 0.0, NEG).astype(np.float32)
    m_prev = np.where(i[None, :] > i[:, None], 0.0, NEG).astype(np.float32)
    mask1 = np.concatenate([m_prev, m_own], axis=1)
    mask0 = mask1.copy()
    if pos == 0:
        mask0[:, :128] = NEG
    hasprev = np.full((128, 1), 1.0 if pos > 0 else 0.0, np.float32)
    pub = np.zeros((128, 4), np.float32)
    pub[:, pos] = 1.0
    prev = max(pos - 1, 0)
    idxprev = (prev * 128 + i).astype(np.int32).reshape(128, 1)
    idxchain = np.stack([(m * 128 + i) for m in range(3)], axis=1).astype(np.int32)
    vmask = np.zeros((128, 3), np.float32)
    for m in range(3):
        if m < pos:
            vmask[:, m] = 1.0
    sel = (i[None, :] // 4 == np.arange(32)[:, None]).astype(np.float32)
    return dict(c_sel=sel, c_ident=ident, c_tri=tri, c_up=up, c_mask0=mask0, c_mask1=mask1, c_hasprev=hasprev,
                c_pub=pub, c_idxprev=idxprev, c_idxchain=idxchain, c_vmask=vmask)


def make_in_maps(inp):
    f = lambda a: np.ascontiguousarray(np.asarray(a, dtype=np.float32))
    xp = f(inp["x_prompt"])
    maps = []
    shared = dict(
        norm_g=f(inp["norm_g"]).reshape(16, D), ab_w_in=f(inp["ab_w_in"]), conf_dw_w=f(inp["conf_dw_w"]),
        conf_dw_b=f(inp["conf_dw_b"]), conf_ln_g=f(inp["conf_ln_g"]), conf_ln_b=f(inp["conf_ln_b"]),
        attn_sinks=f(inp["attn_sinks"]), ab_w_out=f(inp["ab_w_out"]), ssd_w_in=f(inp["ssd_w_in"]),
        ssd_conv_w=f(inp["ssd_conv_w"]), ssd_conv_b=f(inp["ssd_conv_b"]), ssd_dt_bias=f(inp["ssd_dt_bias"]),
        ssd_a_log=f(inp["ssd_a_log"]), ssd_d=f(inp["ssd_d"]), ssd_norm_g=f(inp["ssd_norm_g"]),
        ssd_w_out=f(inp["ssd_w_out"]), mlp_w_up=f(inp["mlp_w_up"]), mlp_w_down=f(inp["mlp_w_down"]))
    for c in range(NCORES):
        seq, pos = c // 4, c % 4
        m = dict(shared)
        m["xp"] = xp[seq, pos * TPC:(pos + 1) * TPC]
        m["xh"] = xp[seq, pos * TPC - 128:pos * TPC] if pos > 0 else np.zeros((128, D), np.float32)
        sl = slice(c * NS, (c + 1) * NS)
        m["xs"] = f(inp["x_sample"])[sl, 0]
        m["ck"] = f(inp["cache_win_k"])[:, sl].reshape(2, NS, 128, 128)
        m["cv"] = f(inp["cache_win_v"])[:, sl].reshape(2, NS, 128, 128)
        m["cconf"] = f(inp["state_conf_conv"])[:, sl]
        m["cssm"] = f(inp["state_ssm"])[:, sl].reshape(2, NS, 2048, 128)
        m["csc"] = f(inp["state_ssd_conv"])[:, sl]
        m.update(_consts(c))
        m = {k: np.ascontiguousarray(v) for k, v in m.items()}
        maps.append(m)
    return maps


_CACHE = {}


def run(inp, nsub=8):
    if nsub not in _CACHE:
        _CACHE[nsub] = build(nsub)
    nc, _ = _CACHE[nsub]
    res = run_bass_kernel_spmd(nc, make_in_maps(inp), core_ids=list(range(NCORES)))
    return res.results


def kernel(**inp):
    r = run(inp, 8)
    cat = lambda name, cores: np.concatenate([r[c][name] for c in cores], axis=0)
    y_p = np.stack([cat("y_p", range(0, 4)), cat("y_p", range(4, 8))])
    y_s = cat("y_s", range(8)).reshape(128, 1, D)
    wk_p = np.stack([r[3]["wk_p"], r[7]["wk_p"]], axis=1).reshape(2, 2, 128, 2, 64)
    wv_p = np.stack([r[3]["wv_p"], r[7]["wv_p"]], axis=1).reshape(2, 2, 128, 2, 64)
    wk_s = np.concatenate([r[c]["wk_s"] for c in range(8)], axis=1).reshape(2, 128, 1, 2, 64)
    wv_s = np.concatenate([r[c]["wv_s"] for c in range(8)], axis=1).reshape(2, 128, 1, 2, 64)
    cc_p = np.stack([r[3]["cc_p"], r[7]["cc_p"]], axis=1)
    cc_s = np.concatenate([r[c]["cc_s"] for c in range(8)], axis=1).reshape(2, 128, 1, 512)
    ssm_p = np.stack([r[3]["ssm_p"], r[7]["ssm_p"]], axis=1).reshape(2, 2, 32, 64, 128)
    ssm_s = np.concatenate([r[c]["ssm_s"] for c in range(8)], axis=1).reshape(2, 128, 32, 64, 128)
    sc_p = np.stack([r[3]["sc_p"], r[7]["sc_p"]], axis=1)
    sc_s = np.concatenate([r[c]["sc_s"] for c in range(8)], axis=1).reshape(2, 128, 1, 3072)
    outs = (y_p, y_s, wk_p, wv_p, wk_s, wv_s, cc_p, cc_s, ssm_p, ssm_s, sc_p, sc_s)
    return tuple(np.ascontiguousarray(o, dtype=np.float32) for o in outs)
```

```python
import os
import numpy as np
import concourse.bass as bass
import concourse.mybir as mybir
from concourse.bass_utils import run_bass_kernel_spmd
from contextlib import ExitStack

F32 = mybir.dt.float32
BF16 = mybir.dt.bfloat16
I32 = mybir.dt.int32
ALU = mybir.AluOpType
AF = mybir.ActivationFunctionType
AX = mybir.AxisListType
DTB = {F32: 4, BF16: 2, I32: 4}

NCORES = 8
NT = 16
TPC = NT * 128
NS = 16
D = 1024
NEG = -30000.0


class Tl:
    def __init__(self, name, ap, nslots=1):
        self.name = name
        self.ap = ap
        self.n = nslots
        self.lw = [None] * nslots
        self.rd = [[] for _ in range(nslots)]

    def __getitem__(self, k):
        return self.ap[k]

    def s(self, lo, hi=None):
        return (self, lo, lo + 1 if hi is None else hi)


def _norm(x):
    if isinstance(x, Tl):
        return (x, 0, x.n)
    return x


class Ins:
    __slots__ = ("eng", "fn", "deps", "is_dma", "sig", "tick", "sem", "idx", "inc")

    def __init__(self, eng, fn, is_dma, inc):
        self.eng = eng
        self.fn = fn
        self.is_dma = is_dma
        self.inc = inc
        self.deps = set()
        self.sig = False
        self.tick = 0
        self.sem = None


COMPUTE = ("pe", "act", "dve", "pool")


class Prog:
    def __init__(self, nc, n_dma_sems=8):
        self.nc = nc
        self.ins = []
        self.n_dma_sems = n_dma_sems
        self.last_on_eng = {}
        self.barrier_pending = {}
        self.dma_rr = {"sp": 0, "pool": 0}
        self.dma_last = {}

    def add(self, eng, fn, r=(), w=(), dma=False, cc=False):
        i = len(self.ins)
        ins = Ins(eng, fn, dma or cc, 1 if cc else 16)
        ins.idx = i
        for x in r:
            t, lo, hi = _norm(x)
            for s in range(lo, hi):
                if t.lw[s] is not None:
                    ins.deps.add(t.lw[s])
        for x in w:
            t, lo, hi = _norm(x)
            for s in range(lo, hi):
                if t.lw[s] is not None:
                    ins.deps.add(t.lw[s])
                for rr in t.rd[s]:
                    ins.deps.add(rr)
        for x in r:
            t, lo, hi = _norm(x)
            for s in range(lo, hi):
                t.rd[s].append(i)
        for x in w:
            t, lo, hi = _norm(x)
            for s in range(lo, hi):
                t.lw[s] = i
                t.rd[s] = []
        if cc:
            ins.sem = ("cc", 0)
            prev = self.dma_last.get(ins.sem)
            if prev is not None:
                ins.deps.add(prev)
            self.dma_last[ins.sem] = i
        elif dma:
            k = self.dma_rr[eng]
            self.dma_rr[eng] = (k + 1) % self.n_dma_sems
            ins.sem = (eng, k)
            prev = self.dma_last.get(ins.sem)
            if prev is not None:
                ins.deps.add(prev)
            self.dma_last[ins.sem] = i
        if eng in self.barrier_pending:
            ins.deps |= self.barrier_pending.pop(eng)
        ins.deps.discard(i)
        self.ins.append(ins)
        self.last_on_eng[eng] = i
        return i

    def barrier(self):
        pend = set(self.last_on_eng.values())
        for v in self.dma_last.values():
            pend.add(v)
        for e in ("pe", "act", "dve", "pool", "sp"):
            self.barrier_pending[e] = set(pend) | self.barrier_pending.get(e, set())

    def emit(self, stack):
        nc = self.nc
        ins = self.ins
        for x in ins:
            nd = set()
            for d in x.deps:
                p = ins[d]
                if (not p.is_dma) and (not x.is_dma) and p.eng == "pe" and x.eng == "pe":
                    continue
                nd.add(d)
            x.deps = nd
            for d in nd:
                ins[d].sig = True
        cnt = {e: 0 for e in COMPUTE}
        dcnt = {}
        for x in ins:
            if x.is_dma:
                dcnt[x.sem] = dcnt.get(x.sem, 0) + x.inc
                x.tick = dcnt[x.sem]
            elif x.sig:
                cnt[x.eng] += 1
                x.tick = cnt[x.eng]
        sems = {}
        for e in COMPUTE:
            sems[e] = stack.enter_context(nc.semaphore("s_" + e))
        for key in dcnt:
            sems[key] = stack.enter_context(nc.semaphore("d_%s%d" % key))
        per_eng = {e: [] for e in ("pe", "act", "dve", "pool", "sp")}
        for x in ins:
            per_eng[x.eng].append(x)
        final_dma = dict(dcnt)

        def run(eng_name, e):
            waited = {}
            for x in per_eng[eng_name]:
                need = {}
                for d in x.deps:
                    p = ins[d]
                    key = p.sem if p.is_dma else p.eng
                    if p.tick > need.get(key, 0):
                        need[key] = p.tick
                for key, v in need.items():
                    if waited.get(key, 0) < v:
                        e.wait_ge(sems[key], v)
                        waited[key] = v
                bi = x.fn(e)
                if x.is_dma:
                    if x.inc == 1:
                        bi.then_inc(sems[x.sem])
                    else:
                        bi.then_inc(sems[x.sem], 16)
                elif x.sig:
                    bi.then_inc(sems[x.eng], 1)
            for key, v in final_dma.items():
                owner = "pool" if key[0] == "cc" else key[0]
                if owner == eng_name and waited.get(key, 0) < v:
                    e.wait_ge(sems[key], v)

        with nc.Block() as block:
            @block.tensor
            def _(e):
                run("pe", e)

            @block.scalar
            def _(e):
                run("act", e)

            @block.vector
            def _(e):
                run("dve", e)

            @block.gpsimd
            def _(e):
                run("pool", e)

            @block.sync
            def _(e):
                run("sp", e)
        return cnt, dcnt


IN_SPECS = [
    ("xp", [TPC, D], F32), ("xh", [128, D], F32), ("xs", [NS, D], F32),
    ("ck", [2, NS, 128, 128], F32), ("cv", [2, NS, 128, 128], F32),
    ("cconf", [2, NS, 30, 512], F32), ("cssm", [2, NS, 2048, 128], F32),
    ("csc", [2, NS, 3, 3072], F32),
    ("norm_g", [16, D], F32), ("ab_w_in", [2, D, 1792], F32),
    ("conf_dw_w", [2, 31, 512], F32), ("conf_dw_b", [2, 512], F32),
    ("conf_ln_g", [2, 512], F32), ("conf_ln_b", [2, 512], F32),
    ("attn_sinks", [2, 8], F32), ("ab_w_out", [2, D, D], F32),
    ("ssd_w_in", [2, D, 5152], F32), ("ssd_conv_w", [2, 4, 3072], F32),
    ("ssd_conv_b", [2, 3072], F32), ("ssd_dt_bias", [2, 32], F32),
    ("ssd_a_log", [2, 32], F32), ("ssd_d", [2, 32], F32),
    ("ssd_norm_g", [2, 2048], F32), ("ssd_w_out", [2, 2048, D], F32),
    ("mlp_w_up", [4, D, 4096], F32), ("mlp_w_down", [4, 4096, D], F32),
    ("c_ident", [128, 128], F32), ("c_tri", [128, 128], F32), ("c_up", [128, 128], F32),
    ("c_mask0", [128, 256], F32), ("c_mask1", [128, 256], F32),
    ("c_hasprev", [128, 1], F32), ("c_pub", [128, 4], F32),
    ("c_sel", [32, 128], F32), ("c_idxprev", [128, 1], I32), ("c_idxchain", [128, 3], I32), ("c_vmask", [128, 3], F32),
]
OUT_SPECS = [
    ("y_p", [TPC, D]), ("y_s", [NS, D]),
    ("wk_p", [2, 128, 128]), ("wv_p", [2, 128, 128]),
    ("wk_s", [2, NS, 128]), ("wv_s", [2, NS, 128]),
    ("cc_p", [2, 30, 512]), ("cc_s", [2, NS, 512]),
    ("ssm_p", [2, 2048, 128]), ("ssm_s", [2, NS, 2048, 128]),
    ("sc_p", [2, 3, 3072]), ("sc_s", [2, NS, 3072]),
]


def build(nsub=8):
    nc = bass.Bass("TRN2", target_bir_lowering=False)
    st = ExitStack()
    P = Prog(nc)
    I = {}
    for name, shape, dt in IN_SPECS:
        I[name] = nc.dram_tensor(name, shape, dt, kind="ExternalInput")
    O = {}
    for name, shape in OUT_SPECS:
        O[name] = nc.dram_tensor(name, shape, F32, kind="ExternalOutput")
    TI = {k: Tl(k, v) for k, v in I.items()}
    TO = {k: Tl(k, v) for k, v in O.items()}
    xres = nc.dram_tensor("xres", [TPC, D], F32)
    T_xres = Tl("xres", xres, NT)
    EX = {}
    for nm_, w_ in (("h", D), ("s", 2048), ("l", 32)):
        a_ = nc.dram_tensor("ex_in_" + nm_, [4 * 128, w_], F32)
        b_ = nc.dram_tensor("ex_out_" + nm_, [4 * 128, w_], F32)
        EX[nm_] = (a_, b_, Tl("exi" + nm_, a_), Tl("exo" + nm_, b_))
    scr = nc.dram_tensor("scr", [NS, 128], F32)
    T_scr = Tl("scr", scr)
    SCR = {}
    for nm_, w_ in (("xdt", 2048), ("B", 512), ("C", 512), ("y", 2048)):
        a_ = nc.dram_tensor("scr_" + nm_, [NS, w_], F32)
        SCR[nm_] = (a_, Tl("scr_" + nm_, a_))

    uid = [0]

    def sb(shape, dt, n=1, name=None):
        uid[0] += 1
        nm = (name or "t") + str(uid[0])
        return Tl(nm, st.enter_context(nc.sbuf_tensor(nm, shape, dt)), n)

    ARENA_COLS = 90 * 1024
    arena = st.enter_context(nc.sbuf_tensor("arena", [128, ARENA_COLS], BF16))
    apos = [0]

    def areset():
        P.barrier()
        apos[0] = 0

    def al(shape, dt, n=1, name="a"):
        cols = int(np.prod(shape[1:])) * DTB[dt] // 2
        cols = (cols + 15) // 16 * 16
        off = apos[0]
        apos[0] += cols
        assert apos[0] <= ARENA_COLS, ("arena overflow", name, apos[0])
        ap = arena[0:shape[0], off:off + cols]
        if dt != BF16:
            ap = ap.bitcast(dt)
        tot = int(np.prod(shape[1:]))
        ap = ap[:, 0:tot]
        if len(shape) == 3:
            ap = ap.rearrange("p (a b) -> p a b", a=shape[1])
        elif len(shape) == 4:
            ap = ap.rearrange("p (a b c) -> p a b c", a=shape[1], b=shape[2])
        uid[0] += 1
        return Tl(name + str(uid[0]), ap, n)

    psA = Tl("psA", st.enter_context(nc.psum_tensor("psA", [128, 1024], F32)))
    psB = Tl("psB", st.enter_context(nc.psum_tensor("psB", [128, 1024], F32)))
    ps1 = [Tl("ps%d" % i, st.enter_context(nc.psum_tensor("ps%d" % i, [128, 512], F32))) for i in range(4)]
    psB_halves = [Tl("psB0", psB[:, 0:512]), Tl("psB1", psB[:, 512:1024])]
    rr = {"ps": 0}

    def nps():
        rr["ps"] = (rr["ps"] + 1) % 4
        return ps1[rr["ps"]]

    def mm(ot, oap, lt, lap, rt, rap, start=True, stop=True):
        P.add("pe", lambda e: e.matmul(oap, lap, rap, start=start, stop=stop), r=[lt, rt], w=[ot])

    def tr(ot, oap, it, iap, identt, idap):
        P.add("pe", lambda e: e.transpose(oap, iap, idap), r=[it, identt], w=[ot])

    def act(ot, oap, it, iap, func, bias=None, scale=None, accum=None, extra_r=(), extra_w=()):
        kw = {}
        if bias is not None:
            kw["bias"] = bias
        if scale is not None:
            kw["scale"] = scale
        if accum is not None:
            kw["accum_out"] = accum
        P.add("act", lambda e: e.activation(out=oap, in_=iap, func=func, **kw),
              r=[it] + list(extra_r), w=[ot] + list(extra_w))

    def tt(ot, oap, at, aap, bt, bap, op, eng="dve"):
        P.add(eng, lambda e: e.tensor_tensor(out=oap, in0=aap, in1=bap, op=op), r=[at, bt], w=[ot])

    def ts(ot, oap, at, aap, s1, s2, op0, op1=None, extra_r=(), eng="dve", accum=None):
        if op1 is None:
            P.add(eng, lambda e: e.tensor_scalar(out=oap, in0=aap, scalar1=s1, scalar2=None, op0=op0),
                  r=[at] + list(extra_r), w=[ot])
        else:
            P.add(eng, lambda e: e.tensor_scalar(out=oap, in0=aap, scalar1=s1, scalar2=s2, op0=op0, op1=op1),
                  r=[at] + list(extra_r), w=[ot])

    def stt(ot, oap, at, aap, scalar, bt, bap, op0, op1, extra_r=()):
        P.add("dve", lambda e: e.scalar_tensor_tensor(out=oap, in0=aap, scalar=scalar, in1=bap, op0=op0, op1=op1),
              r=[at, bt] + list(extra_r), w=[ot])

    def cp(ot, oap, it, iap, eng="dve"):
        if eng == "act":
            P.add("act", lambda e: e.copy(out=oap, in_=iap), r=[it], w=[ot])
        else:
            P.add(eng, lambda e: e.tensor_copy(out=oap, in_=iap), r=[it], w=[ot])

    def recip(ot, oap, it, iap):
        P.add("dve", lambda e: e.reciprocal(out=oap, in_=iap), r=[it], w=[ot])

    def dma(q, ot, oap, it, iap):
        P.add(q, lambda e: e.dma_start(out=oap, in_=iap), r=[it], w=[ot], dma=True)

    def memset(t, ap, v, eng="pool"):
        P.add(eng, lambda e: e.memset(ap, v), w=[t])

    ident_f = sb([128, 128], F32)
    ident_b = sb([128, 128], BF16)
    tri_f = sb([128, 128], F32)
    tri_b = sb([128, 128], BF16)
    up_b = sb([128, 128], BF16)
    trimask_b = sb([128, 128], BF16)
    ones_b = sb([128, 128], BF16)
    ones_f = sb([128, 128], F32)
    mask0 = sb([128, 256], F32)
    mask1 = sb([128, 256], F32)
    hasprev = sb([128, 1], F32)
    pub = sb([128, 4], F32)
    idxprev = sb([128, 1], I32)
    idxchain = sb([128, 3], I32)
    vmask = sb([128, 3], F32)
    XS = sb([NS, D], F32)
    gbc = [sb([128, D], F32), sb([128, D], F32)]
    junk = sb([128, 1024], BF16)
    ss_t = sb([128, 8], F32)
    sd_t = sb([128, 8], F32)
    xhalo = sb([128, D], F32)
    for t, nm in ((ident_f, "c_ident"), (tri_f, "c_tri"), (mask0, "c_mask0"), (mask1, "c_mask1"),
                  (hasprev, "c_hasprev"), (pub, "c_pub"), (idxprev, "c_idxprev"),
                  (idxchain, "c_idxchain"), (vmask, "c_vmask")):
        dma("sp", t, t[:], TI[nm], I[nm].ap())
    tmpc = sb([128, 128], F32)
    dma("sp", tmpc, tmpc[:], TI["c_up"], I["c_up"].ap())
    cp(ident_b, ident_b[:], ident_f, ident_f[:])
    cp(tri_b, tri_b[:], tri_f, tri_f[:])
    cp(trimask_b, trimask_b[:], tri_f, tri_f[:])
    cp(up_b, up_b[:], tmpc, tmpc[:])
    memset(ones_b, ones_b[:], 1.0)
    memset(ones_f, ones_f[:], 1.0)
    dma("sp", XS, XS[:], TI["xs"], I["xs"].ap())
    dma("sp", xhalo, xhalo[:], TI["xh"], I["xh"].ap())

    gsel = [0]

    def load_gamma(row):
        gsel[0] ^= 1
        g = gbc[gsel[0]]
        dma("sp", g, g[:], TI["norm_g"], I["norm_g"][row:row + 1, :].partition_broadcast(128))
        return g

    EPS = 1e-6

    def rstd_of(src_t, src_ap, M, col, ncols=D, eps=EPS):
        act(ss_t, junk[0:M, 0:ncols], src_t, src_ap, AF.Square, accum=ss_t[0:M, col:col + 1])
        act(sd_t, sd_t[0:M, col:col + 1], ss_t, ss_t[0:M, col:col + 1], AF.Ln, scale=1.0 / ncols, bias=eps)
        act(sd_t, sd_t[0:M, col:col + 1], sd_t, sd_t[0:M, col:col + 1], AF.Exp, scale=-0.5)
        return sd_t[0:M, col:col + 1]

    def norm_T(src_t, src_ap, M, g, hT, hT_slot, tok0, xn_t):
        rs = rstd_of(src_t, src_ap, M, 0)
        stt(xn_t, xn_t[0:M, :], src_t, src_ap, rs, g, g[0:M, :], ALU.mult, ALU.mult, extra_r=[sd_t])
        pt = psA
        ptv = pt[:, 0:512].bitcast(BF16).rearrange("p (k m) -> p k m", k=8)
        for kc in range(8):
            tr(pt, ptv[:, kc, 0:M], xn_t, xn_t[0:M, kc * 128:(kc + 1) * 128], ident_b, ident_b[0:M, 0:M])
        cp(hT_slot, hT[:, :, tok0:tok0 + M], pt, ptv[:, :, 0:M], eng="act")

    def post_norm_add(f_t, f_ap, M, g, x_t, x_ap, out_t, out_ap, tmp_t):
        rs = rstd_of(f_t, f_ap, M, 1)
        stt(tmp_t, tmp_t[0:M, :], f_t, f_ap, rs, g, g[0:M, :], ALU.mult, ALU.mult, extra_r=[sd_t])
        tt(out_t, out_ap, tmp_t, tmp_t[0:M, :], x_t, x_ap, ALU.add)

    state = {"xsrc": I["xp"], "xsrc_t": Tl("xp_rows", I["xp"], NT)}

    def xrows(t):
        return state["xsrc"][t * 128:(t + 1) * 128, :]

    def exchange(kind, pieces, tmp):
        ex_in, ex_out, T_exin, T_exout = EX[kind]
        for (t, ap, c0, w) in pieces:
            for r in range(4):
                ts(tmp, tmp[:, 0:w], t, ap, pub[:, r:r + 1], None, ALU.mult, extra_r=[pub])
                dma("sp", T_exin, ex_in[r * 128:(r + 1) * 128, c0:c0 + w], tmp, tmp[:, 0:w])
        P.add("pool", lambda e: e.collective_compute(
            "AllReduce", ALU.add, replica_groups=[[0, 1, 2, 3], [4, 5, 6, 7]],
            ins=[ex_in.ap().opt()], outs=[ex_out.ap().opt()]), r=[T_exin], w=[T_exout], cc=True)

    def gather_rows(kind, dst_t, dst_ap, idx_t, idx_ap, c0, w):
        ex_in, ex_out, T_exin, T_exout = EX[kind]
        P.add("pool", lambda e: e.indirect_dma_start(
            out=dst_ap, out_offset=None, in_=ex_out[:, :],
            in_offset=bass.IndirectOffsetOnAxis(ap=idx_ap, axis=0)),
            r=[T_exout, idx_t], w=[dst_t], dma=True)

    def mlp(layer, last):
        areset()
        g2 = load_gamma(layer * 4 + 2)
        g3 = load_gamma(layer * 4 + 3)
        hT = al([128, 8, TPC + NS], BF16, n=NT + 1, name="hT")
        Fa = al([128, NT, D], F32, n=NT, name="F")
        Fs = al([NS, D], F32, name="Fs")
        xt = [al([128, D], F32, name="xt") for _ in range(2)]
        xn = [al([128, D], BF16, name="xn") for _ in range(2)]
        wup = [al([128, 8, 512], BF16, name="wup") for _ in range(2)]
        wdn = [al([128, 4, D], BF16, name="wdn") for _ in range(2)]
        aT = [al([128, 4, 512], BF16, name="aT") for _ in range(2)]
        rl = [al([128, 512], BF16, name="rl") for _ in range(2)]
        for t in range(NT):
            x = xt[t % 2]
            dma("sp", x, x[:], state["xsrc_t"].s(t), xrows(t))
            norm_T(x, x[:], 128, g2, hT, hT.s(t), t * 128, xn[t % 2])
        norm_T(XS, XS[:], NS, g2, hT, hT.s(NT), TPC, xn[0])
        groups = [(tg * 512, 512, list(range(tg * 4, tg * 4 + 4))) for tg in range(4)] + [(TPC, NS, [NT])]
        STOP = int(os.environ.get('MLP_STOP', '9'))
        if STOP <= 1:
            return
        k = 0
        for fb in range(8):
            wu, wd = wup[fb % 2], wdn[fb % 2]
            dma("pool", wu, wu[:], TI["mlp_w_up"],
                I["mlp_w_up"][layer, :, fb * 512:(fb + 1) * 512].rearrange("(k p) c -> p k c", p=128))
            dma("pool", wd, wd[:], TI["mlp_w_down"],
                I["mlp_w_down"][layer, fb * 512:(fb + 1) * 512, :].rearrange("(k p) c -> p k c", p=128))
            if STOP <= 2 and fb >= 1:
                break
            for (tok0, ntok, slots) in groups:
                a = aT[k % 2]
                k += 1
                for fc in range(4):
                    pu = nps()
                    for kc in range(8):
                        mm(pu, pu[:, 0:ntok], wu, wu[:, kc, fc * 128:(fc + 1) * 128],
                           (hT, slots[0], slots[-1] + 1), hT[:, kc, tok0:tok0 + ntok], start=(kc == 0), stop=(kc == 7))
                    r_ = rl[fc % 2]
                    act(r_, r_[:, 0:ntok], pu, pu[:, 0:ntok], AF.Relu)
                    tt(a, a[:, fc, 0:ntok], r_, r_[:, 0:ntok], r_, r_[:, 0:ntok], ALU.mult)
                if STOP <= 3:
                    continue
                if ntok == NS:
                    for dh in range(2):
                        pd = nps()
                        for fc in range(4):
                            mm(pd, pd[0:NS, :], a, a[:, fc, 0:NS], wd, wd[:, fc, dh * 512:(dh + 1) * 512],
                               start=(fc == 0), stop=(fc == 3))
                        if fb == 0:
                            cp(Fs, Fs[:, dh * 512:(dh + 1) * 512], pd, pd[0:NS, :])
                        else:
                            tt(Fs, Fs[:, dh * 512:(dh + 1) * 512], pd, pd[0:NS, :], Fs, Fs[:, dh * 512:(dh + 1) * 512], ALU.add)
                else:
                    for ti, tslot in enumerate(slots):
                        for dh in range(2):
                            pd = nps()
                            for fc in range(4):
                                mm(pd, pd[:, :], a, a[:, fc, ti * 128:(ti + 1) * 128], wd, wd[:, fc, dh * 512:(dh + 1) * 512],
                                   start=(fc == 0), stop=(fc == 3))
                            fs = Fa.s(tslot)
                            if fb == 0:
                                cp(fs, Fa[:, tslot, dh * 512:(dh + 1) * 512], pd, pd[:, :], eng="act")
                            else:
                                tt(fs, Fa[:, tslot, dh * 512:(dh + 1) * 512], pd, pd[:, :], fs,
                                   Fa[:, tslot, dh * 512:(dh + 1) * 512], ALU.add)
        if STOP <= 4:
            return
        dst = O["y_p"] if last else xres
        dst_t = TO["y_p"] if last else T_xres
        for t in range(NT):
            x = xt[t % 2]
            tm = xn[t % 2]
            dma("sp", x, x[:], state["xsrc_t"].s(t), xrows(t))
            tmpf = xt2[t % 2]
            post_norm_add(Fa.s(t), Fa[:, t, :], 128, g3, x, x[:], x, x[:], tmpf)
            if t == NT - 1 and not last:
                extmp = al([128, D], F32, name="extmp")
                exchange("h", [(x, x[:], 0, D)], extmp)
                gather_rows("h", xhalo, xhalo[:], idxprev, idxprev[:, 0:1], 0, D)
            dma("sp", (dst_t, t, t + 1) if dst_t.n == NT else dst_t, dst[t * 128:(t + 1) * 128, :], x, x[:])
        tmpf = xt2[0]
        post_norm_add(Fs, Fs[:, :], NS, g3, XS, XS[:], XS, XS[:], tmpf)
        if last:
            dma("sp", TO["y_s"], O["y_s"].ap(), XS, XS[:])
        if not last:
            state["xsrc"] = xres
            state["xsrc_t"] = T_xres


    def load_cols(rows, R, C, dst_t, dst_ap_fn, tmp_rows):
        for r_, (t_, ap_) in enumerate(rows):
            dma("sp", tmp_rows, tmp_rows[r_:r_ + 1, 0:C], t_, ap_)
        for cc in range(C // 128):
            pt = nps()
            tr(pt, pt[:, 0:R], tmp_rows, tmp_rows[0:R, cc * 128:(cc + 1) * 128], ident_f, ident_f[0:R, 0:R])
            cp(dst_t, dst_ap_fn(cc), pt, pt[:, 0:R])

    def attn_tiles(streams, sinkbc):
        banks6 = ps1 + psB_halves
        for h in range(8):
            for si, (M, nown, qf, kf, mask_ap, vprev, vown, W, out_t, out_fn) in enumerate(streams):
                nk = 128 + nown
                bk = banks6[3 * si:3 * si + 3]
                sm, p_b, pT, o_b, sc = W["sm"], W["p_b"], W["pT"], W["o_b"], W["sc"]
                i_, two = h % 4, h // 4
                qt, qa = qf(i_, two)
                kt, ka = kf(two)
                s_ps = bk[0]
                mm(s_ps, s_ps[0:M, 0:nk], qt, qa, kt, ka)
                stt(sm, sm[0:M, 0:nk], s_ps, s_ps[0:M, 0:nk], 0.125, mask0, mask_ap, ALU.mult, ALU.add, extra_r=[mask1])
                P.add("dve", (lambda o_, i2: (lambda e: e.reduce_max(out=o_, in_=i2, axis=AX.X)))(sc[0:M, h:h + 1], sm[0:M, 0:nk]),
                      r=[sm], w=[sc])
                ts(sc, sc[0:M, 8 + h:9 + h], sc, sc[0:M, h:h + 1], sinkbc[0:M, h:h + 1], -1.0, ALU.max, ALU.mult, extra_r=[sinkbc])
                act(p_b, p_b[0:M, 0:nk], sm, sm[0:M, 0:nk], AF.Exp, bias=sc[0:M, 8 + h:9 + h],
                    accum=sc[0:M, 16 + h:17 + h], extra_r=[sc], extra_w=[sc])
                act(sc, sc[0:M, 24 + h:25 + h], sinkbc, sinkbc[0:M, h:h + 1], AF.Exp, bias=sc[0:M, 8 + h:9 + h], extra_r=[sc])
                tt(sc, sc[0:M, 32 + h:33 + h], sc, sc[0:M, 16 + h:17 + h], sc, sc[0:M, 24 + h:25 + h], ALU.add)
                recip(sc, sc[0:M, 32 + h:33 + h], sc, sc[0:M, 32 + h:33 + h])
                pT_ps = bk[1]
                pv = pT_ps[:, 0:128].bitcast(BF16).rearrange("p (a b) -> p a b", a=2)
                tr(pT_ps, pv[:, 0, 0:M], p_b, p_b[0:M, 0:128], ident_b, ident_b[0:M, 0:M])
                tr(pT_ps, pv[0:nown, 1, 0:M], p_b, p_b[0:M, 128:128 + nown], ident_b, ident_b[0:M, 0:M])
                cp(pT, pT[:, 0, 0:M], pT_ps, pv[:, 0, 0:M], eng="act")
                cp(pT, pT[0:nown, 1, 0:M], pT_ps, pv[0:nown, 1, 0:M], eng="act")
                o_ps = bk[2]
                mm(o_ps, o_ps[0:M, 0:64], pT, pT[:, 0, 0:M], vprev[0], vprev[1][:, two * 64:(two + 1) * 64], start=True, stop=False)
                mm(o_ps, o_ps[0:M, 0:64], pT, pT[0:nown, 1, 0:M], vown[0], vown[1][0:nown, two * 64:(two + 1) * 64], start=False, stop=True)
                ts(o_b, o_b[0:M, h * 64:(h + 1) * 64], o_ps, o_ps[0:M, 0:64], sc[0:M, 32 + h:33 + h], None, ALU.mult, extra_r=[sc])
        for (M, nown, qf, kf, mask_ap, vprev, vown, W, out_t, out_fn) in streams:
            o_b = W["o_b"]
            ot_ps = nps()
            ov = ot_ps[:, 0:256].bitcast(BF16).rearrange("p (a b) -> p a b", a=4)
            for c4 in range(4):
                tr(ot_ps, ov[:, c4, 0:M], o_b, o_b[0:M, c4 * 128:(c4 + 1) * 128], ident_b, ident_b[0:M, 0:M])
            cp(out_t, out_fn(), ot_ps, ov[:, :, 0:M])

    def ab_layer(layer):
        i = layer // 2
        areset()
        g0 = load_gamma(layer * 4 + 0)
        g1 = load_gamma(layer * 4 + 1)
        Win = al([128, 8, 1792], BF16, name="Win")
        Wout = al([128, 8, D], BF16, name="Wout")
        diag = al([128, 4, 31, 128], BF16, name="diag")
        wT = al([128, 4, 31], F32, name="wT")
        pcol = al([128, 4, 3], F32, name="pcol")
        rowtmp = al([32, 512], F32, name="rowtmp")
        sinkbc = al([128, 8], F32, name="sink")
        wsrc = I["ab_w_in"]
        dma("pool", Win, Win[:, :, 0:1024], TI["ab_w_in"], wsrc[i, :, 0:1024].rearrange("(k p) c -> p k c", p=128))
        for two in range(2):
            for kc in range(8):
                dma("pool", Win, Win[:, kc, 1024:1536].rearrange("p (i t d) -> p t i d", t=2, d=64)[:, two],
                    TI["ab_w_in"], wsrc[i, kc * 128:(kc + 1) * 128, 1024 + two * 256:1024 + (two + 1) * 256]
                    .rearrange("p (i d) -> p i d", d=64))
        dma("pool", Win, Win[:, :, 1536:1792], TI["ab_w_in"], wsrc[i, :, 1536:1792].rearrange("(k p) c -> p k c", p=128))
        dma("pool", Wout, Wout[:], TI["ab_w_out"], I["ab_w_out"][i].rearrange("(k p) c -> p k c", p=128))
        dma("sp", sinkbc, sinkbc[:], TI["attn_sinks"], I["attn_sinks"][i:i + 1, :].partition_broadcast(128))
        load_cols([(TI["conf_dw_w"], I["conf_dw_w"][i, j:j + 1, :]) for j in range(31)], 31, 512, wT,
                  lambda cc: wT[:, cc, :], rowtmp)
        load_cols([(TI["conf_dw_b"], I["conf_dw_b"][i:i + 1, :]), (TI["conf_ln_g"], I["conf_ln_g"][i:i + 1, :]),
                   (TI["conf_ln_b"], I["conf_ln_b"][i:i + 1, :])], 3, 512, pcol, lambda cc: pcol[:, cc, :], rowtmp)
        for cc in range(4):
            for j in range(31):
                ts(diag, diag[:, cc, j, :], ident_b, ident_b[:], wT[:, cc, j:j + 1], None, ALU.mult, extra_r=[wT])

        amark = apos[0]
        hT = [al([128, 8, 512], BF16, n=4, name="hT")] * 2
        hTh = al([128, 8, 128], BF16, name="hTh")
        kT_all = al([128, 17 * 128], BF16, n=17, name="kT")
        v_all = al([128, 17, 128], BF16, n=17, name="v")
        qT = al([128, 4, 512], BF16, name="qT")
        aTe = [al([128, 4, 544], BF16, name="aTe") for _ in range(2)]
        a32l = al([128, 4, 32], F32, name="a32l")
        sg = [al([128, 512], F32, name="sg") for _ in range(2)]
        c32 = al([128, 4, 512], F32, name="c32")
        cb16 = al([128, 4, 512], BF16, name="cb16")
        csq = al([128, 4, 512], BF16, name="csq")
        mean = al([128, 512], F32, name="mean")
        var = al([128, 512], F32, name="var")
        t1 = [al([128, 512], F32, name="t1") for _ in range(2)]
        catT = [al([128, 8, 512], BF16, name="catT")] * 2
        xtg = [al([128, D], F32, name="xtg") for _ in range(4)]
        xn = [al([128, D], BF16, name="xn")] * 2
        kv32 = [al([128, 256], F32, name="kv32")] * 2
        mix32 = [al([128, D], F32, name="mix")] * 2
        AW = [dict(sm=al([128, 256], F32, name="sm"), p_b=al([128, 256], BF16, name="pb"),
                   pT=al([128, 2, 128], BF16, name="pT"), o_b=al([128, 512], BF16, name="ob"),
                   sc=al([128, 40], F32, name="sc")) for _ in range(2)]

        def glu(hT_t, hT_ap_fn, n, cc, sg_t):
            p1 = nps()
            for kc in range(8):
                mm(p1, p1[:, 0:n], Win, Win[:, kc, cc * 128:(cc + 1) * 128], hT_t, hT_ap_fn(kc), start=(kc == 0), stop=(kc == 7))
            p2 = nps()
            for kc in range(8):
                mm(p2, p2[:, 0:n], Win, Win[:, kc, 512 + cc * 128:512 + (cc + 1) * 128], hT_t, hT_ap_fn(kc),
                   start=(kc == 0), stop=(kc == 7))
            act(sg_t, sg_t[:, 0:n], p2, p2[:, 0:n], AF.Sigmoid)
            return p1

        def conf_tail(n, rhs_fn, rhs_t, cat_t, cat_fn):
            for cc in range(4):
                pc = nps()
                for j in range(31):
                    mm(pc, pc[:, 0:n], diag, diag[:, cc, j, :], rhs_t, rhs_fn(cc, j), start=(j == 0), stop=(j == 30))
                act(c32, c32[:, cc, 0:n], pc, pc[:, 0:n], AF.Identity, bias=pcol[:, cc, 0:1], extra_r=[pcol])
                act(csq, csq[:, cc, 0:n], pc, pc[:, 0:n], AF.Square, bias=pcol[:, cc, 0:1], extra_r=[pcol])
                cp(cb16, cb16[:, cc, 0:n], c32, c32[:, cc, 0:n])
            st1 = nps()
            for cc in range(4):
                mm(st1, st1[:, 0:n], ones_b, ones_b[:], cb16, cb16[:, cc, 0:n], start=(cc == 0), stop=(cc == 3))
            st2 = nps()
            for cc in range(4):
                mm(st2, st2[:, 0:n], ones_b, ones_b[:], csq, csq[:, cc, 0:n], start=(cc == 0), stop=(cc == 3))
            ts(mean, mean[:, 0:n], st1, st1[:, 0:n], 1.0 / 512, None, ALU.mult)
            tt(var, var[:, 0:n], mean, mean[:, 0:n], mean, mean[:, 0:n], ALU.mult)
            stt(var, var[:, 0:n], st2, st2[:, 0:n], 1.0 / 512, var, var[:, 0:n], ALU.mult, ALU.subtract)
            act(var, var[:, 0:n], var, var[:, 0:n], AF.Sqrt, bias=1e-5)
            recip(var, var[:, 0:n], var, var[:, 0:n])
            for cc in range(4):
                t_ = t1[cc % 2]
                tt(t_, t_[:, 0:n], c32, c32[:, cc, 0:n], mean, mean[:, 0:n], ALU.subtract)
                tt(t_, t_[:, 0:n], t_, t_[:, 0:n], var, var[:, 0:n], ALU.mult)
                act(cat_t, cat_fn(cc), t_, t_[:, 0:n], AF.Silu, bias=pcol[:, cc, 2:3], scale=pcol[:, cc, 1:2], extra_r=[pcol])

        norm_T(xhalo, xhalo[:], 128, g0, hTh, hTh, 0, xn[0])
        for cc in range(4):
            p1 = glu(hTh, lambda kc: hTh[:, kc, :], 128, cc, sg[0])
            tt(t1[0], t1[0][:, 0:128], p1, p1[:, 0:128], sg[0], sg[0][:, 0:128], ALU.mult)
            ts(aTe[0], aTe[0][:, cc, 0:30], t1[0], t1[0][:, 98:128], hasprev[:, 0:1], None, ALU.mult, extra_r=[hasprev])
        pk = nps()
        for kc in range(8):
            mm(pk, pk[:, 0:128], Win, Win[:, kc, 1536:1664], hTh, hTh[:, kc, :], start=(kc == 0), stop=(kc == 7))
        cp(kT_all.s(0), kT_all[:, 0:128], pk, pk[:, 0:128])
        pv_ = nps()
        for kc in range(8):
            mm(pv_, pv_[:, 0:256], hTh, hTh[:, kc, :], Win, Win[:, kc, 1536:1792], start=(kc == 0), stop=(kc == 7))
        cp(v_all.s(0), v_all[:, 0, :], pv_, pv_[:, 128:256])

        for tg in range(4):
            h_ = hT[tg % 2]
            ae = aTe[tg % 2]
            ct = catT[tg % 2]
            for lt in range(4):
                t = tg * 4 + lt
                dma("sp", xtg[lt], xtg[lt][:], state["xsrc_t"].s(t), xrows(t))
                norm_T(xtg[lt], xtg[lt][:], 128, g0, h_, h_.s(lt), lt * 128, xn[lt % 2])
            for cc in range(4):
                s_ = sg[cc % 2]
                p1 = glu(h_, lambda kc: h_[:, kc, :], 512, cc, s_)
                tt(ae, ae[:, cc, 30:542], p1, p1[:, :], s_, s_[:, :], ALU.mult)
                if tg == 3:
                    tt(a32l, a32l[:, cc, :], p1, p1[:, 480:512], s_, s_[:, 480:512], ALU.mult)
            if tg > 0:
                cp(ae, ae[:, :, 0:30], aTe[(tg - 1) % 2], aTe[(tg - 1) % 2][:, :, 512:542])
            conf_tail(512, lambda cc, j: ae[:, cc, j:j + 512], ae, ct, lambda cc: ct[:, cc, :])
            for qi in range(4):
                pq = nps()
                for kc in range(8):
                    mm(pq, pq[:, :], Win, Win[:, kc, 1024 + qi * 128:1024 + (qi + 1) * 128], h_, h_[:, kc, :],
                       start=(kc == 0), stop=(kc == 7))
                cp(qT, qT[:, qi, :], pq, pq[:, :], eng="act")
            pk = nps()
            for kc in range(8):
                mm(pk, pk[:, :], Win, Win[:, kc, 1536:1664], h_, h_[:, kc, :], start=(kc == 0), stop=(kc == 7))
            cp(kT_all.s(tg * 4 + 1, tg * 4 + 5), kT_all[:, (tg * 4 + 1) * 128:(tg * 4 + 5) * 128], pk, pk[:, :])
            for lt in range(4):
                t = tg * 4 + lt
                kv = kv32[lt % 2]
                pv_ = nps()
                for kc in range(8):
                    mm(pv_, pv_[:, 0:256], h_.s(lt), h_[:, kc, lt * 128:(lt + 1) * 128], Win, Win[:, kc, 1536:1792],
                       start=(kc == 0), stop=(kc == 7))
                cp(kv, kv[:], pv_, pv_[:, 0:256], eng="act")
                cp(v_all.s(t + 1), v_all[:, t + 1, :], kv, kv[:, 128:256])
                if t == NT - 1:
                    dma("sp", TO["wk_p"], O["wk_p"][i], kv, kv[:, 0:128])
                    dma("sp", TO["wv_p"], O["wv_p"][i], kv, kv[:, 128:256])
            for lp in range(2):
                streams = []
                for lt in (2 * lp, 2 * lp + 1):
                    t = tg * 4 + lt
                    mk = mask0 if t == 0 else mask1
                    streams.append((128, 128,
                                    (lambda lt_: (lambda i_, two: (qT, qT[64 * two:64 * two + 64, i_, lt_ * 128:(lt_ + 1) * 128])))(lt),
                                    (lambda t_: (lambda two: (kT_all.s(t_, t_ + 2), kT_all[64 * two:64 * two + 64, t_ * 128:(t_ + 2) * 128])))(t),
                                    mk[:, :], (v_all.s(t), v_all[:, t, :]), (v_all.s(t + 1), v_all[:, t + 1, :]),
                                    AW[lt % 2], ct, (lambda lt_: (lambda: ct[:, 4:8, lt_ * 128:(lt_ + 1) * 128]))(lt)))
                attn_tiles(streams, sinkbc)
            for lt in range(4):
                t = tg * 4 + lt
                mx_ = mix32[lt % 2]
                for dh in range(2):
                    po = nps()
                    for c8 in range(8):
                        mm(po, po[:, :], ct, ct[:, c8, lt * 128:(lt + 1) * 128], Wout, Wout[:, c8, dh * 512:(dh + 1) * 512],
                           start=(c8 == 0), stop=(c8 == 7))
                    cp(mx_, mx_[:, dh * 512:(dh + 1) * 512], po, po[:, :], eng="act")
                post_norm_add(mx_, mx_[:], 128, g1, xtg[lt], xtg[lt][:], xtg[lt], xtg[lt][:], xt2[lt % 2])
                dma("sp", T_xres.s(t), xres[t * 128:(t + 1) * 128, :], xtg[lt], xtg[lt][:])
        pa = nps()
        for cc in range(4):
            tr(pa, pa[0:32, cc * 128:(cc + 1) * 128], a32l, a32l[:, cc, :], ident_f, ident_f[:])
        cp(mean, mean[0:32, :], pa, pa[0:32, :])
        dma("sp", TO["cc_p"], O["cc_p"][i], mean, mean[2:32, :])

        P.barrier()
        apos[0] = amark
        sg = [al([128, 512], F32, name="sg")]
        c32 = al([128, 4, 512], F32, name="c32")
        cb16 = al([128, 4, 512], BF16, name="cb16")
        csq = al([128, 4, 512], BF16, name="csq")
        mean = al([128, 512], F32, name="mean")
        var = al([128, 512], F32, name="var")
        t1 = [al([128, 512], F32, name="t1") for _ in range(2)]
        xn = [al([128, D], BF16, name="xn")]
        mix32 = [al([128, D], F32, name="mix")]
        AW = [dict(sm=al([128, 256], F32, name="sm"), p_b=al([128, 256], BF16, name="pb"),
                   pT=al([128, 2, 128], BF16, name="pT"), o_b=al([128, 512], BF16, name="ob"),
                   sc=al([128, 40], F32, name="sc")) for _ in range(2)]
        hTs = al([128, 8, NS], BF16, name="hTs")
        norm_T(XS, XS[:], NS, g0, hTs, hTs, 0, xn[0])
        a_tok = al([NS, 512], F32, name="a_tok")
        sgs = al([NS, 512], F32, name="sgs")
        pu1 = nps()
        for kc in range(8):
            mm(pu1, pu1[0:NS, :], hTs, hTs[:, kc, :], Win, Win[:, kc, 0:512], start=(kc == 0), stop=(kc == 7))
        pu2 = nps()
        for kc in range(8):
            mm(pu2, pu2[0:NS, :], hTs, hTs[:, kc, :], Win, Win[:, kc, 512:1024], start=(kc == 0), stop=(kc == 7))
        act(sgs, sgs[:], pu2, pu2[0:NS, :], AF.Sigmoid)
        tt(a_tok, a_tok[:], pu1, pu1[0:NS, :], sgs, sgs[:], ALU.mult)
        dma("sp", TO["cc_s"], O["cc_s"][i], a_tok, a_tok[:])
        kvs = al([NS, 256], F32, name="kvs")
        pkv = nps()
        for kc in range(8):
            mm(pkv, pkv[0:NS, 0:256], hTs, hTs[:, kc, :], Win, Win[:, kc, 1536:1792], start=(kc == 0), stop=(kc == 7))
        cp(kvs, kvs[:], pkv, pkv[0:NS, 0:256])
        dma("sp", TO["wk_s"], O["wk_s"][i], kvs, kvs[:, 0:128])
        dma("sp", TO["wv_s"], O["wv_s"][i], kvs, kvs[:, 128:256])
        dma("sp", T_scr, scr[:, 0:128], kvs, kvs[:, 128:256])
        vrow = al([1, NS, 128], F32, name="vrow")
        dma("sp", vrow, vrow[:], T_scr, scr[:, 0:128].rearrange("(o b) c -> o b c", o=1))
        vrow_b = al([1, NS, 128], BF16, name="vrowb")
        cp(vrow_b, vrow_b[:], vrow, vrow[:])
        aTs = al([128, 4, NS, 32], BF16, name="aTs")
        for cc in range(4):
            p1 = glu(hTs, lambda kc: hTs[:, kc, :], NS, cc, sg[0])
            tt(aTs, aTs[:, cc, :, 30], p1, p1[:, 0:NS], sg[0], sg[0][:, 0:NS], ALU.mult)
        hst = [al([32, 512], F32, name="hst") for _ in range(2)]
        for b in range(NS):
            hs = hst[b % 2]
            dma("sp", hs, hs[0:30, :], TI["cconf"], I["cconf"][i, b])
            ph = nps()
            phv = ph[:, 0:128].rearrange("p (c j) -> p c j", c=4)
            for cc in range(4):
                tr(ph, phv[:, cc, 0:30], hs, hs[0:30, cc * 128:(cc + 1) * 128], ident_f, ident_f[0:30, 0:30])
            cp(aTs, aTs[:, :, b, 0:30], ph, phv[:, :, 0:30])
        cts = al([128, 8, NS], BF16, name="cts")
        conf_tail(NS, lambda cc, j: aTs[:, cc, :, j], aTs, cts, lambda cc: cts[:, cc, :])
        qTs = al([128, 4, NS], BF16, name="qTs")
        for qi in range(4):
            pq = nps()
            for kc in range(8):
                mm(pq, pq[:, 0:NS], Win, Win[:, kc, 1024 + qi * 128:1024 + (qi + 1) * 128], hTs, hTs[:, kc, :],
                   start=(kc == 0), stop=(kc == 7))
            cp(qTs, qTs[:, qi, :], pq, pq[:, 0:NS])
        kTs = al([128, NS], BF16, name="kTs")
        pk = nps()
        for kc in range(8):
            mm(pk, pk[:, 0:NS], Win, Win[:, kc, 1536:1664], hTs, hTs[:, kc, :], start=(kc == 0), stop=(kc == 7))
        cp(kTs, kTs[:], pk, pk[:, 0:NS])
        kc_b = [al([128, 128], BF16, name="kcb") for _ in range(2)]
        vc_b = [al([128, 128], BF16, name="vcb") for _ in range(2)]
        kTe = [al([128, 144], BF16, name="kTe") for _ in range(2)]
        for bp in range(NS // 2):
            streams = []
            for b in (2 * bp, 2 * bp + 1):
                kc_, vc_, ke = kc_b[b % 2], vc_b[b % 2], kTe[b % 2]
                dma("pool", kc_, kc_[:], TI["ck"], I["ck"][i, b])
                dma("pool", vc_, vc_[:], TI["cv"], I["cv"][i, b])
                pkt = nps()
                pkv_ = pkt[:, 0:64].bitcast(BF16)
                tr(pkt, pkv_[:, 0:128], kc_, kc_[:], ident_b, ident_b[:])
                cp(ke, ke[:, 0:128], pkt, pkv_[:, 0:128])
                cp(ke, ke[:, 128:129], kTs, kTs[:, b:b + 1])
                streams.append((1, 1,
                                (lambda b_: (lambda i_, two: (qTs, qTs[64 * two:64 * two + 64, i_, b_:b_ + 1])))(b),
                                (lambda ke_: (lambda two: (ke_, ke_[64 * two:64 * two + 64, 0:129])))(ke),
                                mask1[0:1, 0:129], (vc_, vc_[:, :]), (vrow_b, vrow_b[0:1, b, :]),
                                AW[b % 2], cts, (lambda b_: (lambda: cts[:, 4:8, b_:b_ + 1]))(b)))
            attn_tiles(streams, sinkbc)
        mxs = mix32[0]
        for dh in range(2):
            po = nps()
            for c8 in range(8):
                mm(po, po[0:NS, :], cts, cts[:, c8, :], Wout, Wout[:, c8, dh * 512:(dh + 1) * 512], start=(c8 == 0), stop=(c8 == 7))
            cp(mxs, mxs[0:NS, dh * 512:(dh + 1) * 512], po, po[0:NS, :])
        post_norm_add(mxs, mxs[0:NS, :], NS, g1, XS, XS[:], XS, XS[:], xt2[0])
        state["xsrc"] = xres
        state["xsrc_t"] = T_xres


    def ssd_layer(layer):
        i = layer // 2
        areset()
        g0 = load_gamma(layer * 4 + 0)
        g1 = load_gamma(layer * 4 + 1)
        win = I["ssd_w_in"]
        Wout = al([128, 16, D], BF16, name="Wout")
        ngb = al([128, 2048], BF16, name="ngb")
        dbc = al([128, 32], F32, name="dbc")
        abc = al([128, 32], F32, name="abc")
        dtb = al([128, 32], F32, name="dtb")
        dma("sp", dbc, dbc[:], TI["ssd_d"], I["ssd_d"][i:i + 1, :].partition_broadcast(128))
        dma("sp", abc, abc[:], TI["ssd_a_log"], I["ssd_a_log"][i:i + 1, :].partition_broadcast(128))
        dma("sp", dtb, dtb[:], TI["ssd_dt_bias"], I["ssd_dt_bias"][i:i + 1, :].partition_broadcast(128))
        act(abc, abc[:], abc, abc[:], AF.Exp)
        ts(abc, abc[:], abc, abc[:], -1.0, None, ALU.mult)
        diagD = al([128, 32, 128], BF16, name="diagD")
        for h in range(32):
            ts(diagD, diagD[:, h, :], ident_b, ident_b[:], dbc[:, h:h + 1], None, ALU.mult, extra_r=[dbc])
        cwT = al([128, 24, 5], F32, name="cwT")
        rowtmp = al([8, 512], F32, name="rowtmp")
        for blk in range(6):
            rows = [(TI["ssd_conv_w"], I["ssd_conv_w"][i, k:k + 1, blk * 512:(blk + 1) * 512]) for k in range(4)]
            rows.append((TI["ssd_conv_b"], I["ssd_conv_b"][i:i + 1, blk * 512:(blk + 1) * 512]))
            load_cols(rows, 5, 512, cwT, lambda cc: cwT[:, blk * 4 + cc, :], rowtmp)
        hist0 = al([128, 24, 4], BF16, name="hist0")
        hist3 = al([128, 24, 4], BF16, name="hist3")
        hist3f = al([128, 24, 3], F32, name="hist3f")
        Wb = [al([128, 8, 512], BF16, name="Wb") for _ in range(2)]
        wsel = [0]

        def load_wblk(c0, w):
            wsel[0] ^= 1
            wb = Wb[wsel[0]]
            dma("pool", wb, wb[:, :, 0:w], TI["ssd_w_in"], win[i, :, c0:c0 + w].rearrange("(k p) c -> p k c", p=128))
            return wb

        amark = apos[0]
        hT = al([128, 8, 512], BF16, n=4, name="hT")
        xn = al([128, D], BF16, name="xn")
        xtl = [al([128, D], F32, name="xtl")] * 2
        rawb = [al([128, 516], BF16, name="rawb") for _ in range(2)]
        dg = [al([128, 4, 128], BF16, name="dg") for _ in range(2)]
        xc = al([128, 16, 512], BF16, name="xc")
        BT = al([128, 4, 512], BF16, name="BT")
        CT = al([128, 4, 512], BF16, name="CT")
        zs = al([128, 4, 2048], BF16, n=4, name="zs")
        dts = al([128, 4, 64], F32, n=4, name="dts")
        xdt = al([128, 2048], BF16, name="xdt")
        xtok = al([128, 2048], BF16, name="xtok")
        xde = al([128, 2048], BF16, name="xde")
        hTh = Tl("hTh_alias", xde[:, 0:1024].rearrange("p (k m) -> p k m", k=8))
        hTh.lw, hTh.rd = xde.lw, xde.rd
        Btok = al([128, 512], BF16, name="Btok")
        sm_ = al([128, 8, 32], F32, name="small")
        hi_ = al([128, 32], BF16, name="hi")
        lo_ = al([128, 32], BF16, name="lo")
        Zhi = al([128, 8, 128], BF16, name="Zhi")
        Zlo = al([128, 8, 128], BF16, name="Zlo")
        Eb = al([128, 8, 128], BF16, name="Eb")
        cbm = al([128, 128], BF16, name="cbm")
        MT = al([128, 8, 128], BF16, name="MT")
        yacc = al([128, 2048], F32, name="yacc")
        yn = al([128, 2048], BF16, name="yn")
        yo = Tl("yo_alias", yn[:, 0:1024].bitcast(F32))
        yo.lw, yo.rd = yn.lw, yn.rd
        ynT = al([128, 16, 128], BF16, name="ynT")
        H = al([128, 2048], F32, name="H")
        Hb = al([128, 2048], BF16, name="Hb")

        def v3(ap, a):
            return ap.rearrange("p (a b) -> p a b", a=a)

        def conv_chunk(j, pu, n, dst_t, dst_ap, hist_src, keep_hist=True, keep_f32=True):
            rb = rawb[j % 2]
            d_ = dg[j % 2]
            cp(rb, rb[:, 3:3 + n], pu, pu[:, 0:n], eng="act")
            cp(rb, rb[:, 0:3], hist_src, hist_src[:, j, 0:3])
            if keep_hist:
                cp(hist3, hist3[:, j, 0:3], rb, rb[:, n:n + 3])
            if keep_f32:
                cp(hist3f, hist3f[:, j, :], pu, pu[:, n - 3:n], eng="act")
            for k in range(4):
                ts(d_, d_[:, k, :], ident_b, ident_b[:], cwT[:, j, k:k + 1], None, ALU.mult, extra_r=[cwT])

            def tail():
                pc = nps()
                for k in range(4):
                    mm(pc, pc[:, 0:n], d_, d_[:, k, :], rb, rb[:, k:k + n], start=(k == 0), stop=(k == 3))
                act(dst_t, dst_ap, pc, pc[:, 0:n], AF.Silu, bias=cwT[:, j, 4:5], extra_r=[cwT])
            return tail

        def dt_tile(pdt, M, dst_t, dst_ap64):
            tt(dst_t, dst_ap64[:, 0:32], pdt, pdt[0:M, 0:32], dtb, dtb[0:M, :], ALU.add)
            act(dst_t, dst_ap64[:, 0:32], dst_t, dst_ap64[:, 0:32], AF.Exp)
            act(dst_t, dst_ap64[:, 0:32], dst_t, dst_ap64[:, 0:32], AF.Ln, bias=1.0)
            tt(dst_t, dst_ap64[:, 32:64], dst_t, dst_ap64[:, 0:32], abc, abc[0:M, :], ALU.mult)

        norm_T(xhalo, xhalo[:], 128, g0, hTh, hTh, 0, xn)
        for blk in range(6):
            wb = load_wblk(2048 + blk * 512, 512)
            for jj in range(4):
                j = blk * 4 + jj
                pu = nps()
                for kc in range(8):
                    mm(pu, pu[:, 0:128], wb, wb[:, kc, jj * 128:(jj + 1) * 128], hTh, hTh[:, kc, :], start=(kc == 0), stop=(kc == 7))
                ts(hist0, hist0[:, j, 0:3], pu, pu[:, 125:128], hasprev[:, 0:1], None, ALU.mult, extra_r=[hasprev])

        def run_pass(full):
            memset(sm_, sm_[:, 6, :], 0.0, eng="dve")
            for tg in range(int(os.environ.get('SSD_TGS', '4')) if full else 4):
                hsrc = hist0 if tg == 0 else hist3
                for lt in range(4):
                    t = tg * 4 + lt
                    x = xtl[lt % 2]
                    dma("sp", x, x[:], state["xsrc_t"].s(t), xrows(t))
                    norm_T(x, x[:], 128, g0, hT, hT.s(lt), lt * 128, xn)
                pending = None
                for blk in range(6):
                    if not full and blk == 5:
                        continue
                    wb = load_wblk(2048 + blk * 512, 512)
                    for jj in range(4):
                        j = blk * 4 + jj
                        pu = nps()
                        for kc in range(8):
                            mm(pu, pu[:, :], wb, wb[:, kc, jj * 128:(jj + 1) * 128], hT, hT[:, kc, :], start=(kc == 0), stop=(kc == 7))
                        if pending is not None:
                            pending()
                        if j < 16:
                            pending = conv_chunk(j, pu, 512, xc, xc[:, j, :], hsrc, keep_hist=(tg < 3), keep_f32=(full and tg == 3))
                        elif j < 20:
                            pending = conv_chunk(j, pu, 512, BT, BT[:, j - 16, :], hsrc, keep_hist=(tg < 3), keep_f32=(full and tg == 3))
                        else:
                            pending = conv_chunk(j, pu, 512, CT, CT[:, j - 20, :], hsrc, keep_hist=(tg < 3), keep_f32=(full and tg == 3))
                pending()
                wb = load_wblk(5120, 32)
                for lt in range(4):
                    pdt = nps()
                    for kc in range(8):
                        mm(pdt, pdt[:, 0:32], hT.s(lt), hT[:, kc, lt * 128:(lt + 1) * 128], wb, wb[:, kc, 0:32], start=(kc == 0), stop=(kc == 7))
                    dt_tile(pdt, 128, dts.s(lt), dts[:, lt, :])
                if full and not os.environ.get('ZSKIP'):
                    for blk in range(4):
                        wb = load_wblk(blk * 512, 512)
                        for lt in range(4):
                            pz = nps()
                            for kc in range(8):
                                mm(pz, pz[:, :], hT.s(lt), hT[:, kc, lt * 128:(lt + 1) * 128], wb, wb[:, kc, :], start=(kc == 0), stop=(kc == 7))
                            act(zs.s(lt), zs[:, lt, blk * 512:(blk + 1) * 512], pz, pz[:, :], AF.Silu)
                for lt in range(4):
                    t = tg * 4 + lt
                    cols = slice(lt * 128, (lt + 1) * 128)
                    dtv = dts[:, lt, 0:32]
                    dta = dts[:, lt, 32:64]
                    dsl = dts.s(lt)
                    px = psA[:, 0:1024].bitcast(BF16)
                    for j in range(16):
                        tr(psA, px[:, j * 128:(j + 1) * 128], xc, xc[:, j, cols], ident_b, ident_b[:])
                    tt(xdt, v3(xdt[:], 32), psA, v3(px, 32), dsl, dtv.unsqueeze(2).to_broadcast([128, 32, 64]), ALU.mult)
                    if full and not os.environ.get('XSKIP'):
                        cp(xtok, xtok[:], psA, px)
                    pb = nps()
                    pbv = pb[:, 0:256].bitcast(BF16)
                    for g in range(4):
                        tr(pb, pbv[:, g * 128:(g + 1) * 128], BT, BT[:, g, cols], ident_b, ident_b[:])
                    cp(Btok, Btok[:], pb, pbv)
                    pl = nps()
                    mm(pl, pl[:, 0:32], ones_f, ones_f[:], dsl, dta)
                    mm(pl, pl[:, 32:64], tri_f, tri_f[:], dsl, dta)
                    cp(sm_, sm_[:, 0, :], pl, pl[:, 0:32])
                    cp(sm_, sm_[:, 1, :], pl, pl[:, 32:64])
                    tt(sm_, sm_[:, 6, :], sm_, sm_[:, 6, :], sm_, sm_[:, 0, :], ALU.add)
                    tt(sm_, sm_[:, 5, :], sm_, sm_[:, 0, :], sm_, sm_[:, 1, :], ALU.subtract)
                    act(sm_, sm_[:, 3, :], sm_, sm_[:, 5, :], AF.Exp)
                    act(sm_, sm_[:, 4, :], sm_, sm_[:, 0, :], AF.Exp)
                    tt(xde, v3(xde[:], 32), xdt, v3(xdt[:], 32), sm_, sm_[:, 3, :].unsqueeze(2).to_broadcast([128, 32, 64]), ALU.mult)
                    BSTOP = int(os.environ.get('BSTOP', '9'))
                    if full and BSTOP >= 2:
                        act(sm_, sm_[:, 2, :], sm_, sm_[:, 1, :], AF.Exp)
                        cp(hi_, hi_[:], dsl, dta)
                        tt(sm_, sm_[:, 5, :], dsl, dta, hi_, hi_[:], ALU.subtract)
                        cp(lo_, lo_[:], sm_, sm_[:, 5, :])
                        for g in range(4):
                            hs = slice(g * 8, (g + 1) * 8)
                            upb = up_b[:].unsqueeze(1).to_broadcast([128, 8, 128])
                            tt(Zhi, Zhi[:], up_b, upb, hi_, hi_[:, hs].unsqueeze(2).to_broadcast([128, 8, 128]), ALU.mult)
                            tt(Zlo, Zlo[:], up_b, upb, lo_, lo_[:, hs].unsqueeze(2).to_broadcast([128, 8, 128]), ALU.mult)
                            pD = v3(psB[:, :], 8)
                            for hh in range(8):
                                mm(psB, pD[:, hh, :], Zhi, Zhi[:, hh, :], tri_b, tri_b[:], start=True, stop=False)
                                mm(psB, pD[:, hh, :], Zlo, Zlo[:, hh, :], tri_b, tri_b[:], start=False, stop=True)
                            act(Eb, Eb[:, 0:4, :], psB, pD[:, 0:4, :], AF.Exp)
                            act(Eb, Eb[:, 4:8, :], psB, pD[:, 4:8, :], AF.Exp)
                            if BSTOP <= 2:
                                continue
                            pcb = nps()
                            mm(pcb, pcb[:, 0:128], BT, BT[:, g, cols], CT, CT[:, g, cols])
                            tt(cbm, cbm[:], pcb, pcb[:, 0:128], trimask_b, trimask_b[:], ALU.mult)
                            tt(MT, MT[:], Eb, Eb[:], cbm, cbm[:].unsqueeze(1).to_broadcast([128, 8, 128]), ALU.mult)
                            py = nps()
                            for hh in range(8):
                                h = g * 8 + hh
                                mm(py, py[:, hh * 64:(hh + 1) * 64], MT, MT[:, hh, :], xdt, xdt[:, h * 64:(h + 1) * 64], start=True, stop=False)
                                mm(py, py[:, hh * 64:(hh + 1) * 64], diagD, diagD[:, h, :], xtok, xtok[:, h * 64:(h + 1) * 64], start=False, stop=True)
                            po_ = nps()
                            mm(po_, po_[:, :], CT, CT[:, g, cols], Hb, Hb[:, g * 512:(g + 1) * 512])
                            tt(yo, v3(yo[:], 8), po_, v3(po_[:, :], 8), sm_, sm_[:, 2, hs].unsqueeze(2).to_broadcast([128, 8, 64]), ALU.mult)
                            tt(yacc, yacc[:, g * 512:(g + 1) * 512], py, py[:, :], yo, yo[:], ALU.add)
                        if BSTOP <= 3:
                            continue
                        tt(yacc, yacc[:], yacc, yacc[:], zs.s(lt), zs[:, lt, :], ALU.mult)
                        for g in range(4):
                            rs = rstd_of(yacc, yacc[:, g * 512:(g + 1) * 512], 128, 2 + g, ncols=512)
                            stt(yn, yn[:, g * 512:(g + 1) * 512], yacc, yacc[:, g * 512:(g + 1) * 512], rs, ngb,
                                ngb[:, g * 512:(g + 1) * 512], ALU.mult, ALU.mult, extra_r=[sd_t])
                        if BSTOP <= 4:
                            continue
                        pt = psA[:, 0:1024].bitcast(BF16).rearrange("p (k m) -> p k m", k=16)
                        for c in range(16):
                            tr(psA, pt[:, c, :], yn, yn[:, c * 128:(c + 1) * 128], ident_b, ident_b[:])
                        cp(ynT, ynT[:, 0:8, :], psA, pt[:, 0:8, :], eng="act")
                        cp(ynT, ynT[:, 8:16, :], psA, pt[:, 8:16, :], eng="act")
                        for dh in range(2):
                            po = nps()
                            for c in range(16):
                                mm(po, po[:, :], ynT, ynT[:, c, :], Wout, Wout[:, c, dh * 512:(dh + 1) * 512], start=(c == 0), stop=(c == 15))
                            cp(yacc, yacc[:, dh * 512:(dh + 1) * 512], po, po[:, :], eng="act")
                        x = xtl[lt % 2]
                        dma("sp", x, x[:], state["xsrc_t"].s(t), xrows(t))
                        post_norm_add(yacc, yacc[:, 0:D], 128, g1, x, x[:], x, x[:], xt2[lt % 2])
                        dma("sp", T_xres.s(t), xres[t * 128:(t + 1) * 128, :], x, x[:])
                    for g in range(4):
                        hs = slice(g * 8, (g + 1) * 8)
                        psg = nps()
                        mm(psg, psg[:, :], Btok, Btok[:, g * 128:(g + 1) * 128], xde, xde[:, g * 512:(g + 1) * 512])
                        Hg = H[:, g * 512:(g + 1) * 512]
                        tt(H, v3(Hg, 8), H, v3(Hg, 8), sm_, sm_[:, 4, hs].unsqueeze(2).to_broadcast([128, 8, 64]), ALU.mult)
                        tt(H, Hg, H, Hg, psg, psg[:, :], ALU.add)
                    if full:
                        cp(Hb, Hb[:], H, H[:], eng="act")

        SSTOP = int(os.environ.get('SSD_STOP', '9'))

        def prompt_part():
            if SSTOP <= 1:
                return
            memset(H, H[:], 0.0, eng="dve")
            run_pass(False)
            if os.environ.get('EXTRA_A'):
                run_pass(False)
            if SSTOP <= 2:
                return
            extmp = yacc
            exchange("s", [(H, H[:], 0, 2048)], extmp)
            exchange("l", [(sm_, sm_[:, 6, :], 0, 32)], extmp)
            dma("pool", Wout, Wout[:], TI["ssd_w_out"], I["ssd_w_out"][i].rearrange("(k p) c -> p k c", p=128))
            dma("pool", ngb, ngb[:], TI["ssd_norm_g"], I["ssd_norm_g"][i:i + 1, :].partition_broadcast(128))
            memset(H, H[:], 0.0, eng="dve")
            for m in range(3):
                gather_rows("s", yacc, yacc[:], idxchain, idxchain[:, m:m + 1], 0, 2048)
                gather_rows("l", sm_, sm_[:, 7, :], idxchain, idxchain[:, m:m + 1], 0, 32)
                ts(sm_, sm_[:, 7, :], sm_, sm_[:, 7, :], vmask[:, m:m + 1], None, ALU.mult, extra_r=[vmask])
                act(sm_, sm_[:, 7, :], sm_, sm_[:, 7, :], AF.Exp)
                tt(H, v3(H[:], 32), H, v3(H[:], 32), sm_, sm_[:, 7, :].unsqueeze(2).to_broadcast([128, 32, 64]), ALU.mult)
                stt(H, H[:], yacc, yacc[:], vmask[:, m:m + 1], H, H[:], ALU.mult, ALU.add, extra_r=[vmask])
            cp(Hb, Hb[:], H, H[:], eng="act")
            if SSTOP <= 3:
                return
            run_pass(True)
            if SSTOP <= 4:
                return
            pst = [psA, psB]
            for c in range(16):
                p_ = pst[c % 2]
                tr(p_, p_[:, 0:128], H, H[:, c * 128:(c + 1) * 128], ident_f, ident_f[:])
                cp(yacc, yacc[:, (c % 8) * 128:(c % 8 + 1) * 128], p_, p_[:, 0:128])
                if c % 8 == 7:
                    c0 = c - 7
                    dma("sp", TO["ssm_p"], O["ssm_p"][i, c0 * 128:(c0 + 8) * 128, :].rearrange("(c p) n -> p c n", p=128),
                        yacc, v3(yacc[:, 0:1024], 8))
            ph = nps()
            tr(ph, ph[0:72, 0:128], hist3f, hist3f[:].rearrange("p j r -> p (j r)"), ident_f, ident_f[:])
            cp(yo, yo[0:72, 0:128], ph, ph[0:72, 0:128])
            for j in range(24):
                dma("sp", TO["sc_p"], O["sc_p"][i, :, j * 128:(j + 1) * 128], yo, yo[j * 3:(j + 1) * 3, 0:128])


        if not os.environ.get('SKIP_PROMPT'):
            prompt_part()

        if SSTOP <= 5:
            state["xsrc"] = xres
            state["xsrc_t"] = T_xres
            return
        P.barrier()
        apos[0] = amark
        xn = al([128, D], BF16, name="xn")
        hTs = al([128, 8, NS], BF16, name="hTs")
        zs_s = al([NS, 2048], F32, name="zs_s")
        raw_s = al([NS, 3072], F32, name="raw_s")
        xbc_s = al([NS, 3072], F32, name="xbc_s")
        dts_s = al([NS, 64], F32, name="dts_s")
        hs_b = al([NS, 3, 512], F32, name="hs_b")
        cw_b = al([NS, 5, 512], F32, name="cw_b")
        tmp_s = al([NS, 2048], F32, name="tmp_s")
        sel_t = al([32, 128], F32, name="sel")
        dma("sp", sel_t, sel_t[:], TI["c_sel"], I["c_sel"].ap())
        norm_T(XS, XS[:], NS, g0, hTs, hTs, 0, xn)
        for blk in range(11):
            c0 = blk * 512
            w = 512 if blk < 10 else 32
            wb = load_wblk(c0, w)
            pu = nps()
            for kc in range(8):
                mm(pu, pu[0:NS, 0:w], hTs, hTs[:, kc, :], wb, wb[:, kc, 0:w], start=(kc == 0), stop=(kc == 7))
            if blk < 4:
                act(zs_s, zs_s[:, c0:c0 + 512], pu, pu[0:NS, :], AF.Silu)
            elif blk < 10:
                cp(raw_s, raw_s[:, c0 - 2048:c0 - 1536], pu, pu[0:NS, :])
            else:
                dt_tile(pu, NS, dts_s, dts_s[:, :])
        dma("sp", TO["sc_s"], O["sc_s"][i], raw_s, raw_s[:])
        for blk in range(6):
            cs_ = slice(blk * 512, (blk + 1) * 512)
            dma("sp", hs_b, hs_b[:], TI["csc"], I["csc"][i, :, :, cs_])
            for k in range(4):
                dma("sp", cw_b, cw_b[:, k, :], TI["ssd_conv_w"], I["ssd_conv_w"][i, k:k + 1, cs_].partition_broadcast(NS))
            dma("sp", cw_b, cw_b[:, 4, :], TI["ssd_conv_b"], I["ssd_conv_b"][i:i + 1, cs_].partition_broadcast(NS))
            acc = xbc_s[:, cs_]
            tt(xbc_s, acc, raw_s, raw_s[:, cs_], cw_b, cw_b[:, 3, :], ALU.mult)
            tt(xbc_s, acc, xbc_s, acc, cw_b, cw_b[:, 4, :], ALU.add)
            for k in range(3):
                tt(tmp_s, tmp_s[:, 0:512], hs_b, hs_b[:, k, :], cw_b, cw_b[:, k, :], ALU.mult)
                tt(xbc_s, acc, xbc_s, acc, tmp_s, tmp_s[:, 0:512], ALU.add)
        act(xbc_s, xbc_s[:], xbc_s, xbc_s[:], AF.Silu)
        SAMP = int(os.environ.get('SAMP_STOP', '9'))
        if SAMP <= 1:
            return
        xdt_s = al([NS, 2048], F32, name="xdt_s")
        tt(xdt_s, v3(xdt_s[:], 32), xbc_s, v3(xbc_s[:, 0:2048], 32), dts_s, dts_s[:, 0:32].unsqueeze(2).to_broadcast([NS, 32, 64]), ALU.mult)
        dma("sp", SCR["xdt"][1], SCR["xdt"][0].ap(), xdt_s, xdt_s[:])
        dma("sp", SCR["B"][1], SCR["B"][0].ap(), xbc_s, xbc_s[:, 2048:2560])
        dma("sp", SCR["C"][1], SCR["C"][0].ap(), xbc_s, xbc_s[:, 2560:3072])
        xq = al([128, NS, 16], F32, name="xq")
        dma("sp", xq, xq[:], SCR["xdt"][1], SCR["xdt"][0].ap().rearrange("b (q j) -> q b j", j=16))
        Bbc = al([128, NS, 128], F32, name="Bbc")
        Cbc = al([128, NS, 128], F32, name="Cbc")
        for g in range(4):
            dma("sp", Bbc, Bbc[32 * g:32 * (g + 1), :, :], SCR["B"][1], SCR["B"][0][:, g * 128:(g + 1) * 128].partition_broadcast(32))
            dma("sp", Cbc, Cbc[32 * g:32 * (g + 1), :, :], SCR["C"][1], SCR["C"][0][:, g * 128:(g + 1) * 128].partition_broadcast(32))
        da_s = al([NS, 32], F32, name="da_s")
        act(da_s, da_s[:], dts_s, dts_s[:, 32:64], AF.Exp)
        pda = nps()
        tr(pda, pda[0:32, 0:NS], da_s, da_s[:], ident_f, ident_f[0:NS, 0:NS])
        daT = al([32, NS], F32, name="daT")
        cp(daT, daT[:], pda, pda[0:32, 0:NS])
        pdq = nps()
        mm(pdq, pdq[:, 0:NS], sel_t, sel_t[:], daT, daT[:])
        da_q = al([128, NS], F32, name="da_q")
        cp(da_q, da_q[:], pdq, pdq[:, 0:NS])
        if SAMP <= 2:
            return
        yq = al([128, NS, 16], F32, name="yq")
        St = [al([128, 16, 128], F32, name="St")] * 2
        tmp3 = al([128, 16, 128], F32, name="tmp3")
        for b in range(NS):
            S = St[b % 2]
            dma("sp", S, S[:], TI["cssm"], I["cssm"][i, b].rearrange("(q j) n -> q j n", j=16))
            tt(tmp3, tmp3[:], xq, xq[:, b, :].unsqueeze(2).to_broadcast([128, 16, 128]),
               Bbc, Bbc[:, b, :].unsqueeze(1).to_broadcast([128, 16, 128]), ALU.mult)
            stt(S, S[:], S, S[:], da_q[:, b:b + 1], tmp3, tmp3[:], ALU.mult, ALU.add, extra_r=[da_q])
            dma("sp", TO["ssm_s"], O["ssm_s"][i, b].rearrange("(q j) n -> q j n", j=16), S, S[:])
            tt(tmp3, tmp3[:], S, S[:], Cbc, Cbc[:, b, :].unsqueeze(1).to_broadcast([128, 16, 128]), ALU.mult)
            P.add("dve", (lambda o_, i_: (lambda e: e.reduce_sum(out=o_, in_=i_, axis=AX.X)))(yq[:, b, :], tmp3[:]),
                  r=[tmp3], w=[yq])
        if SAMP <= 3:
            return
        dma("sp", SCR["y"][1], SCR["y"][0].ap().rearrange("b (q j) -> q b j", j=16), yq, yq[:])
        y_s = Tl("y_s_alias", raw_s[:, 0:2048])
        y_s.lw, y_s.rd = raw_s.lw, raw_s.rd
        dma("sp", y_s, y_s[:], SCR["y"][1], SCR["y"][0].ap())
        if SAMP <= 4:
            return
        tt(tmp_s, v3(tmp_s[:], 32), xbc_s, v3(xbc_s[:, 0:2048], 32), dbc, dbc[0:NS, :].unsqueeze(2).to_broadcast([NS, 32, 64]), ALU.mult)
        tt(y_s, y_s[:], y_s, y_s[:], tmp_s, tmp_s[:], ALU.add)
        tt(y_s, y_s[:], y_s, y_s[:], zs_s, zs_s[:], ALU.mult)
        yn_s = Tl("yn_s_alias", tmp_s[:, 0:1024].bitcast(BF16))
        yn_s.lw, yn_s.rd = tmp_s.lw, tmp_s.rd
        for g in range(4):
            rs = rstd_of(y_s, y_s[:, g * 512:(g + 1) * 512], NS, 2 + g, ncols=512)
            stt(yn_s, yn_s[:, g * 512:(g + 1) * 512], y_s, y_s[:, g * 512:(g + 1) * 512], rs, ngb,
                ngb[0:NS, g * 512:(g + 1) * 512], ALU.mult, ALU.mult, extra_r=[sd_t])
        if SAMP <= 5:
            return
        ptv_ = psA[:, 0:1024].bitcast(BF16).rearrange("p (k m) -> p k m", k=16)
        for c in range(16):
            tr(psA, ptv_[:, c, 0:NS], yn_s, yn_s[:, c * 128:(c + 1) * 128], ident_b, ident_b[0:NS, 0:NS])
        ynT_s = al([128, 16, NS], BF16, name="ynT_s")
        cp(ynT_s, ynT_s[:], psA, ptv_[:, :, 0:NS])
        mix_s = Tl("mix_s_alias", zs_s[:, 0:1024])
        mix_s.lw, mix_s.rd = zs_s.lw, zs_s.rd
        for dh in range(2):
            po = nps()
            for c in range(16):
                mm(po, po[0:NS, :], ynT_s, ynT_s[:, c, :], Wout, Wout[:, c, dh * 512:(dh + 1) * 512], start=(c == 0), stop=(c == 15))
            cp(mix_s, mix_s[:, dh * 512:(dh + 1) * 512], po, po[0:NS, :])
        post_norm_add(mix_s, mix_s[:], NS, g1, XS, XS[:], XS, XS[:], xt2[0])
        state["xsrc"] = xres
        state["xsrc_t"] = T_xres

    xt2 = [sb([128, D], F32)] * 2

    sub = 0
    for layer in range(4):
        if sub >= nsub:
            break
        if layer % 2 == 0:
            ab_layer(layer)
        else:
            ssd_layer(layer)
        sub += 1
        if sub >= nsub:
            break
        mlp(layer, last=(layer == 3))
        sub += 1
    if nsub < 8:
        for t in range(NT):
            P.add("sp", (lambda tt_: (lambda e: e.dma_start(out=O["y_p"][tt_ * 128:(tt_ + 1) * 128, :],
                                                          in_=state["xsrc"][tt_ * 128:(tt_ + 1) * 128, :])))(t),
                  r=[state["xsrc_t"].s(t)], w=[TO["y_p"]], dma=True)
        dma("sp", TO["y_s"], O["y_s"].ap(), XS, XS[:])
    P.barrier()
    cnt, dcnt = P.emit(st)
    st.close()
    return nc, len(P.ins)


def _consts(c):
    pos = c % 4
    i = np.arange(128)
    ident = np.eye(128, dtype=np.float32)
    tri = (i[:, None] <= i[None, :]).astype(np.float32)
    up = (i[:, None] > i[None, :]).astype(np.float32)
    m_own = np.where(i[None, :] <= i[:, None], 0.0, NEG).astype(np.float32)
    m_prev = np.where(i[None, :] > i[:, None], 0.0, NEG).astype(np.float32)
    mask1 = np.concatenate([m_prev, m_own], axis=1)
    mask0 = mask1.copy()
    if pos == 0:
        mask0[:, :128] = NEG
    hasprev = np.full((128, 1), 1.0 if pos > 0 else 0.0, np.float32)
    pub = np.zeros((128, 4), np.float32)
    pub[:, pos] = 1.0
    prev = max(pos - 1, 0)
    idxprev = (prev * 128 + i).astype(np.int32).reshape(128, 1)
    idxchain = np.stack([(m * 128 + i) for m in range(3)], axis=1).astype(np.int32)
    vmask = np.zeros((128, 3), np.float32)
    for m in range(3):
        if m < pos:
            vmask[:, m] = 1.0
    sel = (i[None, :] // 4 == np.arange(32)[:, None]).astype(np.float32)
    return dict(c_sel=sel, c_ident=ident, c_tri=tri, c_up=up, c_mask0=mask0, c_mask1=mask1, c_hasprev=hasprev,
                c_pub=pub, c_idxprev=idxprev, c_idxchain=idxchain, c_vmask=vmask)


def make_in_maps(inp):
    f = lambda a: np.ascontiguousarray(np.asarray(a, dtype=np.float32))
    xp = f(inp["x_prompt"])
    maps = []
    shared = dict(
        norm_g=f(inp["norm_g"]).reshape(16, D), ab_w_in=f(inp["ab_w_in"]), conf_dw_w=f(inp["conf_dw_w"]),
        conf_dw_b=f(inp["conf_dw_b"]), conf_ln_g=f(inp["conf_ln_g"]), conf_ln_b=f(inp["conf_ln_b"]),
        attn_sinks=f(inp["attn_sinks"]), ab_w_out=f(inp["ab_w_out"]), ssd_w_in=f(inp["ssd_w_in"]),
        ssd_conv_w=f(inp["ssd_conv_w"]), ssd_conv_b=f(inp["ssd_conv_b"]), ssd_dt_bias=f(inp["ssd_dt_bias"]),
        ssd_a_log=f(inp["ssd_a_log"]), ssd_d=f(inp["ssd_d"]), ssd_norm_g=f(inp["ssd_norm_g"]),
        ssd_w_out=f(inp["ssd_w_out"]), mlp_w_up=f(inp["mlp_w_up"]), mlp_w_down=f(inp["mlp_w_down"]))
    for c in range(NCORES):
        seq, pos = c // 4, c % 4
        m = dict(shared)
        m["xp"] = xp[seq, pos * TPC:(pos + 1) * TPC]
        m["xh"] = xp[seq, pos * TPC - 128:pos * TPC] if pos > 0 else np.zeros((128, D), np.float32)
        sl = slice(c * NS, (c + 1) * NS)
        m["xs"] = f(inp["x_sample"])[sl, 0]
        m["ck"] = f(inp["cache_win_k"])[:, sl].reshape(2, NS, 128, 128)
        m["cv"] = f(inp["cache_win_v"])[:, sl].reshape(2, NS, 128, 128)
        m["cconf"] = f(inp["state_conf_conv"])[:, sl]
        m["cssm"] = f(inp["state_ssm"])[:, sl].reshape(2, NS, 2048, 128)
        m["csc"] = f(inp["state_ssd_conv"])[:, sl]
        m.update(_consts(c))
        m = {k: np.ascontiguousarray(v) for k, v in m.items()}
        maps.append(m)
    return maps


_CACHE = {}


def run(inp, nsub=8):
    if nsub not in _CACHE:
        _CACHE[nsub] = build(nsub)
    nc, _ = _CACHE[nsub]
    res = run_bass_kernel_spmd(nc, make_in_maps(inp), core_ids=list(range(NCORES)))
    return res.results


def kernel(**inp):
    r = run(inp, 8)
    cat = lambda name, cores: np.concatenate([r[c][name] for c in cores], axis=0)
    y_p = np.stack([cat("y_p", range(0, 4)), cat("y_p", range(4, 8))])
    y_s = cat("y_s", range(8)).reshape(128, 1, D)
    wk_p = np.stack([r[3]["wk_p"], r[7]["wk_p"]], axis=1).reshape(2, 2, 128, 2, 64)
    wv_p = np.stack([r[3]["wv_p"], r[7]["wv_p"]], axis=1).reshape(2, 2, 128, 2, 64)
    wk_s = np.concatenate([r[c]["wk_s"] for c in range(8)], axis=1).reshape(2, 128, 1, 2, 64)
    wv_s = np.concatenate([r[c]["wv_s"] for c in range(8)], axis=1).reshape(2, 128, 1, 2, 64)
    cc_p = np.stack([r[3]["cc_p"], r[7]["cc_p"]], axis=1)
    cc_s = np.concatenate([r[c]["cc_s"] for c in range(8)], axis=1).reshape(2, 128, 1, 512)
    ssm_p = np.stack([r[3]["ssm_p"], r[7]["ssm_p"]], axis=1).reshape(2, 2, 32, 64, 128)
    ssm_s = np.concatenate([r[c]["ssm_s"] for c in range(8)], axis=1).reshape(2, 128, 32, 64, 128)
    sc_p = np.stack([r[3]["sc_p"], r[7]["sc_p"]], axis=1)
    sc_s = np.concatenate([r[c]["sc_s"] for c in range(8)], axis=1).reshape(2, 128, 1, 3072)
    outs = (y_p, y_s, wk_p, wv_p, wk_s, wv_s, cc_p, cc_s, ssm_p, ssm_s, sc_p, sc_s)
    return tuple(np.ascontiguousarray(o, dtype=np.float32) for o in outs)
```
